# Optimizing a Trainium2 kernel written in Bass

```python
import math
import jax, jax.numpy as jnp
from jax import lax
import numpy as np

D_MODEL = 2048
BATCH = 4
SEQ = 2048
DEPTH = 1

CTX_LEN = 256
GRID_W = 64
EPS = 1e-6
S5_WIDTH = D_MODEL // 2
S5_GROUP = 16
S5_GROUPS = S5_WIDTH // S5_GROUP
S5_STATE = 64
MLA_HEADS = 8
QK_NOPE = 128
QK_ROPE = 64
V_DIM = 128
Q_RANK = 512
KV_RANK = 256
ROPE_BASE = 10000.0
Q_BLOCK = 128
ATTN_SCALE = (QK_NOPE + QK_ROPE) ** -0.5
N_BRANCH = 2
D_FF = -(-8 * D_MODEL // (3 * 256)) * 256
IN_COLS = S5_WIDTH + Q_RANK + KV_RANK + QK_ROPE + N_BRANCH * D_MODEL

kernel_name = 'hybrid_s5_mla_dit_block'


def rmsnorm(x, g):
    xf = x.astype(jnp.float32)
    y = xf * lax.rsqrt(jnp.mean(xf * xf, axis=-1, keepdims=True) + EPS)
    return (y * g.astype(jnp.float32)).astype(x.dtype)


def ada(cvec, w_mod, b_mod):
    m = jax.nn.silu(cvec) @ w_mod + b_mod
    return m.reshape(m.shape[:-1] + (6, D_MODEL))


def rope2d_tables(n_tokens):
    rows = n_tokens // GRID_W
    row = jnp.repeat(jnp.arange(rows, dtype=jnp.float32), GRID_W)
    col = jnp.tile(jnp.arange(GRID_W, dtype=jnp.float32), rows)
    n_freq = QK_ROPE // 4
    inv = ROPE_BASE ** (-jnp.arange(n_freq, dtype=jnp.float32) / n_freq)
    ang = jnp.stack([row[:, None] * inv, col[:, None] * inv], axis=1)
    return jnp.cos(ang), jnp.sin(ang)


def apply_rope2d(x, cos, sin):
    xs = x.reshape(x.shape[:-1] + (2, 2, QK_ROPE // 4))
    x1, x2 = xs[..., 0, :], xs[..., 1, :]
    c = cos[None, :, None].astype(x.dtype)
    s = sin[None, :, None].astype(x.dtype)
    out = jnp.stack([x1 * c - x2 * s, x2 * c + x1 * s], axis=-2)
    return out.reshape(x.shape)


def split_in(h):
    o = S5_WIDTH
    u = h[..., :o]
    cq = h[..., o:o + Q_RANK]
    o += Q_RANK
    ckv = h[..., o:o + KV_RANK]
    o += KV_RANK
    kr = h[..., o:o + QK_ROPE]
    o += QK_ROPE
    return u, cq, ckv, kr, h[..., o:]


def s5_discretize(a_re, a_im, log_dt, b_re, b_im):
    f32 = jnp.float32
    dt = jnp.exp(log_dt.astype(f32))[:, None]
    lr, li = a_re.astype(f32), a_im.astype(f32)
    mag = jnp.exp(lr * dt)
    ab_re, ab_im = mag * jnp.cos(li * dt), mag * jnp.sin(li * dt)
    den = lr * lr + li * li
    nr, ni = ab_re - 1.0, ab_im
    co_re = (nr * lr + ni * li) / den
    co_im = (ni * lr - nr * li) / den
    br, bi = b_re.astype(f32), b_im.astype(f32)
    bb_re = co_re[..., None] * br - co_im[..., None] * bi
    bb_im = co_re[..., None] * bi + co_im[..., None] * br
    return ab_re, ab_im, bb_re, bb_im


def _ssm_combine(e1, e2):
    a1r, a1i, b1r, b1i = e1
    a2r, a2i, b2r, b2i = e2
    return (a2r * a1r - a2i * a1i, a2r * a1i + a2i * a1r,
            a2r * b1r - a2i * b1i + b2r, a2r * b1i + a2i * b1r + b2i)


def s5_scan(u, disc, h0, reverse):
    ab_re, ab_im, bb_re, bb_im = disc
    bu_re = jnp.einsum('blgp,gnp->blgn', u, bb_re)
    bu_im = jnp.einsum('blgp,gnp->blgn', u, bb_im)
    if h0 is not None:
        idx = -1 if reverse else 0
        h_re, h_im = h0
        bu_re = bu_re.at[:, idx].add(ab_re * h_re - ab_im * h_im)
        bu_im = bu_im.at[:, idx].add(ab_re * h_im + ab_im * h_re)
    a_re = jnp.broadcast_to(ab_re, bu_re.shape)
    a_im = jnp.broadcast_to(ab_im, bu_re.shape)
    _, _, h_re, h_im = lax.associative_scan(_ssm_combine, (a_re, a_im, bu_re, bu_im),
                                            reverse=reverse, axis=1)
    return h_re, h_im


def s5_readout(h, c_re, c_im):
    h_re, h_im = h
    return (jnp.einsum('blgn,gpn->blgp', h_re, c_re)
            - jnp.einsum('blgn,gpn->blgp', h_im, c_im))


def s5_mixer(u_ctx, u_lat, p, need_ctx_out):
    f32 = jnp.float32
    B, L = u_lat.shape[:2]
    Lc = u_ctx.shape[1]
    uc = u_ctx.astype(f32).reshape(B, Lc, S5_GROUPS, S5_GROUP)
    ul = u_lat.astype(f32).reshape(B, L, S5_GROUPS, S5_GROUP)
    d_skip = p['s5_d'].astype(f32)
    y_lat = d_skip * ul
    y_ctx = d_skip * uc if need_ctx_out else None
    for d, rev in enumerate((False, True)):
        disc = s5_discretize(p['s5_a_re'][d], p['s5_a_im'][d], p['s5_log_dt'][d],
                             p['s5_b_re'][d], p['s5_b_im'][d])
        c_re, c_im = p['s5_c_re'][d].astype(f32), p['s5_c_im'][d].astype(f32)
        hc = s5_scan(uc, disc, None, rev)
        last = 0 if rev else -1
        hl = s5_scan(ul, disc, (hc[0][:, last], hc[1][:, last]), rev)
        y_lat = y_lat + s5_readout(hl, c_re, c_im)
        if need_ctx_out:
            y_ctx = y_ctx + s5_readout(hc, c_re, c_im)
    y_lat = y_lat.reshape(B, L, S5_WIDTH).astype(u_lat.dtype)
    if need_ctx_out:
        y_ctx = y_ctx.reshape(B, Lc, S5_WIDTH).astype(u_ctx.dtype)
    return y_lat, y_ctx


def mla_qkv(cq, ckv, kr, p, rope):
    B, L = cq.shape[:2]
    q = (rmsnorm(cq, p['q_norm']) @ p['w_uq']).reshape(B, L, MLA_HEADS, QK_NOPE + QK_ROPE)
    kv = (rmsnorm(ckv, p['kv_norm']) @ p['w_ukv']).reshape(B, L, MLA_HEADS, QK_NOPE + V_DIM)
    q_nope, q_rope = q[..., :QK_NOPE], q[..., QK_NOPE:]
    k_nope, v = kv[..., :QK_NOPE], kv[..., QK_NOPE:]
    k_rope = kr[:, :, None, :]
    if rope is not None:
        cos, sin = rope
        q_rope = apply_rope2d(q_rope, cos, sin)
        k_rope = apply_rope2d(k_rope, cos, sin)
    q = jnp.concatenate([q_nope, q_rope], axis=-1)
    k = jnp.concatenate([k_nope, jnp.broadcast_to(k_rope, (B, L, MLA_HEADS, QK_ROPE))], axis=-1)
    return q, k, v


def attend(q, k, v):
    s = jnp.einsum('bqhd,bkhd->bhqk', q, k, preferred_element_type=jnp.float32) * ATTN_SCALE
    pr = jax.nn.softmax(s, axis=-1).astype(v.dtype)
    return jnp.einsum('bhqk,bkhd->bqhd', pr, v)


def blocked_attend(q, k, v):
    B, L, H, dk = q.shape
    nb = L // Q_BLOCK
    qb = q.reshape(B, nb, Q_BLOCK, H, dk).transpose(1, 0, 2, 3, 4)
    ob = lax.map(lambda qi: attend(qi, k, v), qb)
    return ob.transpose(1, 0, 2, 3, 4).reshape(B, L, H, v.shape[-1])


def merge_branches(y5, o_mla, gate_cols, p):
    z = jax.nn.gelu(y5)
    a, b = jnp.split(z @ p['w_glu'], 2, axis=-1)
    br_s5 = a * jax.nn.sigmoid(b)
    br_mla = o_mla.reshape(o_mla.shape[:2] + (MLA_HEADS * V_DIM,)) @ p['w_mla_o']
    g_s5, g_mla = jnp.split(jax.nn.sigmoid(gate_cols), 2, axis=-1)
    return (g_s5 * br_s5 + g_mla * br_mla) @ p['w_out']


def swiglu(h, p):
    a, b = jnp.split(h @ p['w_ffn_in'], 2, axis=-1)
    return (jax.nn.silu(a) * b) @ p['w_ffn_out']


def layer(x, xc, m_lat, m_ctx, cos, sin, p, need_ctx_out):
    sh1, sc1, g1, sh2, sc2, g2 = (m_lat[..., i, :] for i in range(6))
    csh1, csc1, cg1, csh2, csc2, cg2 = (m_ctx[..., i, :] for i in range(6))
    hl = (rmsnorm(x, p['norm1']) * (1.0 + sc1) + sh1) @ p['w_in']
    hc = (rmsnorm(xc, p['norm1']) * (1.0 + csc1) + csh1) @ p['w_in']
    ul, cql, ckvl, krl, gl = split_in(hl)
    uc, cqc, ckvc, krc, gc = split_in(hc)
    y5_lat, y5_ctx = s5_mixer(uc, ul, p, need_ctx_out)
    qc, kc, vc = mla_qkv(cqc, ckvc, krc, p, None)
    ql, kl, vl = mla_qkv(cql, ckvl, krl, p, (cos, sin))
    k_all = jnp.concatenate([kl, kc], axis=1)
    v_all = jnp.concatenate([vl, vc], axis=1)
    ol = blocked_attend(ql, k_all, v_all)
    x = x + g1 * merge_branches(y5_lat, ol, gl, p)
    x = x + g2 * swiglu(rmsnorm(x, p['norm2']) * (1.0 + sc2) + sh2, p)
    if need_ctx_out:
        oc = attend(qc, kc, vc)
        xc = xc + cg1 * merge_branches(y5_ctx, oc, gc, p)
        xc = xc + cg2 * swiglu(rmsnorm(xc, p['norm2']) * (1.0 + csc2) + csh2, p)
    return x, xc


def setup_inputs(seed: int = 0) -> dict:
    key = jax.random.key(seed)
    ks = jax.random.split(key, 32)
    f32 = jnp.float32

    def nrm(k, shape, scale):
        return jax.random.normal(k, shape, f32) * scale

    G, N, P = S5_GROUPS, S5_STATE, S5_GROUP
    n_idx = jnp.arange(N, dtype=f32)
    return {
        'x': nrm(ks[0], (BATCH, SEQ, D_MODEL), 1.0),
        'c': nrm(ks[1], (BATCH, D_MODEL), 1.0),
        'ctx': nrm(ks[2], (BATCH, CTX_LEN, D_MODEL), 1.0),
        'c_ctx': nrm(ks[3], (D_MODEL,), 1.0),
        'w_mod': nrm(ks[4], (DEPTH, D_MODEL, 6 * D_MODEL), 0.3 * D_MODEL ** -0.5),
        'b_mod': nrm(ks[5], (DEPTH, 6 * D_MODEL), 0.02),
        'norm1': 1.0 + nrm(ks[6], (DEPTH, D_MODEL), 0.01),
        'norm2': 1.0 + nrm(ks[7], (DEPTH, D_MODEL), 0.01),
        'w_in': nrm(ks[8], (DEPTH, D_MODEL, IN_COLS), D_MODEL ** -0.5),
        's5_a_re': -0.5 + nrm(ks[9], (DEPTH, 2, G, N), 0.01),
        's5_a_im': math.pi * n_idx + nrm(ks[10], (DEPTH, 2, G, N), 0.01),
        's5_log_dt': jax.random.uniform(ks[11], (DEPTH, 2, G), f32, math.log(1e-3), math.log(1e-1)),
        's5_b_re': nrm(ks[12], (DEPTH, 2, G, N, P), (2 * P) ** -0.5),
        's5_b_im': nrm(ks[13], (DEPTH, 2, G, N, P), (2 * P) ** -0.5),
        's5_c_re': nrm(ks[14], (DEPTH, 2, G, P, N), N ** -0.5),
        's5_c_im': nrm(ks[15], (DEPTH, 2, G, P, N), N ** -0.5),
        's5_d': nrm(ks[16], (DEPTH, G, P), 0.5),
        'w_glu': nrm(ks[17], (DEPTH, S5_WIDTH, 2 * D_MODEL), S5_WIDTH ** -0.5),
        'q_norm': 1.0 + nrm(ks[18], (DEPTH, Q_RANK), 0.01),
        'kv_norm': 1.0 + nrm(ks[19], (DEPTH, KV_RANK), 0.01),
        'w_uq': nrm(ks[20], (DEPTH, Q_RANK, MLA_HEADS * (QK_NOPE + QK_ROPE)), Q_RANK ** -0.5),
        'w_ukv': nrm(ks[21], (DEPTH, KV_RANK, MLA_HEADS * (QK_NOPE + V_DIM)), KV_RANK ** -0.5),
        'w_mla_o': nrm(ks[22], (DEPTH, MLA_HEADS * V_DIM, D_MODEL), (MLA_HEADS * V_DIM) ** -0.5),
        'w_out': nrm(ks[23], (DEPTH, D_MODEL, D_MODEL), D_MODEL ** -0.5),
        'w_ffn_in': nrm(ks[24], (DEPTH, D_MODEL, 2 * D_FF), D_MODEL ** -0.5),
        'w_ffn_out': nrm(ks[25], (DEPTH, D_FF, D_MODEL), D_FF ** -0.5),
        'norm_f': 1.0 + nrm(ks[26], (D_MODEL,), 0.01),
    }


def reference(x, c, ctx, c_ctx, w_mod, b_mod, norm1, norm2, w_in, s5_a_re, s5_a_im, s5_log_dt,
              s5_b_re, s5_b_im, s5_c_re, s5_c_im, s5_d, w_glu, q_norm, kv_norm, w_uq, w_ukv,
              w_mla_o, w_out, w_ffn_in, w_ffn_out, norm_f):
    cos, sin = rope2d_tables(x.shape[1])
    xc = ctx
    for l in range(DEPTH):
        p = {
            'norm1': norm1[l], 'norm2': norm2[l], 'w_in': w_in[l],
            's5_a_re': s5_a_re[l], 's5_a_im': s5_a_im[l], 's5_log_dt': s5_log_dt[l],
            's5_b_re': s5_b_re[l], 's5_b_im': s5_b_im[l], 's5_c_re': s5_c_re[l], 's5_c_im': s5_c_im[l],
            's5_d': s5_d[l], 'w_glu': w_glu[l], 'q_norm': q_norm[l], 'kv_norm': kv_norm[l],
            'w_uq': w_uq[l], 'w_ukv': w_ukv[l], 'w_mla_o': w_mla_o[l], 'w_out': w_out[l],
            'w_ffn_in': w_ffn_in[l], 'w_ffn_out': w_ffn_out[l],
        }
        m_lat = ada(c, w_mod[l], b_mod[l])[:, None]
        m_ctx = ada(c_ctx, w_mod[l], b_mod[l])
        x, xc = layer(x, xc, m_lat, m_ctx, cos, sin, p, l < DEPTH - 1)
    return rmsnorm(x, norm_f)
```

```python
import math
import numpy as np
import concourse.bass as bass
import concourse.mybir as mybir
from concourse.bass_utils import run_bass_kernel_spmd

F32 = mybir.dt.float32
BF16 = mybir.dt.bfloat16
I32 = mybir.dt.int32
AF = mybir.ActivationFunctionType
ALU = mybir.AluOpType

ENGS = ("pe", "act", "dve", "pool", "sp")
D = 2048
KT = 16
L = 2048
LC = 256
NTOK = 2304
OWN = 1024
EPS = 1e-6
ATTN_SCALE = 192.0 ** -0.5
DFF = 5632
SB_BASE = 16512
SB_END = 229376


class Res:
    __slots__ = ("name", "w", "rs", "lo", "hi")

    def __init__(self, name, lo=None, hi=None):
        self.name = name
        self.w = None
        self.rs = []
        self.lo = lo
        self.hi = hi


class Op:
    __slots__ = ("eng", "fn", "deps", "dma", "needed", "sem", "val", "prev", "gi")

    def __init__(self, eng, fn, dma):
        self.eng = eng
        self.fn = fn
        self.deps = []
        self.dma = dma
        self.needed = False
        self.sem = None
        self.val = 0
        self.prev = None
        self.gi = 0


class Prog:
    def __init__(self, nc, n_dma_sems=48):
        self.nc = nc
        self.ops = {e: [] for e in ENGS}
        self.n_dma_sems = n_dma_sems
        self.count = 0
        self.final = []

    def add(self, eng, fn, reads=(), writes=(), dma=False):
        op = Op(eng, fn, dma)
        op.gi = self.count
        self.count += 1
        deps = []
        for r in reads:
            if r.w is not None:
                deps.append(r.w)
        for w in writes:
            if w.w is not None:
                deps.append(w.w)
            lastr = {}
            for o in w.rs:
                if o.dma:
                    deps.append(o)
                elif o.eng not in lastr or lastr[o.eng].gi < o.gi:
                    lastr[o.eng] = o
            deps.extend(lastr.values())
        seen = set()
        for d in deps:
            if id(d) in seen or d is op:
                continue
            seen.add(id(d))
            if d.eng == eng and not d.dma:
                if eng == "pe" or eng == "sp":
                    continue
                if not any(r.w is d for r in reads):
                    continue
            op.deps.append(d)
        for r in reads:
            r.rs.append(op)
        for w in writes:
            w.w = op
            w.rs = []
        self.ops[eng].append(op)
        return op

    def dma(self, eng, out, in_, reads=(), writes=(), **kw):
        return self.add(eng, lambda e: e.dma_start(out=out, in_=in_, **kw), reads, writes, dma=True)

    def emit(self):
        nc = self.nc
        for e in ENGS:
            for op in self.ops[e]:
                for d in op.deps:
                    d.needed = True
        for op in self.final:
            op.needed = True
        esem = {e: nc.alloc_semaphore(name="eng_" + e) for e in ENGS}
        dsems = [nc.alloc_semaphore(name="dma_%d" % i) for i in range(self.n_dma_sems)]
        dval = [0] * self.n_dma_sems
        dlast = [None] * self.n_dma_sems
        allops = sorted([op for e in ENGS for op in self.ops[e]], key=lambda o: o.gi)
        cnt = {e: 0 for e in ENGS}
        rr = 0
        for op in allops:
            if op.dma:
                j = rr % self.n_dma_sems
                rr += 1
                dval[j] += 16
                op.sem = dsems[j]
                op.val = dval[j]
                op.prev = dlast[j]
                dlast[j] = op
            elif op.needed:
                cnt[op.eng] += 1
                op.sem = esem[op.eng]
                op.val = cnt[op.eng]
        engobj = {"pe": "tensor", "act": "scalar", "dve": "vector", "pool": "gpsimd", "sp": "sync"}
        nwaits = {e: 0 for e in ENGS}
        with nc.Block() as block:
            for e in ENGS:
                ops = self.ops[e]
                final = self.final if e == "sp" else []

                def body(eng, ops=ops, e=e, final=final):
                    seen = {}

                    def wait(sem, val):
                        k = id(sem)
                        if seen.get(k, 0) >= val:
                            return
                        seen[k] = val
                        eng.wait_ge(sem, val)
                        nwaits[e] += 1

                    for op in ops:
                        need = {}
                        for d in op.deps:
                            k = id(d.sem)
                            if k not in need or need[k][1] < d.val:
                                need[k] = (d.sem, d.val)
                        for sem_, val_ in need.values():
                            wait(sem_, val_)
                        if op.dma and op.prev is not None:
                            wait(op.prev.sem, op.prev.val)
                        ins = op.fn(eng)
                        if op.dma:
                            ins.then_inc(op.sem, 16)
                        elif op.needed:
                            ins.then_inc(op.sem, 1)
                    for op in final:
                        wait(op.sem, op.val)

                getattr(block, engobj[e])(body)
        self.stats = {e: (len(self.ops[e]), nwaits[e]) for e in ENGS}


class SBAlloc:
    def __init__(self, nc):
        self.nc = nc
        self.off = SB_BASE
        self.peak = SB_BASE
        self.n = 0
        self.live = []
        self.dead = []

    def tile(self, name, shape, dt):
        esz = 2 if dt == BF16 else 4
        nbytes = int(np.prod(shape[1:])) * esz
        nbytes = (nbytes + 63) // 64 * 64
        lo = self.off
        hi = lo + nbytes
        assert hi <= SB_END, "SBUF overflow at %s: %d" % (name, hi)
        self.n += 1
        t = self.nc.alloc_sbuf_tensor_at("%s_%d" % (name, self.n), list(shape), dt, offset=lo)
        self.off = hi
        self.peak = max(self.peak, hi)
        ent = (lo, hi, [])
        self.live.append(ent)
        t_res = self.res(name, ent)
        return t, t_res

    def res(self, name, ent):
        r = Res(name, ent[0], ent[1])
        for dres in self.dead:
            if dres.lo < r.hi and r.lo < dres.hi:
                if dres.w is not None:
                    r.rs.append(dres.w)
                r.rs.extend(dres.rs)
        ent[2].append(r)
        return r

    def sub(self, name, parent):
        for ent in self.live:
            if parent in ent[2]:
                return self.res(name, ent)
        raise KeyError(name)

    def mark(self):
        return (self.off, len(self.live))

    def release(self, m):
        off, n = m
        for ent in self.live[n:]:
            self.dead.extend(ent[2])
        del self.live[n:]
        self.off = off


def build(debug=(), stop_after=None):
    nc = bass.Bass("TRN2", target_bir_lowering=False)
    P = Prog(nc)
    A = SBAlloc(nc)
    dbg_out = {}

    def din(name, shape):
        return nc.dram_tensor(name, list(shape), F32, kind="ExternalInput").ap()

    xs = din("xs", [NTOK, D])
    cvec = din("cvec", [2, D])
    cfg = din("cfg", [1, 4])
    w_mod = din("w_mod", [D, 6 * D])
    b_mod = din("b_mod", [6 * D])
    norm1 = din("norm1", [D])
    norm2 = din("norm2", [D])
    w_in = din("w_in", [D, 5952])
    a_re = din("s5_a_re", [2, 64, 64])
    a_im = din("s5_a_im", [2, 64, 64])
    log_dt = din("s5_log_dt", [2, 64])
    b_re = din("s5_b_re", [2, 64, 64, 16])
    b_im = din("s5_b_im", [2, 64, 64, 16])
    c_re = din("s5_c_re", [2, 64, 16, 64])
    c_im = din("s5_c_im", [2, 64, 16, 64])
    s5_d = din("s5_d", [64, 16])
    w_glu = din("w_glu", [1024, 4096])
    q_norm = din("q_norm", [512])
    kv_norm = din("kv_norm", [256])
    w_uq = din("w_uq", [512, 1536])
    w_ukv = din("w_ukv", [256, 2048])
    w_mla_o = din("w_mla_o", [1024, 2048])
    w_out = din("w_out", [D, D])
    w_ffn_in = din("w_ffn_in", [D, 2 * DFF])
    w_ffn_out = din("w_ffn_out", [DFF, D])
    norm_f = din("norm_f", [D])
    out = nc.dram_tensor("out", [OWN, D], F32, kind="ExternalOutput").ap()
    xts_d = nc.dram_tensor("xts_scr", [128, KT, OWN], BF16, kind="Internal").ap()
    x1s_d = nc.dram_tensor("x1s_scr", [128, KT, OWN], F32, kind="Internal").ap()
    r_xts = Res("xts_d")
    r_x1s = Res("x1s_d")

    def dump(name, tile_ap, res, shape, dt=F32):
        if name not in debug:
            return
        o = nc.dram_tensor("dbg_" + name, list(shape), dt, kind="ExternalOutput").ap()
        dbg_out[name] = o
        P.final.append(P.dma("sp", o, tile_ap, reads=res))

    psb = []
    psr = []
    for i in range(8):
        psb.append(nc.alloc_psum_tensor("psbank%d" % i, [128, 512], F32))
        psr.append(Res("psum%d" % i))

    class Rot:
        def __init__(self, banks):
            self.banks = banks
            self.i = 0

        def next(self):
            b = self.banks[self.i % len(self.banks)]
            self.i += 1
            return psb[b], psr[b]

    alt = {"i": 0}

    def evac_eng():
        alt["i"] += 1
        return "act" if alt["i"] % 2 else "dve"

    ident_f, r_identf = A.tile("identf", [128, 128], F32)
    ident_b, r_identb = A.tile("identb", [128, 128], BF16)
    ones_b, r_onesb = A.tile("onesb", [128, 128], BF16)
    iot, r_iot = A.tile("iot", [128, 128], F32)
    P.add("pool", lambda e: e.iota(iot[:], pattern=[[1, 128]], base=0, channel_multiplier=-1,
                                   allow_small_or_imprecise_dtypes=True), writes=[r_iot])
    P.add("dve", lambda e: e.tensor_scalar(out=ident_f[:], in0=iot[:], scalar1=0.0, scalar2=None,
                                           op0=ALU.is_equal), reads=[r_iot], writes=[r_identf])
    P.add("dve", lambda e: e.tensor_copy(out=ident_b[:], in_=ident_f[:]), reads=[r_identf], writes=[r_identb])
    P.add("pool", lambda e: e.memset(ones_b[:], 1.0), writes=[r_onesb])

    modt, r_modt = A.tile("modt", [128, 96, 2], F32)
    s1, r_s1 = A.tile("s1", [128, 16, 2], F32)
    n1t, r_n1 = A.tile("n1t", [128, 16], F32)
    n2t, r_n2 = A.tile("n2t", [128, 16], F32)
    nft, r_nf = A.tile("nft", [128, 16], F32)
    bmt, r_bm = A.tile("bmt", [128, 96], F32)
    cs, r_cs = A.tile("cs", [128, 16, 2], F32)
    csb, r_csb = A.tile("csb", [128, 16, 2], BF16)
    s2, r_s2 = A.tile("s2", [128, 16], F32)
    for v_ in range(2):
        P.dma("sp", cs[:, :, v_], cvec[v_].rearrange("(kt p) -> p kt", p=128), writes=[r_cs], allow_slow_non_contiguous=True)
    P.dma("sp", bmt[:], b_mod.rearrange("(nt p) -> p nt", p=128), writes=[r_bm], allow_slow_non_contiguous=True)
    P.dma("sp", n1t[:], norm1.rearrange("(dt p) -> p dt", p=128), writes=[r_n1], allow_slow_non_contiguous=True)
    P.dma("sp", n2t[:], norm2.rearrange("(dt p) -> p dt", p=128), writes=[r_n2], allow_slow_non_contiguous=True)
    P.dma("sp", nft[:], norm_f.rearrange("(dt p) -> p dt", p=128), writes=[r_nf], allow_slow_non_contiguous=True)
    P.add("act", lambda e: e.activation(out=csb[:], in_=cs[:], func=AF.Silu), reads=[r_cs], writes=[r_csb])

    wmb = []

    def mod_slabs(j0, j1, bank, cols=1024, bufs=None):
        bufs = bufs if bufs is not None else wmb
        npt = cols // 128
        nsl = (j1 - j0) * 1024 // cols
        for si in range(nsl):
            wt, wr = bufs[si % 2]
            cbase = j0 * 1024 + si * cols
            P.dma("pool", wt[:, :, 0:cols], w_mod[:, cbase:cbase + cols].rearrange("(kt p) n -> p kt n", p=128),
                  writes=[wr])
            for ntl in range(npt):
                col = (si * npt + ntl) * 2
                for kt in range(KT):
                    P.add("pe", lambda e, kt=kt, ntl=ntl, col=col, wt=wt: e.matmul(
                        psb[bank][:, col:col + 2], wt[:, kt, ntl * 128:(ntl + 1) * 128], csb[:, kt, :],
                        start=(kt == 0), stop=(kt == KT - 1)),
                        reads=[wr, r_csb], writes=[psr[bank]])
        n0 = j0 * 8
        n1_ = j1 * 8

        def finish():
            P.add("dve", lambda e: e.tensor_tensor(
                out=modt[:, n0:n1_, :], in0=psb[bank][:, 0:(n1_ - n0) * 2].rearrange("p (n v) -> p n v", v=2),
                in1=bmt[:, n0:n1_].unsqueeze(2).to_broadcast([128, n1_ - n0, 2]), op=ALU.add),
                reads=[psr[bank], r_bm], writes=[r_modt])
        return finish

    MODJOBS = []
    m_ug = A.mark()
    ug, r_ug = A.tile("ug", [128, 64, 288], BF16)
    m_mla = A.mark()
    cqn, r_cqn = A.tile("cqn", [128, 4, OWN], BF16)
    ckvn, r_ckvn = A.tile("ckvn", [128, 2, NTOK], BF16)
    krt, r_krt = A.tile("krt", [64, NTOK], BF16)

    cosT, r_cos = A.tile("cosT", [64, L], F32)
    sinT, r_sin = A.tile("sinT", [64, L], F32)
    m_rt = A.mark()
    pos, r_pos = A.tile("pos", [64, L], F32)
    tmpa, r_tmpa = A.tile("tmpa", [64, L], F32)
    tmpb, r_tmpb = A.tile("tmpb", [64, L], F32)
    tmpi, r_tmpi = A.tile("tmpi", [64, L], I32)
    pidx, r_pidx = A.tile("pidx", [64, 4], F32)
    cfgt, r_cfg = A.tile("cfgt", [64, 4], F32)
    P.dma("sp", cfgt[:], cfg.partition_broadcast(64), writes=[r_cfg])
    P.add("pool", lambda e: e.iota(pos[0:32, :], pattern=[[1, 32], [0, 64]], base=0, channel_multiplier=0,
                                   allow_small_or_imprecise_dtypes=True), writes=[r_pos])
    P.add("pool", lambda e: e.iota(pos[32:64, :], pattern=[[0, 32], [1, 64]], base=0, channel_multiplier=0,
                                   allow_small_or_imprecise_dtypes=True), writes=[r_pos])
    P.add("dve", lambda e: e.tensor_scalar(out=pos[0:32, :], in0=pos[0:32, :], scalar1=cfgt[0:32, 2:3],
                                           scalar2=cfgt[0:32, 0:1], op0=ALU.mult, op1=ALU.add),
          reads=[r_pos, r_cfg], writes=[r_pos])
    P.add("dve", lambda e: e.tensor_scalar(out=pos[32:64, :], in0=pos[32:64, :], scalar1=cfgt[32:64, 2:3],
                                           scalar2=cfgt[32:64, 1:2], op0=ALU.mult, op1=ALU.add),
          reads=[r_pos, r_cfg], writes=[r_pos])
    P.add("pool", lambda e: e.iota(pidx[:, 0:1], pattern=[[0, 1]], base=0, channel_multiplier=1,
                                   allow_small_or_imprecise_dtypes=True), writes=[r_pidx])
    pc, r_pc = A.tile("pc", [64, 4], F32)
    for i, thr in enumerate((16.0, 32.0, 48.0)):
        P.add("dve", lambda e, i=i, thr=thr: e.tensor_scalar(out=pc[:, i:i + 1], in0=pidx[:, 0:1], scalar1=thr, scalar2=None,
                                                             op0=ALU.is_ge), reads=[r_pidx], writes=[r_pc])
    P.add("dve", lambda e: e.tensor_tensor(out=pidx[:, 1:2], in0=pc[:, 0:1], in1=pc[:, 1:2], op=ALU.add),
          reads=[r_pc], writes=[r_pidx])
    P.add("dve", lambda e: e.tensor_tensor(out=pidx[:, 1:2], in0=pidx[:, 1:2], in1=pc[:, 2:3], op=ALU.add),
          reads=[r_pc, r_pidx], writes=[r_pidx])
    P.add("dve", lambda e: e.scalar_tensor_tensor(out=pidx[:, 2:3], in0=pidx[:, 1:2], scalar=-16.0, in1=pidx[:, 0:1],
                                                  op0=ALU.mult, op1=ALU.add), reads=[r_pidx], writes=[r_pidx])
    P.add("act", lambda e: e.activation(out=pidx[:, 2:3], in_=pidx[:, 2:3], func=AF.Exp, scale=-math.log(10000.0) / 16.0),
          reads=[r_pidx], writes=[r_pidx])
    P.add("dve", lambda e: e.tensor_tensor(out=pidx[:, 3:4], in0=pc[:, 0:1], in1=pc[:, 1:2], op=ALU.subtract),
          reads=[r_pc], writes=[r_pidx])
    P.add("dve", lambda e: e.tensor_tensor(out=pidx[:, 3:4], in0=pidx[:, 3:4], in1=pc[:, 2:3], op=ALU.add),
          reads=[r_pc, r_pidx], writes=[r_pidx])
    P.add("dve", lambda e: e.tensor_scalar(out=pidx[:, 3:4], in0=pidx[:, 3:4], scalar1=2.0, scalar2=-1.0,
                                           op0=ALU.mult, op1=ALU.add), reads=[r_pidx], writes=[r_pidx])

    def range_reduce(eng, dst, dres, src, sres, shift, n, tmp, tres, tint, tires):
        inv2pi = 1.0 / (2.0 * math.pi)
        P.add(eng, lambda e: e.tensor_scalar(out=tmp, in0=src, scalar1=inv2pi, scalar2=shift * inv2pi,
                                             op0=ALU.mult, op1=ALU.add), reads=sres, writes=tres)
        P.add(eng, lambda e: e.tensor_copy(out=tint, in_=tmp), reads=tres, writes=tires)
        P.add(eng, lambda e: e.tensor_copy(out=dst, in_=tint), reads=tires, writes=dres)
        P.add(eng, lambda e: e.tensor_tensor(out=tmp, in0=tmp, in1=dst, op=ALU.subtract), reads=tres + dres, writes=tres)
        P.add(eng, lambda e: e.tensor_scalar(out=dst, in0=tmp, scalar1=0.5, scalar2=None, op0=ALU.is_gt),
              reads=tres, writes=dres)
        P.add(eng, lambda e: e.tensor_tensor(out=tmp, in0=tmp, in1=dst, op=ALU.subtract), reads=tres + dres, writes=tres)
        P.add(eng, lambda e: e.tensor_scalar(out=dst, in0=tmp, scalar1=-0.5, scalar2=None, op0=ALU.is_lt),
              reads=tres, writes=dres)
        P.add(eng, lambda e: e.tensor_tensor(out=tmp, in0=tmp, in1=dst, op=ALU.add), reads=tres + dres, writes=tres)
        P.add(eng, lambda e: e.tensor_scalar(out=dst, in0=tmp, scalar1=2.0 * math.pi, scalar2=None, op0=ALU.mult),
              reads=tres, writes=dres)

    P.add("dve", lambda e: e.tensor_scalar(out=pos[:], in0=pos[:], scalar1=pidx[:, 2:3], scalar2=None, op0=ALU.mult),
          reads=[r_pos, r_pidx], writes=[r_pos])
    range_reduce("dve", tmpb[:], [r_tmpb], pos[:], [r_pos], 0.0, L, tmpa[:], [r_tmpa], tmpi[:], [r_tmpi])
    P.add("act", lambda e: e.activation(out=sinT[:], in_=tmpb[:], func=AF.Sin), reads=[r_tmpb], writes=[r_sin])
    P.add("dve", lambda e: e.tensor_scalar(out=sinT[:], in0=sinT[:], scalar1=pidx[:, 3:4],
                                           scalar2=None, op0=ALU.mult), reads=[r_sin, r_pidx], writes=[r_sin])
    range_reduce("dve", tmpb[:], [r_tmpb], pos[:], [r_pos], math.pi / 2.0, L, tmpa[:], [r_tmpa], tmpi[:], [r_tmpi])
    P.add("act", lambda e: e.activation(out=cosT[:], in_=tmpb[:], func=AF.Sin), reads=[r_tmpb], writes=[r_cos])
    A.release(m_rt)
    m_w = A.mark()
    dump("cosT", cosT[:], [r_cos], [64, L])
    dump("sinT", sinT[:], [r_sin], [64, L])

    A.release(m_w)
    xt, r_xt_all = A.tile("xt", [128, KT, NTOK], BF16)
    r_xt = [A.sub("xt_g%d" % g, r_xt_all) for g in range(9)]
    m_p1 = A.mark()
    xbuf = [A.tile("xbuf%d" % i, [128, 2, D], F32) for i in range(2)]
    junk, r_junk = A.tile("junk", [128, D], BF16)
    sst, r_sst = A.tile("sst", [128, 8], F32)
    wm0 = [A.tile("wm0_%d" % i, [128, KT, 256], BF16) for i in range(2)]
    rot4 = Rot([1, 2, 3, 4])
    mod_state = {"si": 0}

    def mod_early(nsl):
        for _ in range(nsl):
            si = mod_state["si"]
            if si >= 16:
                return
            mod_state["si"] += 1
            wt, wr = wm0[si % 2]
            P.dma("pool", wt[:], w_mod[:, si * 256:(si + 1) * 256].rearrange("(kt p) n -> p kt n", p=128), writes=[wr])
            for ntl in range(2):
                col = (si * 2 + ntl) * 2
                for kt in range(KT):
                    P.add("pe", lambda e, kt=kt, ntl=ntl, col=col, wt=wt: e.matmul(
                        psb[0][:, col:col + 2], wt[:, kt, ntl * 128:(ntl + 1) * 128], csb[:, kt, :],
                        start=(kt == 0), stop=(kt == KT - 1)), reads=[wr, r_csb], writes=[psr[0]])

    mod_early(2)
    groups = [(0, 2, 1)] + [(LC + 256 * g, 2, 0) for g in range(8)]
    for gi, (c0, ntl, v) in enumerate(groups):
        xb, xr = xbuf[gi % 2]
        P.dma("sp", xb[:, 0:ntl, :], xs[c0:c0 + ntl * 128, :].rearrange("(j p) d -> p j d", p=128), writes=[xr])
        for j in range(ntl):
            P.add("act", lambda e, j=j, xb=xb: e.activation(out=junk[:], in_=xb[:, j, :], func=AF.Square,
                                                         accum_out=sst[:, j:j + 1]),
                  reads=[xr], writes=[r_junk, r_sst])
        P.add("act", lambda e, ntl=ntl: e.activation(out=sst[:, 4:4 + ntl], in_=sst[:, 0:ntl], func=AF.Sqrt,
                                                    scale=1.0 / D, bias=EPS), reads=[r_sst], writes=[r_sst])
        P.add("dve", lambda e, ntl=ntl: e.reciprocal(out=sst[:, 4:4 + ntl], in_=sst[:, 4:4 + ntl]),
              reads=[r_sst], writes=[r_sst])
        for j in range(ntl):
            P.add("dve", lambda e, j=j, xb=xb: e.tensor_scalar(out=xb[:, j, :], in0=xb[:, j, :], scalar1=sst[:, 4 + j:5 + j],
                                                            scalar2=None, op0=ALU.mult),
                  reads=[xr, r_sst], writes=[xr])
        for dt_ in range(KT):
            pb_, pr_ = rot4.next()
            for j in range(ntl):
                P.add("pe", lambda e, j=j, dt_=dt_, xb=xb, pb_=pb_: e.transpose(
                    pb_[:, j * 128:(j + 1) * 128], xb[:, j, dt_ * 128:(dt_ + 1) * 128], ident_f[:]),
                    reads=[xr, r_identf], writes=[pr_])
            n = ntl * 128
            if dt_ % 2 == 0:
                P.add("act", lambda e, dt_=dt_, pb_=pb_, n=n, c0=c0: e.copy(out=xt[:, dt_, c0:c0 + n], in_=pb_[:, 0:n]),
                      reads=[pr_], writes=[r_xt[gi]])
            else:
                P.add("dve", lambda e, dt_=dt_, pb_=pb_, n=n, c0=c0: e.tensor_copy(out=xt[:, dt_, c0:c0 + n], in_=pb_[:, 0:n]),
                      reads=[pr_], writes=[r_xt[gi]])
        mod_early(2)
    mod_early(16)
    P.add("dve", lambda e: e.tensor_tensor(
        out=modt[:, 0:32, :], in0=psb[0][:, 0:64].rearrange("p (n v) -> p n v", v=2),
        in1=bmt[:, 0:32].unsqueeze(2).to_broadcast([128, 32, 2]), op=ALU.add), reads=[psr[0], r_bm], writes=[r_modt])
    P.add("dve", lambda e: e.tensor_scalar(out=s1[:], in0=modt[:, 16:32, :], scalar1=1.0, scalar2=None, op0=ALU.add),
          reads=[r_modt], writes=[r_s1])
    P.add("dve", lambda e: e.tensor_tensor(out=s1[:], in0=s1[:], in1=n1t[:].unsqueeze(2).to_broadcast([128, 16, 2]),
                                           op=ALU.mult), reads=[r_s1, r_n1], writes=[r_s1])
    dump("modt01", modt[:, 0:32, :], [r_modt], [128, 32, 2])
    for dt_ in range(KT):
        for (c0_, n_, v_) in ((0, LC, 1), (LC, L, 0)):
            if (dt_ + v_) % 2 == 0:
                P.add("dve", lambda e, dt_=dt_, c0_=c0_, n_=n_, v_=v_: e.tensor_scalar(
                    out=xt[:, dt_, c0_:c0_ + n_], in0=xt[:, dt_, c0_:c0_ + n_], scalar1=s1[:, dt_, v_:v_ + 1],
                    scalar2=modt[:, dt_, v_:v_ + 1], op0=ALU.mult, op1=ALU.add),
                    reads=r_xt + [r_s1, r_modt], writes=r_xt)
            else:
                P.add("act", lambda e, dt_=dt_, c0_=c0_, n_=n_, v_=v_: e.activation(
                    out=xt[:, dt_, c0_:c0_ + n_], in_=xt[:, dt_, c0_:c0_ + n_], func=AF.Identity,
                    scale=s1[:, dt_, v_:v_ + 1], bias=modt[:, dt_, v_:v_ + 1]),
                    reads=r_xt + [r_s1, r_modt], writes=r_xt)
    A.release(m_p1)
    P.dma("sp", xts_d, xt[:, :, LC:LC + OWN], reads=r_xt[1:5], writes=[r_xts])
    dump("xt", xt[:], r_xt, [128, KT, NTOK], BF16)
    if stop_after == "prep":
        P.emit()
        return nc, P, dbg_out

    m_p2 = A.mark()
    wbuf = [A.tile("wbuf%d" % i, [128, KT, 512], BF16) for i in range(2)]
    wbi = {"i": 0}

    def wnext():
        t = wbuf[wbi["i"] % 2]
        wbi["i"] += 1
        return t

    def wsl(c0, c1):
        return w_in[:, c0:c1].rearrange("(kt p) n -> p kt n", p=128)

    def xt_res(c0, n):
        out_ = []
        for gi, (g0, ntl, v) in enumerate(groups):
            if g0 < c0 + n and c0 < g0 + ntl * 128:
                out_.append(r_xt[gi])
        return out_

    uc, r_uc = A.tile("uc", [128, 32, 128], BF16)
    rot = Rot([0, 1, 2, 3, 4, 5, 6, 7])
    ctiles = [(0, 32, 0), (LC, 128, 32), (LC + 1024, 128, 160)]
    for cb in range(2):
        wb, wr = wnext()
        P.dma("pool", wb[:], wsl(cb * 512, (cb + 1) * 512), writes=[wr])
        for (c0, M, q0) in ctiles:
            ntk = M * 8
            xres = xt_res(c0, ntk)
            for s in range(8):
                pb_, pr_ = rot.next()
                for kt in range(KT):
                    P.add("pe", lambda e, kt=kt, pb_=pb_, M=M, c0=c0, s=s, ntk=ntk, wb=wb: e.matmul(
                        pb_[0:M, 0:512], xt[:, kt, c0 + s:c0 + ntk:8], wb[:, kt, :],
                        start=(kt == 0), stop=(kt == KT - 1)), reads=xres + [wr], writes=[pr_])
                eng = evac_eng()
                if eng == "act":
                    P.add("act", lambda e, pb_=pb_, M=M, s=s: e.copy(
                        out=uc[0:M, :, s * 16:(s + 1) * 16], in_=pb_[0:M, 0:512].rearrange("m (g p) -> m g p", p=16)),
                          reads=[pr_], writes=[r_uc])
                else:
                    P.add("dve", lambda e, pb_=pb_, M=M, s=s: e.tensor_copy(
                        out=uc[0:M, :, s * 16:(s + 1) * 16], in_=pb_[0:M, 0:512].rearrange("m (g p) -> m g p", p=16)),
                          reads=[pr_], writes=[r_uc])
            for gb in range(4):
                pb_, pr_ = rot.next()
                pbb = pb_[:].bitcast(BF16)
                for gl in range(8):
                    gloc = gb * 8 + gl
                    P.add("pe", lambda e, pbb=pbb, gl=gl, M=M, gloc=gloc: e.transpose(
                        pbb[:, gl * 128:gl * 128 + M], uc[0:M, gloc, :], ident_b[0:M, 0:M]),
                        reads=[r_uc, r_identb], writes=[pr_])
                g0 = cb * 32 + gb * 8
                eng = evac_eng()
                src = pbb.rearrange("p (g c) -> p g c", g=8)[:, :, 0:M]
                if eng == "act":
                    P.add("act", lambda e, src=src, g0=g0, q0=q0, M=M: e.copy(out=ug[:, g0:g0 + 8, q0:q0 + M], in_=src),
                          reads=[pr_], writes=[r_ug])
                else:
                    P.add("dve", lambda e, src=src, g0=g0, q0=q0, M=M: e.tensor_copy(out=ug[:, g0:g0 + 8, q0:q0 + M], in_=src),
                          reads=[pr_], writes=[r_ug])
    dump("ug", ug[:], [r_ug], [128, 64, 288], BF16)

    gq, r_gq = A.tile("gq", [128, 4], F32)
    gkv, r_gkv = A.tile("gkv", [128, 2], F32)
    P.dma("sp", gq[:], q_norm.rearrange("(mt p) -> p mt", p=128), writes=[r_gq], allow_slow_non_contiguous=True)
    P.dma("sp", gkv[:], kv_norm.rearrange("(mt p) -> p mt", p=128), writes=[r_gkv], allow_slow_non_contiguous=True)
    sq, r_sq = A.tile("sq", [128, 4, 512], BF16)
    rsd, r_rsd = A.tile("rsd", [128, 512], F32)
    tr1, r_tr1 = A.tile("tr1", [64, 512], F32)
    tr2, r_tr2 = A.tile("tr2", [64, 512], F32)
    own_blocks = [(LC, 512), (LC + 512, 512)]
    all_blocks = [(0, 256), (LC, 512), (LC + 512, 512), (LC + 1024, 512), (LC + 1536, 512)]

    def norm_block(nmt, c0, n, wb, wr, wcol0, gam, r_gam, rank, scale, dst, r_dst, dcol0):
        xres = xt_res(c0, n)
        for mt in range(nmt):
            for kt in range(KT):
                P.add("pe", lambda e, mt=mt, kt=kt: e.matmul(
                    psb[mt][:, 0:n], wb[:, kt, wcol0 + mt * 128:wcol0 + (mt + 1) * 128], xt[:, kt, c0:c0 + n],
                    start=(kt == 0), stop=(kt == KT - 1)), reads=xres + [wr], writes=[psr[mt]])
            P.add("act", lambda e, mt=mt: e.activation(out=sq[:, mt, 0:n], in_=psb[mt][:, 0:n], func=AF.Square),
                  reads=[psr[mt]], writes=[r_sq])
        for mt in range(nmt):
            P.add("pe", lambda e, mt=mt: e.matmul(psb[4][:, 0:n], ones_b[:], sq[:, mt, 0:n],
                                                  start=(mt == 0), stop=(mt == nmt - 1)),
                  reads=[r_sq, r_onesb], writes=[psr[4]])
        P.add("act", lambda e: e.activation(out=rsd[:, 0:n], in_=psb[4][:, 0:n], func=AF.Sqrt,
                                            scale=1.0 / (rank * scale * scale), bias=EPS / (scale * scale)),
              reads=[psr[4]], writes=[r_rsd])
        P.add("dve", lambda e: e.reciprocal(out=rsd[:, 0:n], in_=rsd[:, 0:n]), reads=[r_rsd], writes=[r_rsd])
        for mt in range(nmt):
            P.add("dve", lambda e, mt=mt: e.scalar_tensor_tensor(
                out=dst[:, mt, dcol0:dcol0 + n], in0=psb[mt][:, 0:n], scalar=gam[:, mt:mt + 1], in1=rsd[:, 0:n],
                op0=ALU.mult, op1=ALU.mult), reads=[psr[mt], r_gam, r_rsd], writes=[r_dst])

    wb, wr = wnext()
    P.dma("pool", wb[:], wsl(1024, 1536), writes=[wr])
    for (c0, n) in own_blocks:
        norm_block(4, c0, n, wb, wr, 0, gq, r_gq, 512.0, ATTN_SCALE, cqn, r_cqn, c0 - LC)
    wb, wr = wnext()
    P.dma("pool", wb[:, :, 0:320], wsl(1536, 1856), writes=[wr])
    for hh in range(2):
        for a_ in range(2):
            c_src = 1792 + a_ * 32 + (1 - hh) * 16
            c_dst = 320 + a_ * 32 + hh * 16
            P.dma("pool", wb[:, :, c_dst:c_dst + 16], wsl(c_src, c_src + 16), writes=[wr])
    for (c0, n) in all_blocks:
        norm_block(2, c0, n, wb, wr, 0, gkv, r_gkv, 256.0, 1.0, ckvn, r_ckvn, c0)
        xres = xt_res(c0, n)
        for i_, wc in enumerate((256, 320)):
            if c0 < LC and i_ == 1:
                continue
            for kt in range(KT):
                P.add("pe", lambda e, kt=kt, i_=i_, wc=wc, c0=c0, n=n, wb=wb: e.matmul(
                    psb[5 + i_][0:64, 0:n], wb[:, kt, wc:wc + 64], xt[:, kt, c0:c0 + n],
                    start=(kt == 0), stop=(kt == KT - 1)), reads=xres + [wr], writes=[psr[5 + i_]])
        if c0 < LC:
            P.add("act", lambda e, c0=c0, n=n: e.copy(out=krt[:, c0:c0 + n], in_=psb[5][0:64, 0:n]), reads=[psr[5]], writes=[r_krt])
        else:
            l0 = c0 - LC
            P.add("dve", lambda e, l0=l0, n=n: e.tensor_tensor(out=tr1[:, 0:n], in0=psb[5][0:64, 0:n], in1=cosT[:, l0:l0 + n],
                                                             op=ALU.mult), reads=[psr[5], r_cos], writes=[r_tr1])
            P.add("dve", lambda e, l0=l0, n=n: e.tensor_tensor(out=tr2[:, 0:n], in0=psb[6][0:64, 0:n], in1=sinT[:, l0:l0 + n],
                                                             op=ALU.mult), reads=[psr[6], r_sin], writes=[r_tr2])
            P.add("pool", lambda e, c0=c0, n=n: e.tensor_tensor(out=krt[:, c0:c0 + n], in0=tr1[:, 0:n], in1=tr2[:, 0:n], op=ALU.add),
                  reads=[r_tr1, r_tr2], writes=[r_krt])
    dump("cqn", cqn[:], [r_cqn], [128, 4, OWN], BF16)
    dump("ckvn", ckvn[:], [r_ckvn], [128, 2, NTOK], BF16)
    dump("krt", krt[:], [r_krt], [64, NTOK], BF16)
    A.release(m_p2)
    if stop_after == "inproj":
        P.emit()
        return nc, P, dbg_out

    A.release(m_w)
    m_p3 = A.mark()
    ot_tmp, r_ott = A.tile("ot_tmp", [128, 8, OWN], BF16)
    wuq, r_wuq = A.tile("wuq", [128, 4, 1536], BF16)
    wuqs, r_wuqs = A.tile("wuqs", [128, 4, 8, 64], BF16)
    wukv, r_wukv = A.tile("wukv", [128, 2, 2048], BF16)
    P.dma("pool", wuq[:], w_uq.rearrange("(kt p) n -> p kt n", p=128), writes=[r_wuq])
    P.dma("pool", wukv[:], w_ukv.rearrange("(kt p) n -> p kt n", p=128), writes=[r_wukv])
    for kt in range(4):
        w3 = w_uq[kt * 128:(kt + 1) * 128, :].rearrange("p (h x) -> p h x", x=192)
        for hh in range(2):
            for a_ in range(2):
                cs_ = 128 + a_ * 32 + (1 - hh) * 16
                cd_ = a_ * 32 + hh * 16
                P.dma("pool", wuqs[:, kt, :, cd_:cd_ + 16], w3[:, :, cs_:cs_ + 16], writes=[r_wuqs])
    kn, r_kn = A.tile("kn", [128, 4, NTOK], BF16)
    vt, r_vt = A.tile("vt", [128, 18, 4, 128], BF16)
    qn, r_qn = A.tile("qn", [128, 4, OWN], BF16)
    qr, r_qr = A.tile("qr", [64, 4, OWN], BF16)
    pts = [A.tile("pt%d" % i, [128, 512], BF16) for i in range(3)]
    rcp, r_rcp = A.tile("rcp", [128, 512], F32)
    t3a, r_t3a = A.tile("t3a", [64, 512], F32)
    t3b, r_t3b = A.tile("t3b", [64, 512], F32)
    rotg = Rot([0, 1, 2, 3])
    for hp in range(2):
        for hl in range(4):
            h = hp * 4 + hl
            for tb in range(2):
                t0 = tb * 512
                pb_, pr_ = rotg.next()
                for kt in range(4):
                    P.add("pe", lambda e, kt=kt, pb_=pb_, h=h, t0=t0: e.matmul(
                        pb_[:, 0:512], wuq[:, kt, h * 192:h * 192 + 128], cqn[:, kt, t0:t0 + 512],
                        start=(kt == 0), stop=(kt == 3)), reads=[r_wuq, r_cqn], writes=[pr_])
                P.add("act", lambda e, pb_=pb_, hl=hl, t0=t0: e.copy(out=qn[:, hl, t0:t0 + 512], in_=pb_[:, 0:512]),
                      reads=[pr_], writes=[r_qn])
                pa_, pra_ = rotg.next()
                pw_, prw_ = rotg.next()
                for kt in range(4):
                    P.add("pe", lambda e, kt=kt, pa_=pa_, h=h, t0=t0: e.matmul(
                        pa_[0:64, 0:512], wuq[:, kt, h * 192 + 128:h * 192 + 192], cqn[:, kt, t0:t0 + 512],
                        start=(kt == 0), stop=(kt == 3)), reads=[r_wuq, r_cqn], writes=[pra_])
                for kt in range(4):
                    P.add("pe", lambda e, kt=kt, pw_=pw_, h=h, t0=t0: e.matmul(
                        pw_[0:64, 0:512], wuqs[:, kt, h, :], cqn[:, kt, t0:t0 + 512],
                        start=(kt == 0), stop=(kt == 3)), reads=[r_wuqs, r_cqn], writes=[prw_])
                P.add("dve", lambda e, pa_=pa_, t0=t0: e.tensor_tensor(out=t3a[:], in0=pa_[0:64, 0:512], in1=cosT[:, t0:t0 + 512],
                                                                    op=ALU.mult), reads=[pra_, r_cos], writes=[r_t3a])
                P.add("dve", lambda e, pw_=pw_, t0=t0: e.tensor_tensor(out=t3b[:], in0=pw_[0:64, 0:512], in1=sinT[:, t0:t0 + 512],
                                                                    op=ALU.mult), reads=[prw_, r_sin], writes=[r_t3b])
                P.add("pool", lambda e, hl=hl, t0=t0: e.tensor_tensor(out=qr[:, hl, t0:t0 + 512], in0=t3a[:], in1=t3b[:], op=ALU.add),
                      reads=[r_t3a, r_t3b], writes=[r_qr])
        for hl in range(4):
            h = hp * 4 + hl
            for (c0, n) in all_blocks:
                pb_, pr_ = rotg.next()
                for kt in range(2):
                    P.add("pe", lambda e, kt=kt, pb_=pb_, h=h, c0=c0, n=n: e.matmul(
                        pb_[:, 0:n], wukv[:, kt, h * 256:h * 256 + 128], ckvn[:, kt, c0:c0 + n],
                        start=(kt == 0), stop=(kt == 1)), reads=[r_wukv, r_ckvn], writes=[pr_])
                eng = evac_eng()
                if eng == "act":
                    P.add("act", lambda e, pb_=pb_, hl=hl, c0=c0, n=n: e.copy(out=kn[:, hl, c0:c0 + n], in_=pb_[:, 0:n]),
                          reads=[pr_], writes=[r_kn])
                else:
                    P.add("dve", lambda e, pb_=pb_, hl=hl, c0=c0, n=n: e.tensor_copy(out=kn[:, hl, c0:c0 + n], in_=pb_[:, 0:n]),
                          reads=[pr_], writes=[r_kn])
        wv4 = [wukv[:, kt, :].rearrange("p (h x) -> p h x", x=256)[:, hp * 4:(hp + 1) * 4, 128:256] for kt in range(2)]
        for ti in range(18):
            pb_, pr_ = rotg.next()
            for kt in range(2):
                P.add("pe", lambda e, kt=kt, pb_=pb_, ti=ti, wv=wv4[kt]: e.matmul(
                    pb_[:, 0:512], ckvn[:, kt, ti * 128:(ti + 1) * 128], wv,
                    start=(kt == 0), stop=(kt == 1)), reads=[r_wukv, r_ckvn], writes=[pr_])
            eng = evac_eng()
            if eng == "act":
                P.add("act", lambda e, pb_=pb_, ti=ti: e.copy(out=vt[:, ti, :, :], in_=pb_[:, 0:512].rearrange("p (h x) -> p h x", x=128)),
                      reads=[pr_], writes=[r_vt])
            else:
                P.add("dve", lambda e, pb_=pb_, ti=ti: e.tensor_copy(out=vt[:, ti, :, :], in_=pb_[:, 0:512].rearrange("p (h x) -> p h x", x=128)),
                      reads=[pr_], writes=[r_vt])
        pti = 0
        for hl in range(4):
            h = hp * 4 + hl
            for qb in range(2):
                t0 = qb * 512
                acc = (hl * 2 + qb) % 2
                pO, rO = psb[4 + acc], psr[4 + acc]
                pR, rR = psb[6 + acc], psr[6 + acc]

                def s_mm(ki, h=h, hl=hl, t0=t0):
                    pS, rS = psb[ki % 2], psr[ki % 2]
                    P.add("pe", lambda e: e.matmul(pS[:, 0:512], kn[:, hl, ki * 128:(ki + 1) * 128], qn[:, hl, t0:t0 + 512],
                                                   start=True, stop=False), reads=[r_kn, r_qn], writes=[rS])
                    P.add("pe", lambda e: e.matmul(pS[:, 0:512], krt[:, ki * 128:(ki + 1) * 128], qr[:, hl, t0:t0 + 512],
                                                   start=False, stop=True), reads=[r_krt, r_qr], writes=[rS])

                s_mm(0)
                for ki in range(18):
                    if ki + 1 < 18:
                        s_mm(ki + 1)
                    pS, rS = psb[ki % 2], psr[ki % 2]
                    ptt, ptr = pts[pti % 3]
                    pti += 1
                    P.add("act", lambda e, pS=pS, ptt=ptt: e.activation(out=ptt[:], in_=pS[:, 0:512], func=AF.Exp),
                          reads=[rS], writes=[ptr])
                    P.add("pe", lambda e, ki=ki, ptt=ptt, hl=hl, pO=pO: e.matmul(
                        pO[:, 0:512], vt[:, ki, hl, :], ptt[:], start=(ki == 0), stop=(ki == 17)),
                        reads=[r_vt, ptr], writes=[rO])
                    P.add("pe", lambda e, ki=ki, ptt=ptt, pR=pR: e.matmul(
                        pR[:, 0:512], ones_b[:], ptt[:], start=(ki == 0), stop=(ki == 17)),
                        reads=[r_onesb, ptr], writes=[rR])
                P.add("dve", lambda e, pR=pR: e.reciprocal(out=rcp[:], in_=pR[:, 0:512]), reads=[rR], writes=[r_rcp])
                P.add("dve", lambda e, pO=pO, h=h, t0=t0: e.tensor_tensor(out=ot_tmp[:, h, t0:t0 + 512], in0=pO[:, 0:512], in1=rcp[:],
                                                                       op=ALU.mult), reads=[rO, r_rcp], writes=[r_ott])
    A.release(m_mla)
    zt, r_zt = A.tile("zt", [128, 8, OWN], BF16)
    dump("ot", ot_tmp[:], [r_ott], [128, 8, OWN], BF16)
    if stop_after == "mla":
        P.emit()
        return nc, P, dbg_out

    ots_d = nc.dram_tensor("ots_scr", [128, 8, OWN], BF16, kind="Internal").ap()
    r_ots = Res("ots_d")
    P.dma("sp", ots_d, ot_tmp[:], reads=[r_ott], writes=[r_ots])
    sp_, r_sp = A.tile("sprime", [128, 288, 2, 64], BF16)
    r_hs = A.sub("hstates", r_sp)
    F1 = [128, 64]
    are, r_are = A.tile("are", F1, F32)
    aim, r_aim = A.tile("aim", F1, F32)
    dtt, r_dtt = A.tile("dtt", F1, F32)
    lrdt, r_lrdt = A.tile("lrdt", F1, F32)
    ang, r_ang = A.tile("ang", F1, F32)
    pwr, r_pwr = A.tile("pwr", [128, 24, 64], F32)
    pwi, r_pwi = A.tile("pwi", [128, 24, 64], F32)
    bbr, r_bbr = A.tile("bbr", [128, 64, 16], F32)
    bbi, r_bbi = A.tile("bbi", [128, 64, 16], F32)
    ctr, r_ctr = A.tile("ctr", [128, 64, 16], F32)
    cti, r_cti = A.tile("cti", [128, 64, 16], F32)
    dsk, r_dsk = A.tile("dsk", [128, 64], F32)
    mkf, r_mkf = A.tile("mkf", [128, 128], F32)
    mkb, r_mkb = A.tile("mkb", [128, 128], F32)
    arar, r_arar = A.tile("arar", [128, 2, 64], F32)
    aiai, r_aiai = A.tile("aiai", [128, 2, 64], F32)
    hc, r_hc = A.tile("hc", [128, 2, 64], F32)
    t1, r_t1 = A.tile("t1", [128, 2, 64], F32)
    t2, r_t2 = A.tile("t2", [128, 2, 64], F32)
    t3, r_t3 = A.tile("t3", [128, 2, 64], F32)
    pws = {}
    for nm in ("phi", "psim", "psir"):
        pws[nm] = (A.tile(nm + "_re", [128, 8, 64], F32), A.tile(nm + "_im", [128, 8, 64], F32))
    m_g0 = A.mark()
    for d_ in range(2):
        hs_ = slice(d_ * 64, (d_ + 1) * 64)
        P.dma("sp", are[hs_, :], a_re[d_].rearrange("g n -> n g"), writes=[r_are], allow_slow_non_contiguous=True)
        P.dma("sp", aim[hs_, :], a_im[d_].rearrange("g n -> n g"), writes=[r_aim], allow_slow_non_contiguous=True)
        P.dma("sp", dtt[hs_, :], log_dt[d_:d_ + 1, :].partition_broadcast(64), writes=[r_dtt])
    for s_ in range(8):
        P.dma("sp", dsk[s_ * 16:(s_ + 1) * 16, :], s5_d.rearrange("g p -> p g"), writes=[r_dsk], allow_slow_non_contiguous=True)
    cnr, r_cnr = A.tile("cnr", [128, 8, 2, 64], F32)
    cni, r_cni = A.tile("cni", [128, 8, 2, 64], F32)
    for d_ in range(2):
        P.dma("sp", cnr[:, :, d_, :], c_re[d_].rearrange("(gt gl) p n -> (gl p) gt n", gl=8), writes=[r_cnr])
        P.dma("sp", cni[:, :, d_, :], c_im[d_].rearrange("(gt gl) p n -> (gl p) gt n", gl=8), writes=[r_cni])
    for (cn_, r_cn_, ct_, r_ct_) in ((cnr, r_cnr, ctr, r_ctr), (cni, r_cni, cti, r_cti)):
        for half_ in range(2):
            pb_, pr_ = psb[half_], psr[half_]
            for q_ in range(4):
                gt_ = half_ * 4 + q_
                P.add("pe", lambda e, pb_=pb_, q_=q_, gt_=gt_, cn_=cn_: e.transpose(
                    pb_[:, q_ * 128:(q_ + 1) * 128], cn_[:, gt_, :, :].rearrange("q d n -> q (d n)"), ident_f[:]),
                    reads=[r_cn_, r_identf], writes=[pr_])
            P.add("dve", lambda e, pb_=pb_, half_=half_, ct_=ct_: e.tensor_copy(
                out=ct_[:, half_ * 32:(half_ + 1) * 32, :].rearrange("q g p -> q (g p)"), in_=pb_[:, 0:512]),
                reads=[pr_], writes=[r_ct_])

    A.release(m_g0)

    def exp_poly(dst, r_dst, z, r_z, tmp, r_tmp):
        cf = [1.0 / math.factorial(i) for i in range(11)]
        P.add("dve", lambda e: e.tensor_scalar(out=tmp, in0=z, scalar1=cf[10], scalar2=None, op0=ALU.mult),
              reads=[r_z], writes=[r_tmp])
        for i in range(9, 0, -1):
            P.add("dve", lambda e, i=i: e.scalar_tensor_tensor(out=tmp, in0=tmp, scalar=cf[i], in1=z, op0=ALU.add, op1=ALU.mult),
                  reads=[r_tmp, r_z], writes=[r_tmp])
        P.add("dve", lambda e: e.tensor_scalar(out=dst, in0=tmp, scalar1=1.0, scalar2=None, op0=ALU.add),
              reads=[r_tmp], writes=[r_dst])

    zt_, r_zt_ = A.tile("ztmp", F1, F32)
    pt_, r_pt_ = A.tile("ptmp", F1, F32)
    P.add("dve", lambda e: e.tensor_scalar(out=zt_[:], in0=dtt[:], scalar1=0.125, scalar2=None, op0=ALU.mult),
          reads=[r_dtt], writes=[r_zt_])
    exp_poly(dtt[:], r_dtt, zt_[:], r_zt_, pt_[:], r_pt_)
    for _ in range(3):
        P.add("dve", lambda e: e.tensor_tensor(out=dtt[:], in0=dtt[:], in1=dtt[:], op=ALU.mult), reads=[r_dtt], writes=[r_dtt])
    P.add("dve", lambda e: e.tensor_tensor(out=lrdt[:], in0=are[:], in1=dtt[:], op=ALU.mult), reads=[r_are, r_dtt], writes=[r_lrdt])
    P.add("dve", lambda e: e.tensor_tensor(out=ang[:], in0=aim[:], in1=dtt[:], op=ALU.mult), reads=[r_aim, r_dtt], writes=[r_ang])
    emag, r_emag = A.tile("emag", [128, 24, 64], F32)
    ka, r_ka = A.tile("ka", [128, 17, 64], F32)
    kb, r_kb = A.tile("kb", [128, 17, 64], F32)
    kc, r_kc = A.tile("kc", [128, 17, 64], F32)
    ki_, r_ki = A.tile("ki", [128, 17, 64], I32)
    P.add("pool", lambda e: e.iota(emag[:], pattern=[[1, 24], [0, 64]], base=-7, channel_multiplier=0,
                                   allow_small_or_imprecise_dtypes=True), writes=[r_emag])
    P.add("dve", lambda e: e.tensor_tensor(out=ka[:], in0=emag[:, 7:24, :], in1=ang[:].unsqueeze(1).to_broadcast([128, 17, 64]),
                                           op=ALU.mult), reads=[r_emag, r_ang], writes=[r_ka])
    P.add("dve", lambda e: e.tensor_tensor(out=emag[:], in0=emag[:], in1=lrdt[:].unsqueeze(1).to_broadcast([128, 24, 64]),
                                           op=ALU.mult), reads=[r_emag, r_lrdt], writes=[r_emag])
    P.add("dve", lambda e: e.tensor_copy(out=zt_[:], in_=emag[:, 15, :]), reads=[r_emag], writes=[r_zt_])
    P.add("act", lambda e: e.activation(out=emag[:], in_=emag[:], func=AF.Exp), reads=[r_emag], writes=[r_emag])
    exp_poly(emag[:, 15, :], r_emag, zt_[:], r_zt_, pt_[:], r_pt_)
    range_reduce("dve", kb[:], [r_kb], ka[:], [r_ka], 0.0, 0, kc[:], [r_kc], ki_[:], [r_ki])
    P.add("act", lambda e: e.activation(out=pwi[:, 7:24, :], in_=kb[:], func=AF.Sin), reads=[r_kb], writes=[r_pwi])
    range_reduce("dve", kb[:], [r_kb], ka[:], [r_ka], math.pi / 2.0, 0, kc[:], [r_kc], ki_[:], [r_ki])
    P.add("act", lambda e: e.activation(out=pwr[:, 7:24, :], in_=kb[:], func=AF.Sin), reads=[r_kb], writes=[r_pwr])
    P.add("dve", lambda e: e.tensor_tensor(out=pwr[:, 0:7, :], in0=emag[:, 0:7, :], in1=pwr[:, 14:7:-1, :], op=ALU.mult),
          reads=[r_emag, r_pwr], writes=[r_pwr])
    P.add("dve", lambda e: e.scalar_tensor_tensor(out=pwi[:, 0:7, :], in0=emag[:, 0:7, :], scalar=-1.0, in1=pwi[:, 14:7:-1, :],
                                                  op0=ALU.mult, op1=ALU.mult), reads=[r_emag, r_pwi], writes=[r_pwi])
    P.add("dve", lambda e: e.tensor_tensor(out=pwr[:, 7:24, :], in0=pwr[:, 7:24, :], in1=emag[:, 7:24, :], op=ALU.mult),
          reads=[r_emag, r_pwr], writes=[r_pwr])
    P.add("dve", lambda e: e.tensor_tensor(out=pwi[:, 7:24, :], in0=pwi[:, 7:24, :], in1=emag[:, 7:24, :], op=ALU.mult),
          reads=[r_emag, r_pwi], writes=[r_pwi])
    for c_ in range(2):
        P.add("dve", lambda e, c_=c_: e.tensor_copy(out=arar[:, c_, :], in_=pwr[:, 15, :]), reads=[r_pwr], writes=[r_arar])
    P.add("dve", lambda e: e.tensor_scalar(out=aiai[:, 0, :], in0=pwi[:, 15, :], scalar1=-1.0, scalar2=None, op0=ALU.mult),
          reads=[r_pwi], writes=[r_aiai])
    P.add("dve", lambda e: e.tensor_copy(out=aiai[:, 1, :], in_=pwi[:, 15, :]), reads=[r_pwi], writes=[r_aiai])
    A.release(m_g0)
    braw, r_braw = A.tile("braw", [128, 64, 16], F32)
    biraw, r_biraw = A.tile("biraw", [128, 64, 16], F32)
    for d_ in range(2):
        hs_ = slice(d_ * 64, (d_ + 1) * 64)
        P.dma("sp", braw[hs_, :, :], b_re[d_].rearrange("g n p -> n g p"), writes=[r_braw])
        P.dma("sp", biraw[hs_, :, :], b_im[d_].rearrange("g n p -> n g p"), writes=[r_biraw])
    zq_, r_zq_ = A.tile("ztmp2", F1, F32)
    den, r_den = A.tile("den", F1, F32)
    cor, r_cor = A.tile("cor", F1, F32)
    coi, r_coi = A.tile("coi", F1, F32)
    nr_, r_nr = A.tile("nr", F1, F32)
    P.add("dve", lambda e: e.tensor_tensor(out=den[:], in0=are[:], in1=are[:], op=ALU.mult), reads=[r_are], writes=[r_den])
    P.add("dve", lambda e: e.tensor_tensor(out=zq_[:], in0=aim[:], in1=aim[:], op=ALU.mult), reads=[r_aim], writes=[r_zq_])
    P.add("dve", lambda e: e.tensor_tensor(out=den[:], in0=den[:], in1=zq_[:], op=ALU.add), reads=[r_den, r_zq_], writes=[r_den])
    P.add("dve", lambda e: e.reciprocal(out=den[:], in_=den[:]), reads=[r_den], writes=[r_den])
    P.add("dve", lambda e: e.tensor_scalar(out=nr_[:], in0=pwr[:, 8, :], scalar1=-1.0, scalar2=None, op0=ALU.add),
          reads=[r_pwr], writes=[r_nr])
    P.add("dve", lambda e: e.tensor_tensor(out=cor[:], in0=nr_[:], in1=are[:], op=ALU.mult), reads=[r_nr, r_are], writes=[r_cor])
    P.add("dve", lambda e: e.tensor_tensor(out=zq_[:], in0=pwi[:, 8, :], in1=aim[:], op=ALU.mult), reads=[r_pwi, r_aim], writes=[r_zq_])
    P.add("dve", lambda e: e.tensor_tensor(out=cor[:], in0=cor[:], in1=zq_[:], op=ALU.add), reads=[r_cor, r_zq_], writes=[r_cor])
    P.add("dve", lambda e: e.tensor_tensor(out=cor[:], in0=cor[:], in1=den[:], op=ALU.mult), reads=[r_cor, r_den], writes=[r_cor])
    P.add("dve", lambda e: e.tensor_tensor(out=coi[:], in0=pwi[:, 8, :], in1=are[:], op=ALU.mult), reads=[r_pwi, r_are], writes=[r_coi])
    P.add("dve", lambda e: e.tensor_tensor(out=zq_[:], in0=nr_[:], in1=aim[:], op=ALU.mult), reads=[r_nr, r_aim], writes=[r_zq_])
    P.add("dve", lambda e: e.tensor_tensor(out=coi[:], in0=coi[:], in1=zq_[:], op=ALU.subtract), reads=[r_coi, r_zq_], writes=[r_coi])
    P.add("dve", lambda e: e.tensor_tensor(out=coi[:], in0=coi[:], in1=den[:], op=ALU.mult), reads=[r_coi, r_den], writes=[r_coi])
    B3 = [128, 64, 16]

    def bc3(t):
        return t[:].unsqueeze(2).to_broadcast(B3)

    P.add("dve", lambda e: e.tensor_tensor(out=bbr[:], in0=braw[:], in1=bc3(cor), op=ALU.mult), reads=[r_braw, r_cor], writes=[r_bbr])
    P.add("dve", lambda e: e.tensor_tensor(out=bbi[:], in0=biraw[:], in1=bc3(coi), op=ALU.mult), reads=[r_biraw, r_coi], writes=[r_bbi])
    P.add("dve", lambda e: e.tensor_tensor(out=bbr[:], in0=bbr[:], in1=bbi[:], op=ALU.subtract), reads=[r_bbr, r_bbi], writes=[r_bbr])
    P.add("dve", lambda e: e.tensor_tensor(out=bbi[:], in0=biraw[:], in1=bc3(cor), op=ALU.mult), reads=[r_biraw, r_cor], writes=[r_bbi])
    P.add("dve", lambda e: e.tensor_tensor(out=braw[:], in0=braw[:], in1=bc3(coi), op=ALU.mult), reads=[r_braw, r_coi], writes=[r_braw])
    P.add("dve", lambda e: e.tensor_tensor(out=bbi[:], in0=bbi[:], in1=braw[:], op=ALU.add), reads=[r_bbi, r_braw], writes=[r_bbi])
    P.add("pool", lambda e: e.iota(mkf[:], pattern=[[16, 8], [0, 16]], base=0, channel_multiplier=-1,
                                   allow_small_or_imprecise_dtypes=True), writes=[r_mkf])
    P.add("dve", lambda e: e.tensor_scalar(out=mkb[:], in0=mkf[:], scalar1=0.5, scalar2=None, op0=ALU.is_le),
          reads=[r_mkf], writes=[r_mkb])
    P.add("dve", lambda e: e.tensor_scalar(out=mkf[:], in0=mkf[:], scalar1=-15.5, scalar2=None, op0=ALU.is_ge),
          reads=[r_mkf], writes=[r_mkf])
    FW, BW = slice(0, 64), slice(64, 128)
    for nm, fsl, bsl in (("phi", slice(7, None, -1), slice(7, 15)), ("psim", slice(7, 15), slice(7, None, -1)),
                         ("psir", slice(15, 23), slice(15, 7, -1))):
        for ri, src, r_src in ((0, pwr, r_pwr), (1, pwi, r_pwi)):
            (tt, r_tt) = pws[nm][ri]
            P.add("dve", lambda e, tt=tt, src=src, fsl=fsl: e.tensor_copy(out=tt[FW, :, :], in_=src[FW, fsl, :]),
                  reads=[r_src], writes=[r_tt])
            P.add("dve", lambda e, tt=tt, src=src, bsl=bsl: e.tensor_copy(out=tt[BW, :, :], in_=src[BW, bsl, :]),
                  reads=[r_src], writes=[r_tt])
    A.release(m_g0)
    dump("dtt", dtt[:], [r_dtt], [128, 64])
    dump("lrdt", lrdt[:], [r_lrdt], [128, 64])
    dump("ang", ang[:], [r_ang], [128, 64])
    dump("are", are[:], [r_are], [128, 64])
    dump("pwr", pwr[:], [r_pwr], [128, 24, 64])
    dump("pwi", pwi[:], [r_pwi], [128, 24, 64])
    dump("bbr", bbr[:], [r_bbr], [128, 64, 16])
    dump("ctr", ctr[:], [r_ctr], [128, 64, 16])

    G8 = [128, 8, 8, 16]
    m_b = A.mark()
    phr, r_phr = A.tile("phr", G8, BF16)
    phi_, r_phi = A.tile("phi", G8, BF16)
    gtm, r_gtm = A.tile("gtm", G8, F32)
    gtn, r_gtn = A.tile("gtn", G8, F32)
    phr2, r_phr2 = A.tile("phr2", G8, BF16)
    phi2, r_phi2 = A.tile("phi2", G8, BF16)
    PHB = [(phr, r_phr, phi_, r_phi), (phr2, r_phr2, phi2, r_phi2)]

    def gen_cplx(outr, r_outr, outi, r_outi, tab, vr, r_vr, vi, r_vi, gt_, conj_neg_im=False):
        (tr_, r_tr_), (ti_, r_ti_) = tab
        gs = slice(gt_ * 8, (gt_ + 1) * 8)
        tb_r = tr_[:, :, gs].rearrange("q s g -> q g s").unsqueeze(3).to_broadcast(G8)
        tb_i = ti_[:, :, gs].rearrange("q s g -> q g s").unsqueeze(3).to_broadcast(G8)
        v_r = vr[:, gs, :].unsqueeze(2).to_broadcast(G8)
        v_i = vi[:, gs, :].unsqueeze(2).to_broadcast(G8)
        P.add("dve", lambda e: e.tensor_tensor(out=gtn[:], in0=tb_r, in1=v_r, op=ALU.mult), reads=[r_tr_, r_vr], writes=[r_gtn])
        P.add("pool", lambda e: e.tensor_tensor(out=gtm[:], in0=tb_i, in1=v_i, op=ALU.mult), reads=[r_ti_, r_vi], writes=[r_gtm])
        P.add("dve", lambda e: e.tensor_tensor(out=outr[:], in0=gtn[:], in1=gtm[:], op=ALU.subtract),
              reads=[r_gtn, r_gtm], writes=[r_outr])
        P.add("dve", lambda e: e.tensor_tensor(out=gtn[:], in0=tb_r, in1=v_i, op=ALU.mult), reads=[r_tr_, r_vi], writes=[r_gtn])
        P.add("pool", lambda e: e.tensor_tensor(out=gtm[:], in0=tb_i, in1=v_r, op=ALU.mult), reads=[r_ti_, r_vr], writes=[r_gtm])
        if conj_neg_im:
            P.add("dve", lambda e: e.scalar_tensor_tensor(out=outi[:], in0=gtn[:], scalar=-1.0, in1=gtm[:], op0=ALU.mult,
                                                          op1=ALU.subtract), reads=[r_gtn, r_gtm], writes=[r_outi])
        else:
            P.add("dve", lambda e: e.tensor_tensor(out=outi[:], in0=gtn[:], in1=gtm[:], op=ALU.add),
                  reads=[r_gtn, r_gtm], writes=[r_outi])

    m_ws = A.mark()
    wsb = [A.tile("ws%d" % i, [128, 8, 2, 128], BF16) for i in range(2)]
    for gt_ in range(8):
        ws, r_ws = wsb[gt_ % 2]
        pa_r, r_pa_r, pa_i, r_pa_i = PHB[gt_ % 2]
        gen_cplx(pa_r, r_pa_r, pa_i, r_pa_i, pws["phi"], bbr, r_bbr, bbi, r_bbi, gt_)
        for half_ in range(2):
            pb_, pr_ = psb[half_ % 2], psr[half_ % 2]
            pbh = pb_[:].bitcast(BF16)
            for q_ in range(8):
                gl = half_ * 4 + q_ // 2
                src_ = (pa_r, pa_i)[q_ % 2]
                r_src_ = (r_pa_r, r_pa_i)[q_ % 2]
                P.add("pe", lambda e, pbh=pbh, q_=q_, gl=gl, src_=src_: e.transpose(
                    pbh[:, q_ * 128:(q_ + 1) * 128], src_[:, gl, :, :].rearrange("q s p -> q (s p)"), ident_b[:]),
                    reads=[r_src_, r_identb], writes=[pr_])
            P.add("act", lambda e, pbh=pbh, half_=half_, ws=ws: e.copy(
                out=ws[:, half_ * 4:half_ * 4 + 4, :, :].rearrange("q g r n -> q (g r n)"), in_=pbh[:, 0:1024]),
                reads=[pr_], writes=[r_ws])
        pY, rY = psb[4 + gt_ % 2], psr[4 + gt_ % 2]
        for gl in range(8):
            g = gt_ * 8 + gl
            pX, rX = psb[2 + gl % 2], psr[2 + gl % 2]
            for ri in range(2):
                P.add("pe", lambda e, pX=pX, gl=gl, g=g, ri=ri, ws=ws: e.matmul(
                    pX[:, ri * 256:(ri + 1) * 256], ws[:, gl, ri, :], ug[:, g, 32:288], start=True, stop=True),
                    reads=[r_ws, r_ug], writes=[rX])
                P.add("pe", lambda e, pY=pY, gl=gl, g=g, ri=ri, ws=ws: e.matmul(
                    pY[:, gl * 64 + ri * 32:gl * 64 + ri * 32 + 32], ws[:, gl, ri, :], ug[:, g, 0:32], start=True, stop=True),
                    reads=[r_ws, r_ug], writes=[rY])
            eng = evac_eng()
            src = pX[:, 0:512].rearrange("q (r c) -> q c r", r=2)
            if eng == "act":
                P.add("act", lambda e, src=src, g=g: e.copy(out=sp_[:, 32:288, :, g], in_=src), reads=[rX], writes=[r_sp])
            else:
                P.add("dve", lambda e, src=src, g=g: e.tensor_copy(out=sp_[:, 32:288, :, g], in_=src), reads=[rX], writes=[r_sp])
        srcY = pY[:, 0:512].rearrange("q (g r c) -> q c r g", g=8, r=2)
        P.add("dve", lambda e, srcY=srcY, gt_=gt_: e.tensor_copy(out=sp_[:, 0:32, :, gt_ * 8:(gt_ + 1) * 8], in_=srcY),
              reads=[rY], writes=[r_sp])
    dump("sprime", sp_[:], [r_sp], [128, 288, 2, 64], BF16)
    if stop_after == "s5a":
        P.emit()
        return nc, P, dbg_out

    wmq = [A.tile("wmq%d" % i, [128, KT, 128], BF16) for i in range(2)]
    mod_finish = mod_slabs(4, 12, 7, cols=128, bufs=wmq)
    t3b_, r_t3b_ = A.tile("t3b", [128, 2, 64], F32)
    t3s = [(t3, r_t3), (t3b_, r_t3b_)]
    P.add("dve", lambda e: e.memset(hc[:], 0.0), writes=[r_hc])
    for i in range(288):
        qf = i
        qb = 31 - i if i < 32 else 319 - i
        tt3, r_tt3 = t3s[i % 2]
        P.add("dve", lambda e: e.tensor_tensor(out=t1[:], in0=hc[:], in1=arar[:], op=ALU.mult), reads=[r_hc, r_arar], writes=[r_t1])
        P.add("dve", lambda e: e.tensor_tensor(out=t2[:], in0=hc[:, ::-1, :], in1=aiai[:], op=ALU.mult),
              reads=[r_hc, r_aiai], writes=[r_t2])
        P.add("dve", lambda e, tt3=tt3: e.tensor_tensor(out=tt3[:], in0=t1[:], in1=t2[:], op=ALU.add),
              reads=[r_t1, r_t2], writes=[r_tt3])
        P.add("dve", lambda e, qf=qf, tt3=tt3: e.tensor_tensor(out=hc[FW, :, :], in0=tt3[FW, :, :], in1=sp_[FW, qf, :, :], op=ALU.add),
              reads=[r_tt3, r_sp], writes=[r_hc])
        P.add("dve", lambda e, qb=qb, tt3=tt3: e.tensor_tensor(out=hc[BW, :, :], in0=tt3[BW, :, :], in1=sp_[BW, qb, :, :], op=ALU.add),
              reads=[r_tt3, r_sp], writes=[r_hc])
        if 32 <= qf <= 159:
            P.add("act", lambda e, qf=qf, tt3=tt3: e.copy(out=sp_[FW, qf, :, :], in_=tt3[FW, :, :]), reads=[r_tt3, r_hc], writes=[r_hs])
        if 32 <= qb <= 159:
            P.add("act", lambda e, qb=qb, tt3=tt3: e.copy(out=sp_[BW, qb, :, :], in_=tt3[BW, :, :]), reads=[r_tt3, r_hc], writes=[r_hs])
    dump("hstates", sp_[:], [r_hs], [128, 288, 2, 64], BF16)
    mod_finish()
    P.add("dve", lambda e: e.tensor_scalar(out=s2[:], in0=modt[:, 64:80, 0], scalar1=1.0, scalar2=None, op0=ALU.add),
          reads=[r_modt], writes=[r_s2])
    P.add("dve", lambda e: e.tensor_tensor(out=s2[:], in0=s2[:], in1=n2t[:], op=ALU.mult), reads=[r_s2, r_n2], writes=[r_s2])
    if stop_after == "s5b":
        P.emit()
        return nc, P, dbg_out

    A.release(m_ws)
    psr_, r_psr = A.tile("psr_", G8, BF16)
    psi_, r_psi = A.tile("psi_", G8, BF16)
    psr2, r_psr2 = A.tile("psr2", G8, BF16)
    psi2, r_psi2 = A.tile("psi2", G8, BF16)
    PSB = [(psr_, r_psr, psi_, r_psi), (psr2, r_psr2, psi2, r_psi2)]
    mi, r_mi = A.tile("mi", [128, 8, 128], BF16)
    mtm4, r_mtm4 = A.tile("mtm4", [128, 4, 128], F32)
    dg4, r_dg4 = A.tile("dg4", [128, 4, 128], F32)
    zc, r_zc = A.tile("zc", [128, 8, 128], BF16)
    M4 = [128, 4, 128]
    for gt_ in range(8):
        fr, r_fr, fi, r_fi = PHB[gt_ % 2]
        qr_, r_qr_, qi_, r_qi_ = PSB[gt_ % 2]
        gen_cplx(fr, r_fr, fi, r_fi, pws["phi"], bbr, r_bbr, bbi, r_bbi, gt_)
        gen_cplx(qr_, r_qr_, qi_, r_qi_, pws["psim"], ctr, r_ctr, cti, r_cti, gt_, conj_neg_im=True)
        for hb in range(2):
            pF, rF = psb[0 + 6 * hb], psr[0 + 6 * hb]
            pB, rB = psb[1 + 6 * hb], psr[1 + 6 * hb]
            g0 = gt_ * 8 + hb * 4
            for gq in range(4):
                gl = hb * 4 + gq
                for (pp, rr_, hsl) in ((pF, rF, FW), (pB, rB, BW)):
                    osl = pp[:, gq * 128:(gq + 1) * 128]
                    P.add("pe", lambda e, osl=osl, hsl=hsl, gl=gl, fr=fr, qr_=qr_: e.matmul(
                        osl, fr[hsl, gl, :, :].rearrange("q s p -> q (s p)"), qr_[hsl, gl, :, :].rearrange("q s p -> q (s p)"),
                        start=True, stop=False), reads=[r_fr, r_qr_], writes=[rr_])
                    P.add("pe", lambda e, osl=osl, hsl=hsl, gl=gl, fi=fi, qi_=qi_: e.matmul(
                        osl, fi[hsl, gl, :, :].rearrange("q s p -> q (s p)"), qi_[hsl, gl, :, :].rearrange("q s p -> q (s p)"),
                        start=False, stop=True), reads=[r_fi, r_qi_], writes=[rr_])
            P.add("pool", lambda e, g0=g0: e.tensor_tensor(
                out=dg4[:], in0=ident_f[:].unsqueeze(1).to_broadcast(M4), in1=dsk[:, g0:g0 + 4].unsqueeze(2).to_broadcast(M4),
                op=ALU.mult), reads=[r_identf, r_dsk], writes=[r_dg4])
            P.add("dve", lambda e, pF=pF: e.tensor_tensor(
                out=mtm4[:], in0=pF[:, 0:512].rearrange("q (g c) -> q g c", g=4), in1=mkf[:].unsqueeze(1).to_broadcast(M4),
                op=ALU.mult), reads=[rF, r_mkf], writes=[r_mtm4])
            P.add("pool", lambda e: e.tensor_tensor(out=mtm4[:], in0=mtm4[:], in1=dg4[:], op=ALU.add),
                  reads=[r_mtm4, r_dg4], writes=[r_mtm4])
            P.add("dve", lambda e, pB=pB, hb=hb: e.tensor_tensor(
                out=mi[:, hb * 4:(hb + 1) * 4, :], in0=pB[:, 0:512].rearrange("q (g c) -> q g c", g=4),
                in1=mkb[:].unsqueeze(1).to_broadcast(M4), op=ALU.mult), reads=[rB, r_mkb], writes=[r_mi])
            P.add("pool", lambda e, hb=hb: e.tensor_tensor(out=mi[:, hb * 4:(hb + 1) * 4, :], in0=mi[:, hb * 4:(hb + 1) * 4, :],
                                                          in1=mtm4[:], op=ALU.add), reads=[r_mi, r_mtm4], writes=[r_mi])
        for hb in range(2):
            pO, rO = psb[2 + hb], psr[2 + hb]
            for gq in range(4):
                gl = hb * 4 + gq
                g = gt_ * 8 + gl
                osl = pO[:, gq * 128:(gq + 1) * 128]
                P.add("pe", lambda e, osl=osl, gl=gl, g=g: e.matmul(osl, ug[:, g, 32:160], mi[:, gl, :], start=True, stop=False),
                      reads=[r_ug, r_mi], writes=[rO])
                P.add("pe", lambda e, osl=osl, gl=gl, g=g, qr_=qr_: e.matmul(osl, sp_[:, 32:160, 0, g], qr_[:, gl, :, :].rearrange("q s p -> q (s p)"),
                                                                 start=False, stop=False), reads=[r_hs, r_qr_], writes=[rO])
                P.add("pe", lambda e, osl=osl, gl=gl, g=g, qi_=qi_: e.matmul(osl, sp_[:, 32:160, 1, g], qi_[:, gl, :, :].rearrange("q s p -> q (s p)"),
                                                                 start=False, stop=True), reads=[r_hs, r_qi_], writes=[rO])
            P.add("act", lambda e, hb=hb, pO=pO: e.activation(
                out=zc[:, :, hb * 64:(hb + 1) * 64].rearrange("c j (g p) -> c g j p", g=4),
                in_=pO[:, 0:512].rearrange("c (g j p) -> c g j p", g=4, j=8), func=AF.Gelu_apprx_tanh),
                reads=[rO], writes=[r_zc])
        pT, rT = psb[4 + gt_ % 2], psr[4 + gt_ % 2]
        pTb = pT[:].bitcast(BF16)
        for j in range(8):
            P.add("pe", lambda e, pTb=pTb, j=j: e.transpose(pTb[:, j * 128:(j + 1) * 128], zc[:, j, :], ident_b[:]),
                  reads=[r_zc, r_identb], writes=[rT])
        P.add("dve", lambda e, pTb=pTb, gt_=gt_: e.tensor_copy(
            out=zt[:, gt_, :].rearrange("q (c j) -> q j c", j=8), in_=pTb[:, 0:1024].rearrange("q (j c) -> q j c", j=8)),
            reads=[rT], writes=[r_zt])
    dump("zt", zt[:], [r_zt], [128, 8, OWN], BF16)
    A.release(m_b)
    if stop_after == "s5":
        P.emit()
        return nc, P, dbg_out

    A.release(m_ug)
    zt2, r_zt2 = A.tile("zt2", [128, 8, OWN], BF16)
    for hh_ in range(2):
        P.add("pool", lambda e, hh_=hh_: e.tensor_copy(out=zt2[:, hh_ * 4:(hh_ + 1) * 4, :], in_=zt[:, hh_ * 4:(hh_ + 1) * 4, :]),
              reads=[r_zt], writes=[r_zt2])
    m5 = A.mark()
    A.release(m5)
    mg, r_mg = A.tile("mg", [128, KT, OWN], BF16)
    m5a = A.mark()
    otl, r_otl = A.tile("otl", [128, 8, OWN], BF16)
    xto, r_xto = A.tile("xto", [128, KT, OWN], BF16)
    P.dma("sp", otl[:], ots_d, reads=[r_ots], writes=[r_otl])
    P.dma("sp", xto[:], xts_d, reads=[r_xts], writes=[r_xto])
    wmg = [A.tile("wmg%d" % i, [128, 8, 384], BF16) for i in range(2)]
    wgt = [A.tile("wgt%d" % i, [128, KT, 256], BF16) for i in range(2)]
    ta, r_ta = A.tile("ta", [128, 512], F32)
    tb_, r_tb = A.tile("tb_", [128, 512], F32)
    tc, r_tc = A.tile("tc", [128, 512], F32)
    td, r_td = A.tile("td", [128, 512], F32)
    te, r_te = A.tile("te", [128, 512], F32)
    rot8 = Rot([0, 1, 2, 3, 4, 5, 6])
    dump("zt2", zt2[:], [r_zt2], [128, 8, OWN], BF16)
    dump("otl", otl[:], [r_otl], [128, 8, OWN], BF16)
    dump("xto", xto[:], [r_xto], [128, KT, OWN], BF16)

    def mm_acc(pb_, pr_, nk, lhs_fn, rhs_fn, reads):
        for kt in range(nk):
            P.add("pe", lambda e, kt=kt: e.matmul(pb_[:, 0:512], lhs_fn(kt), rhs_fn(kt), start=(kt == 0), stop=(kt == nk - 1)),
                  reads=reads, writes=[pr_])

    def merge_dma(mt_):
        wm_, r_wm_ = wmg[mt_ % 2]
        wg_, r_wg_ = wgt[mt_ % 2]
        c_ = mt_ * 128
        P.dma("pool", wm_[:, :, 0:128], w_glu[:, c_:c_ + 128].rearrange("(kt p) n -> p kt n", p=128), writes=[r_wm_])
        P.dma("pool", wm_[:, :, 128:256], w_glu[:, D + c_:D + c_ + 128].rearrange("(kt p) n -> p kt n", p=128), writes=[r_wm_])
        P.dma("pool", wm_[:, :, 256:384], w_mla_o[:, c_:c_ + 128].rearrange("(kt p) n -> p kt n", p=128), writes=[r_wm_])
        P.dma("pool", wg_[:, :, 0:128], w_in[:, 1856 + c_:1856 + c_ + 128].rearrange("(kt p) n -> p kt n", p=128), writes=[r_wg_])
        P.dma("pool", wg_[:, :, 128:256], w_in[:, 1856 + D + c_:1856 + D + c_ + 128].rearrange("(kt p) n -> p kt n", p=128),
              writes=[r_wg_])

    merge_dma(0)
    for mt in range(16):
        wm, r_wm = wmg[mt % 2]
        wg, r_wg = wgt[mt % 2]
        if mt + 1 < 16:
            merge_dma(mt + 1)
        for tbk in range(2):
            t0 = tbk * 512
            pB, rB_ = rot8.next()
            mm_acc(pB, rB_, 8, lambda kt, wm=wm: wm[:, kt, 128:256], lambda kt, t0=t0: zt2[:, kt, t0:t0 + 512], [r_wm, r_zt2])
            P.add("act", lambda e, pB=pB: e.activation(out=ta[:], in_=pB[:, 0:512], func=AF.Sigmoid), reads=[rB_], writes=[r_ta])
            pA, rA_ = rot8.next()
            mm_acc(pA, rA_, 8, lambda kt, wm=wm: wm[:, kt, 0:128], lambda kt, t0=t0: zt2[:, kt, t0:t0 + 512], [r_wm, r_zt2])
            P.add("dve", lambda e, pA=pA: e.tensor_tensor(out=tb_[:], in0=pA[:, 0:512], in1=ta[:], op=ALU.mult),
                  reads=[rA_, r_ta], writes=[r_tb])
            pG, rG_ = rot8.next()
            mm_acc(pG, rG_, KT, lambda kt, wg=wg: wg[:, kt, 0:128], lambda kt, t0=t0: xto[:, kt, t0:t0 + 512], [r_wg, r_xto])
            P.add("act", lambda e, pG=pG: e.activation(out=tc[:], in_=pG[:, 0:512], func=AF.Sigmoid), reads=[rG_], writes=[r_tc])
            P.add("dve", lambda e: e.tensor_tensor(out=tb_[:], in0=tb_[:], in1=tc[:], op=ALU.mult), reads=[r_tb, r_tc], writes=[r_tb])
            pH, rH_ = rot8.next()
            mm_acc(pH, rH_, KT, lambda kt, wg=wg: wg[:, kt, 128:256], lambda kt, t0=t0: xto[:, kt, t0:t0 + 512], [r_wg, r_xto])
            P.add("act", lambda e, pH=pH: e.activation(out=td[:], in_=pH[:, 0:512], func=AF.Sigmoid), reads=[rH_], writes=[r_td])
            pM, rM_ = rot8.next()
            mm_acc(pM, rM_, 8, lambda kt, wm=wm: wm[:, kt, 256:384], lambda kt, t0=t0: otl[:, kt, t0:t0 + 512], [r_wm, r_otl])
            P.add("dve", lambda e, pM=pM: e.tensor_tensor(out=te[:], in0=pM[:, 0:512], in1=td[:], op=ALU.mult),
                  reads=[rM_, r_td], writes=[r_te])
            P.add("dve", lambda e, mt=mt, t0=t0: e.tensor_tensor(out=mg[:, mt, t0:t0 + 512], in0=tb_[:], in1=te[:], op=ALU.add),
                  reads=[r_tb, r_te], writes=[r_mg])
    dump("mg", mg[:], [r_mg], [128, KT, OWN], BF16)
    A.release(m5a)
    xT, r_xT = A.tile("xT", [128, KT, OWN], F32)
    m5b = A.mark()
    xrb = [A.tile("xrb%d" % i, [128, D], F32) for i in range(2)]
    for ti in range(8):
        xb, xr = xrb[ti % 2]
        P.dma("sp", xb[:], xs[LC + ti * 128:LC + (ti + 1) * 128, :], writes=[xr])
        for dq in range(4):
            pb_, pr_ = rot8.next()
            for i_ in range(4):
                dt_ = dq * 4 + i_
                P.add("pe", lambda e, pb_=pb_, i_=i_, dt_=dt_, xb=xb: e.transpose(
                    pb_[:, i_ * 128:(i_ + 1) * 128], xb[:, dt_ * 128:(dt_ + 1) * 128], ident_f[:]),
                    reads=[xr, r_identf], writes=[pr_])
            eng = evac_eng()
            src = pb_[:, 0:512].rearrange("q (i t) -> q i t", i=4)
            if eng == "act":
                P.add("act", lambda e, src=src, dq=dq, ti=ti: e.copy(out=xT[:, dq * 4:(dq + 1) * 4, ti * 128:(ti + 1) * 128], in_=src),
                      reads=[pr_], writes=[r_xT])
            else:
                P.add("dve", lambda e, src=src, dq=dq, ti=ti: e.tensor_copy(out=xT[:, dq * 4:(dq + 1) * 4, ti * 128:(ti + 1) * 128], in_=src),
                      reads=[pr_], writes=[r_xT])
    wob = [A.tile("wob%d" % i, [128, KT, 128], BF16) for i in range(2)]
    for mt in range(16):
        wo, r_wo = wob[mt % 2]
        P.dma("pool", wo[:], w_out[:, mt * 128:(mt + 1) * 128].rearrange("(kt p) n -> p kt n", p=128), writes=[r_wo])
        for tbk in range(2):
            t0 = tbk * 512
            pb_, pr_ = rot8.next()
            mm_acc(pb_, pr_, KT, lambda kt, wo=wo: wo[:, kt, :], lambda kt, t0=t0: mg[:, kt, t0:t0 + 512], [r_wo, r_mg])
            P.add("dve", lambda e, pb_=pb_, mt=mt, t0=t0: e.scalar_tensor_tensor(
                out=xT[:, mt, t0:t0 + 512], in0=pb_[:, 0:512], scalar=modt[:, 32 + mt, 0:1], in1=xT[:, mt, t0:t0 + 512],
                op0=ALU.mult, op1=ALU.add), reads=[pr_, r_modt, r_xT], writes=[r_xT])
    dump("x1", xT[:], [r_xT], [128, KT, OWN])
    A.release(m5b)
    if stop_after == "merge":
        P.emit()
        return nc, P, dbg_out

    sqb, r_sqb = A.tile("sqb", [128, KT, 512], BF16)
    rn, r_rn = A.tile("rn", [128, OWN], F32)
    tn, r_tn = A.tile("tn", [128, 512], F32)

    def rms_bcast(src, r_src, sqt, r_sqt, rnt, r_rnt):
        for tbk in range(2):
            t0 = tbk * 512
            for dt_ in range(KT):
                P.add("act", lambda e, dt_=dt_, t0=t0: e.activation(out=sqt[:, dt_, :], in_=src[:, dt_, t0:t0 + 512], func=AF.Square),
                      reads=[r_src], writes=[r_sqt])
            pb_, pr_ = rot8.next()
            mm_acc(pb_, pr_, KT, lambda kt: ones_b[:], lambda kt: sqt[:, kt, :], [r_onesb, r_sqt])
            P.add("act", lambda e, pb_=pb_, t0=t0: e.activation(out=rnt[:, t0:t0 + 512], in_=pb_[:, 0:512], func=AF.Sqrt,
                                                               scale=1.0 / D, bias=EPS), reads=[pr_], writes=[r_rnt])
            P.add("dve", lambda e, t0=t0: e.reciprocal(out=rnt[:, t0:t0 + 512], in_=rnt[:, t0:t0 + 512]), reads=[r_rnt], writes=[r_rnt])

    rms_bcast(xT, r_xT, sqb, r_sqb, rn, r_rn)
    h2, r_h2 = A.tile("h2", [128, KT, OWN], BF16)
    for tbk in range(2):
        t0 = tbk * 512
        for dt_ in range(KT):
            P.add("dve", lambda e, dt_=dt_, t0=t0: e.tensor_tensor(out=tn[:], in0=xT[:, dt_, t0:t0 + 512], in1=rn[:, t0:t0 + 512],
                                                                op=ALU.mult), reads=[r_xT, r_rn], writes=[r_tn])
            P.add("act", lambda e, dt_=dt_, t0=t0: e.activation(out=h2[:, dt_, t0:t0 + 512], in_=tn[:], func=AF.Identity,
                                                               scale=s2[:, dt_:dt_ + 1], bias=modt[:, 48 + dt_, 0:1]),
                  reads=[r_tn, r_s2, r_modt], writes=[r_h2])
    dump("h2", h2[:], [r_h2], [128, KT, OWN], BF16)
    if stop_after == "norm2":
        P.emit()
        return nc, P, dbg_out
    P.dma("sp", x1s_d, xT[:], reads=[r_xT], writes=[r_x1s])
    h2b, r_h2b = h2, r_h2
    h2_end = A.off
    A.release(m5)
    hh, r_hh = A.tile("hh", [128, 44, OWN], BF16)
    hh_end = A.off
    A.off = h2_end
    m6 = A.mark()
    wfb = [A.tile("wfb%d" % i, [128, KT, 256], BF16) for i in range(2)]
    sa, r_sa = A.tile("sa", [128, 512], F32)
    for j in range(44):
        wf, r_wf = wfb[j % 2]
        P.dma("pool", wf[:, :, 0:128], w_ffn_in[:, j * 128:(j + 1) * 128].rearrange("(kt p) n -> p kt n", p=128), writes=[r_wf])
        P.dma("pool", wf[:, :, 128:256], w_ffn_in[:, DFF + j * 128:DFF + (j + 1) * 128].rearrange("(kt p) n -> p kt n", p=128),
              writes=[r_wf])
        for tbk in range(2):
            t0 = tbk * 512
            pA, rA_ = rot8.next()
            mm_acc(pA, rA_, KT, lambda kt, wf=wf: wf[:, kt, 0:128], lambda kt, t0=t0: h2b[:, kt, t0:t0 + 512], [r_wf, r_h2b])
            P.add("act", lambda e, pA=pA: e.activation(out=sa[:], in_=pA[:, 0:512], func=AF.Silu), reads=[rA_], writes=[r_sa])
            pB, rB_ = rot8.next()
            mm_acc(pB, rB_, KT, lambda kt, wf=wf: wf[:, kt, 128:256], lambda kt, t0=t0: h2b[:, kt, t0:t0 + 512], [r_wf, r_h2b])
            P.add("dve", lambda e, pB=pB, j=j, t0=t0: e.tensor_tensor(out=hh[:, j, t0:t0 + 512], in0=pB[:, 0:512], in1=sa[:],
                                                                    op=ALU.mult), reads=[rB_, r_sa], writes=[r_hh])
    A.release(m6)
    dump("hh", hh[:], [r_hh], [128, 44, OWN], BF16)
    A.off = hh_end
    xT2, r_xT2 = A.tile("xT2", [128, KT, OWN], F32)
    P.dma("sp", xT2[:], x1s_d, reads=[r_x1s], writes=[r_xT2])
    wfo = [A.tile("wfo%d" % i, [128, 44, 128], BF16) for i in range(2)]
    for mt in range(16):
        wo, r_wo = wfo[mt % 2]
        P.dma("pool", wo[:], w_ffn_out[:, mt * 128:(mt + 1) * 128].rearrange("(kt p) n -> p kt n", p=128), writes=[r_wo])
        for tbk in range(2):
            t0 = tbk * 512
            pb_, pr_ = rot8.next()
            mm_acc(pb_, pr_, 44, lambda kt, wo=wo: wo[:, kt, :], lambda kt, t0=t0: hh[:, kt, t0:t0 + 512], [r_wo, r_hh])
            P.add("dve", lambda e, pb_=pb_, mt=mt, t0=t0: e.scalar_tensor_tensor(
                out=xT2[:, mt, t0:t0 + 512], in0=pb_[:, 0:512], scalar=modt[:, 80 + mt, 0:1], in1=xT2[:, mt, t0:t0 + 512],
                op0=ALU.mult, op1=ALU.add), reads=[pr_, r_modt, r_xT2], writes=[r_xT2])
    dump("x2", xT2[:], [r_xT2], [128, KT, OWN])

    A.release(m5)
    sqb2, r_sqb2 = A.tile("sqb2", [128, KT, 512], BF16)
    rn2, r_rn2 = A.tile("rn2", [128, OWN], F32)
    rms_bcast(xT2, r_xT2, sqb2, r_sqb2, rn2, r_rn2)
    yv, r_yv = A.tile("yv", [128, KT, 128], F32)
    obuf = [A.tile("obuf%d" % i, [128, D], F32) for i in range(2)]
    for ti in range(8):
        c0 = ti * 128
        for dt_ in range(KT):
            P.add("dve", lambda e, dt_=dt_, c0=c0: e.scalar_tensor_tensor(
                out=yv[:, dt_, :], in0=xT2[:, dt_, c0:c0 + 128], scalar=nft[:, dt_:dt_ + 1], in1=rn2[:, c0:c0 + 128],
                op0=ALU.mult, op1=ALU.mult), reads=[r_xT2, r_nf, r_rn2], writes=[r_yv])
        ob, r_ob = obuf[ti % 2]
        for dq in range(4):
            pb_, pr_ = rot8.next()
            for i_ in range(4):
                P.add("pe", lambda e, pb_=pb_, i_=i_, dq=dq: e.transpose(pb_[:, i_ * 128:(i_ + 1) * 128], yv[:, dq * 4 + i_, :], ident_f[:]),
                      reads=[r_yv, r_identf], writes=[pr_])
            eng = evac_eng()
            if eng == "act":
                P.add("act", lambda e, pb_=pb_, dq=dq, ob=ob: e.copy(out=ob[:, dq * 512:(dq + 1) * 512], in_=pb_[:, 0:512]),
                      reads=[pr_], writes=[r_ob])
            else:
                P.add("dve", lambda e, pb_=pb_, dq=dq, ob=ob: e.tensor_copy(out=ob[:, dq * 512:(dq + 1) * 512], in_=pb_[:, 0:512]),
                      reads=[pr_], writes=[r_ob])
        P.final.append(P.dma("sp", out[c0:c0 + 128, :], ob[:], reads=[r_ob]))
    P.emit()
    return nc, P, dbg_out


def make_in_maps(inputs):
    f = np.float32
    g = lambda k: np.ascontiguousarray(np.asarray(inputs[k], dtype=f))
    shared = {
        "w_mod": g("w_mod")[0], "b_mod": g("b_mod")[0], "norm1": g("norm1")[0], "norm2": g("norm2")[0],
        "w_in": g("w_in")[0], "s5_d": g("s5_d")[0], "w_glu": g("w_glu")[0], "q_norm": g("q_norm")[0],
        "kv_norm": g("kv_norm")[0], "w_uq": g("w_uq")[0], "w_ukv": g("w_ukv")[0], "w_mla_o": g("w_mla_o")[0],
        "w_out": g("w_out")[0], "w_ffn_in": g("w_ffn_in")[0], "w_ffn_out": g("w_ffn_out")[0], "norm_f": g("norm_f"),
    }
    s5n = ["s5_a_re", "s5_a_im", "s5_log_dt", "s5_b_re", "s5_b_im", "s5_c_re", "s5_c_im"]
    s5 = {k: g(k)[0] for k in s5n}
    s5sw = {k: np.ascontiguousarray(v[::-1]) for k, v in s5.items()}
    x = g("x")
    ctx = g("ctx")
    c = g("c")
    cc = g("c_ctx")
    maps = []
    for core in range(8):
        b, hf = core // 2, core % 2
        if hf == 0:
            seq = np.concatenate([ctx[b], x[b]], axis=0)
            cfgv = np.array([[0.0, 0.0, 1.0, 0.0]], dtype=f)
            sp = s5
        else:
            seq = np.concatenate([ctx[b][::-1], x[b][::-1]], axis=0)
            cfgv = np.array([[31.0, 63.0, -1.0, 0.0]], dtype=f)
            sp = s5sw
        m = dict(shared)
        m.update(sp)
        m["xs"] = np.ascontiguousarray(seq)
        m["cvec"] = np.ascontiguousarray(np.stack([c[b], cc], axis=0))
        m["cfg"] = cfgv
        maps.append(m)
    return maps


def assemble(results):
    outp = np.zeros((4, L, D), dtype=np.float32)
    for core in range(8):
        b, hf = core // 2, core % 2
        o = np.asarray(results[core]["out"])
        if hf == 0:
            outp[b, 0:OWN] = o
        else:
            outp[b, OWN:L] = o[::-1]
    return outp


def kernel(**inputs):
    nc, P, _ = build()
    maps = make_in_maps(inputs)
    res = run_bass_kernel_spmd(nc, maps, core_ids=list(range(8)))
    return assemble(res.results)
```

```python
import math
import numpy as np
import concourse.bass as bass
import concourse.mybir as mybir
from concourse.bass_utils import run_bass_kernel_spmd

F32 = mybir.dt.float32
BF16 = mybir.dt.bfloat16
I32 = mybir.dt.int32
AF = mybir.ActivationFunctionType
ALU = mybir.AluOpType

ENGS = ("pe", "act", "dve", "pool", "sp")
D = 2048
KT = 16
L = 2048
LC = 256
NTOK = 2304
OWN = 1024
EPS = 1e-6
ATTN_SCALE = 192.0 ** -0.5
DFF = 5632
SB_BASE = 16512
SB_END = 229376


class Res:
    __slots__ = ("name", "w", "rs", "lo", "hi")

    def __init__(self, name, lo=None, hi=None):
        self.name = name
        self.w = None
        self.rs = []
        self.lo = lo
        self.hi = hi


class Op:
    __slots__ = ("eng", "fn", "deps", "dma", "needed", "sem", "val", "prev", "gi")

    def __init__(self, eng, fn, dma):
        self.eng = eng
        self.fn = fn
        self.deps = []
        self.dma = dma
        self.needed = False
        self.sem = None
        self.val = 0
        self.prev = None
        self.gi = 0


class Prog:
    def __init__(self, nc, n_dma_sems=48):
        self.nc = nc
        self.ops = {e: [] for e in ENGS}
        self.n_dma_sems = n_dma_sems
        self.count = 0
        self.final = []

    def add(self, eng, fn, reads=(), writes=(), dma=False):
        op = Op(eng, fn, dma)
        op.gi = self.count
        self.count += 1
        deps = []
        for r in reads:
            if r.w is not None:
                deps.append(r.w)
        for w in writes:
            if w.w is not None:
                deps.append(w.w)
            lastr = {}
            for o in w.rs:
                if o.dma:
                    deps.append(o)
                elif o.eng not in lastr or lastr[o.eng].gi < o.gi:
                    lastr[o.eng] = o
            deps.extend(lastr.values())
        seen = set()
        for d in deps:
            if id(d) in seen or d is op:
                continue
            seen.add(id(d))
            if d.eng == eng and not d.dma:
                if eng == "pe" or eng == "sp":
                    continue
                if not any(r.w is d for r in reads):
                    continue
            op.deps.append(d)
        for r in reads:
            r.rs.append(op)
        for w in writes:
            w.w = op
            w.rs = []
        self.ops[eng].append(op)
        return op

    def dma(self, eng, out, in_, reads=(), writes=(), **kw):
        return self.add(eng, lambda e: e.dma_start(out=out, in_=in_, **kw), reads, writes, dma=True)

    def emit(self):
        nc = self.nc
        for e in ENGS:
            for op in self.ops[e]:
                for d in op.deps:
                    d.needed = True
        for op in self.final:
            op.needed = True
        esem = {e: nc.alloc_semaphore(name="eng_" + e) for e in ENGS}
        dsems = [nc.alloc_semaphore(name="dma_%d" % i) for i in range(self.n_dma_sems)]
        dval = [0] * self.n_dma_sems
        dlast = [None] * self.n_dma_sems
        allops = sorted([op for e in ENGS for op in self.ops[e]], key=lambda o: o.gi)
        cnt = {e: 0 for e in ENGS}
        rr = 0
        for op in allops:
            if op.dma:
                j = rr % self.n_dma_sems
                rr += 1
                dval[j] += 16
                op.sem = dsems[j]
                op.val = dval[j]
                op.prev = dlast[j]
                dlast[j] = op
            elif op.needed:
                cnt[op.eng] += 1
                op.sem = esem[op.eng]
                op.val = cnt[op.eng]
        engobj = {"pe": "tensor", "act": "scalar", "dve": "vector", "pool": "gpsimd", "sp": "sync"}
        nwaits = {e: 0 for e in ENGS}
        with nc.Block() as block:
            for e in ENGS:
                ops = self.ops[e]
                final = self.final if e == "sp" else []

                def body(eng, ops=ops, e=e, final=final):
                    seen = {}

                    def wait(sem, val):
                        k = id(sem)
                        if seen.get(k, 0) >= val:
                            return
                        seen[k] = val
                        eng.wait_ge(sem, val)
                        nwaits[e] += 1

                    for op in ops:
                        need = {}
                        for d in op.deps:
                            k = id(d.sem)
                            if k not in need or need[k][1] < d.val:
                                need[k] = (d.sem, d.val)
                        for sem_, val_ in need.values():
                            wait(sem_, val_)
                        if op.dma and op.prev is not None:
                            wait(op.prev.sem, op.prev.val)
                        ins = op.fn(eng)
                        if op.dma:
                            ins.then_inc(op.sem, 16)
                        elif op.needed:
                            ins.then_inc(op.sem, 1)
                    for op in final:
                        wait(op.sem, op.val)

                getattr(block, engobj[e])(body)
        self.stats = {e: (len(self.ops[e]), nwaits[e]) for e in ENGS}


class SBAlloc:
    def __init__(self, nc):
        self.nc = nc
        self.off = SB_BASE
        self.peak = SB_BASE
        self.n = 0
        self.live = []
        self.dead = []

    def tile(self, name, shape, dt):
        esz = 2 if dt == BF16 else 4
        nbytes = int(np.prod(shape[1:])) * esz
        nbytes = (nbytes + 63) // 64 * 64
        lo = self.off
        hi = lo + nbytes
        assert hi <= SB_END, "SBUF overflow at %s: %d" % (name, hi)
        self.n += 1
        t = self.nc.alloc_sbuf_tensor_at("%s_%d" % (name, self.n), list(shape), dt, offset=lo)
        self.off = hi
        self.peak = max(self.peak, hi)
        ent = (lo, hi, [])
        self.live.append(ent)
        t_res = self.res(name, ent)
        return t, t_res

    def res(self, name, ent):
        r = Res(name, ent[0], ent[1])
        for dres in self.dead:
            if dres.lo < r.hi and r.lo < dres.hi:
                if dres.w is not None:
                    r.rs.append(dres.w)
                r.rs.extend(dres.rs)
        ent[2].append(r)
        return r

    def sub(self, name, parent):
        for ent in self.live:
            if parent in ent[2]:
                return self.res(name, ent)
        raise KeyError(name)

    def mark(self):
        return (self.off, len(self.live))

    def release(self, m):
        off, n = m
        for ent in self.live[n:]:
            self.dead.extend(ent[2])
        del self.live[n:]
        self.off = off


def build(debug=(), stop_after=None):
    nc = bass.Bass("TRN2", target_bir_lowering=False)
    P = Prog(nc)
    A = SBAlloc(nc)
    dbg_out = {}

    def din(name, shape):
        return nc.dram_tensor(name, list(shape), F32, kind="ExternalInput").ap()

    xs = din("xs", [NTOK, D])
    cvec = din("cvec", [2, D])
    cfg = din("cfg", [1, 4])
    w_mod = din("w_mod", [D, 6 * D])
    b_mod = din("b_mod", [6 * D])
    norm1 = din("norm1", [D])
    norm2 = din("norm2", [D])
    w_in = din("w_in", [D, 5952])
    a_re = din("s5_a_re", [2, 64, 64])
    a_im = din("s5_a_im", [2, 64, 64])
    log_dt = din("s5_log_dt", [2, 64])
    b_re = din("s5_b_re", [2, 64, 64, 16])
    b_im = din("s5_b_im", [2, 64, 64, 16])
    c_re = din("s5_c_re", [2, 64, 16, 64])
    c_im = din("s5_c_im", [2, 64, 16, 64])
    s5_d = din("s5_d", [64, 16])
    w_glu = din("w_glu", [1024, 4096])
    q_norm = din("q_norm", [512])
    kv_norm = din("kv_norm", [256])
    w_uq = din("w_uq", [512, 1536])
    w_ukv = din("w_ukv", [256, 2048])
    w_mla_o = din("w_mla_o", [1024, 2048])
    w_out = din("w_out", [D, D])
    w_ffn_in = din("w_ffn_in", [D, 2 * DFF])
    w_ffn_out = din("w_ffn_out", [DFF, D])
    norm_f = din("norm_f", [D])
    out = nc.dram_tensor("out", [OWN, D], F32, kind="ExternalOutput").ap()
    xts_d = nc.dram_tensor("xts_scr", [128, KT, OWN], BF16, kind="Internal").ap()
    x1s_d = nc.dram_tensor("x1s_scr", [128, KT, OWN], F32, kind="Internal").ap()
    r_xts = Res("xts_d")
    r_x1s = Res("x1s_d")

    def dump(name, tile_ap, res, shape, dt=F32):
        if name not in debug:
            return
        o = nc.dram_tensor("dbg_" + name, list(shape), dt, kind="ExternalOutput").ap()
        dbg_out[name] = o
        P.final.append(P.dma("sp", o, tile_ap, reads=res))

    psb = []
    psr = []
    for i in range(8):
        psb.append(nc.alloc_psum_tensor("psbank%d" % i, [128, 512], F32))
        psr.append(Res("psum%d" % i))

    class Rot:
        def __init__(self, banks):
            self.banks = banks
            self.i = 0

        def next(self):
            b = self.banks[self.i % len(self.banks)]
            self.i += 1
            return psb[b], psr[b]

    alt = {"i": 0}

    def evac_eng():
        alt["i"] += 1
        return "act" if alt["i"] % 2 else "dve"

    ident_f, r_identf = A.tile("identf", [128, 128], F32)
    ident_b, r_identb = A.tile("identb", [128, 128], BF16)
    ones_b, r_onesb = A.tile("onesb", [128, 128], BF16)
    iot, r_iot = A.tile("iot", [128, 128], F32)
    P.add("pool", lambda e: e.iota(iot[:], pattern=[[1, 128]], base=0, channel_multiplier=-1,
                                   allow_small_or_imprecise_dtypes=True), writes=[r_iot])
    P.add("dve", lambda e: e.tensor_scalar(out=ident_f[:], in0=iot[:], scalar1=0.0, scalar2=None,
                                           op0=ALU.is_equal), reads=[r_iot], writes=[r_identf])
    P.add("dve", lambda e: e.tensor_copy(out=ident_b[:], in_=ident_f[:]), reads=[r_identf], writes=[r_identb])
    P.add("pool", lambda e: e.memset(ones_b[:], 1.0), writes=[r_onesb])

    modt, r_modt = A.tile("modt", [128, 96, 2], F32)
    s1, r_s1 = A.tile("s1", [128, 16, 2], F32)
    n1t, r_n1 = A.tile("n1t", [128, 16], F32)
    n2t, r_n2 = A.tile("n2t", [128, 16], F32)
    nft, r_nf = A.tile("nft", [128, 16], F32)
    bmt, r_bm = A.tile("bmt", [128, 96], F32)
    cs, r_cs = A.tile("cs", [128, 16, 2], F32)
    csb, r_csb = A.tile("csb", [128, 16, 2], BF16)
    s2, r_s2 = A.tile("s2", [128, 16], F32)
    for v_ in range(2):
        P.dma("sp", cs[:, :, v_], cvec[v_].rearrange("(kt p) -> p kt", p=128), writes=[r_cs], allow_slow_non_contiguous=True)
    P.dma("sp", bmt[:], b_mod.rearrange("(nt p) -> p nt", p=128), writes=[r_bm], allow_slow_non_contiguous=True)
    P.dma("sp", n1t[:], norm1.rearrange("(dt p) -> p dt", p=128), writes=[r_n1], allow_slow_non_contiguous=True)
    P.dma("sp", n2t[:], norm2.rearrange("(dt p) -> p dt", p=128), writes=[r_n2], allow_slow_non_contiguous=True)
    P.dma("sp", nft[:], norm_f.rearrange("(dt p) -> p dt", p=128), writes=[r_nf], allow_slow_non_contiguous=True)
    P.add("act", lambda e: e.activation(out=csb[:], in_=cs[:], func=AF.Silu), reads=[r_cs], writes=[r_csb])

    wmb = []

    def mod_slabs(j0, j1, bank, cols=1024, bufs=None):
        bufs = bufs if bufs is not None else wmb
        npt = cols // 128
        nsl = (j1 - j0) * 1024 // cols
        for si in range(nsl):
            wt, wr = bufs[si % 2]
            cbase = j0 * 1024 + si * cols
            P.dma("pool", wt[:, :, 0:cols], w_mod[:, cbase:cbase + cols].rearrange("(kt p) n -> p kt n", p=128),
                  writes=[wr])
            for ntl in range(npt):
                col = (si * npt + ntl) * 2
                for kt in range(KT):
                    P.add("pe", lambda e, kt=kt, ntl=ntl, col=col, wt=wt: e.matmul(
                        psb[bank][:, col:col + 2], wt[:, kt, ntl * 128:(ntl + 1) * 128], csb[:, kt, :],
                        start=(kt == 0), stop=(kt == KT - 1)),
                        reads=[wr, r_csb], writes=[psr[bank]])
        n0 = j0 * 8
        n1_ = j1 * 8

        def finish():
            P.add("dve", lambda e: e.tensor_tensor(
                out=modt[:, n0:n1_, :], in0=psb[bank][:, 0:(n1_ - n0) * 2].rearrange("p (n v) -> p n v", v=2),
                in1=bmt[:, n0:n1_].unsqueeze(2).to_broadcast([128, n1_ - n0, 2]), op=ALU.add),
                reads=[psr[bank], r_bm], writes=[r_modt])
        return finish

    MODJOBS = []
    m_ug = A.mark()
    ug, r_ug = A.tile("ug", [128, 64, 288], BF16)
    m_mla = A.mark()
    cqn, r_cqn = A.tile("cqn", [128, 4, OWN], BF16)
    ckvn, r_ckvn = A.tile("ckvn", [128, 2, NTOK], BF16)
    krt, r_krt = A.tile("krt", [64, NTOK], BF16)

    cosT, r_cos = A.tile("cosT", [64, L], F32)
    sinT, r_sin = A.tile("sinT", [64, L], F32)
    m_rt = A.mark()
    pos, r_pos = A.tile("pos", [64, L], F32)
    tmpa, r_tmpa = A.tile("tmpa", [64, L], F32)
    tmpb, r_tmpb = A.tile("tmpb", [64, L], F32)
    tmpi, r_tmpi = A.tile("tmpi", [64, L], I32)
    pidx, r_pidx = A.tile("pidx", [64, 4], F32)
    cfgt, r_cfg = A.tile("cfgt", [64, 4], F32)
    P.dma("sp", cfgt[:], cfg.partition_broadcast(64), writes=[r_cfg])
    P.add("pool", lambda e: e.iota(pos[0:32, :], pattern=[[1, 32], [0, 64]], base=0, channel_multiplier=0,
                                   allow_small_or_imprecise_dtypes=True), writes=[r_pos])
    P.add("pool", lambda e: e.iota(pos[32:64, :], pattern=[[0, 32], [1, 64]], base=0, channel_multiplier=0,
                                   allow_small_or_imprecise_dtypes=True), writes=[r_pos])
    P.add("dve", lambda e: e.tensor_scalar(out=pos[0:32, :], in0=pos[0:32, :], scalar1=cfgt[0:32, 2:3],
                                           scalar2=cfgt[0:32, 0:1], op0=ALU.mult, op1=ALU.add),
          reads=[r_pos, r_cfg], writes=[r_pos])
    P.add("dve", lambda e: e.tensor_scalar(out=pos[32:64, :], in0=pos[32:64, :], scalar1=cfgt[32:64, 2:3],
                                           scalar2=cfgt[32:64, 1:2], op0=ALU.mult, op1=ALU.add),
          reads=[r_pos, r_cfg], writes=[r_pos])
    P.add("pool", lambda e: e.iota(pidx[:, 0:1], pattern=[[0, 1]], base=0, channel_multiplier=1,
                                   allow_small_or_imprecise_dtypes=True), writes=[r_pidx])
    pc, r_pc = A.tile("pc", [64, 4], F32)
    for i, thr in enumerate((16.0, 32.0, 48.0)):
        P.add("dve", lambda e, i=i, thr=thr: e.tensor_scalar(out=pc[:, i:i + 1], in0=pidx[:, 0:1], scalar1=thr, scalar2=None,
                                                             op0=ALU.is_ge), reads=[r_pidx], writes=[r_pc])
    P.add("dve", lambda e: e.tensor_tensor(out=pidx[:, 1:2], in0=pc[:, 0:1], in1=pc[:, 1:2], op=ALU.add),
          reads=[r_pc], writes=[r_pidx])
    P.add("dve", lambda e: e.tensor_tensor(out=pidx[:, 1:2], in0=pidx[:, 1:2], in1=pc[:, 2:3], op=ALU.add),
          reads=[r_pc, r_pidx], writes=[r_pidx])
    P.add("dve", lambda e: e.scalar_tensor_tensor(out=pidx[:, 2:3], in0=pidx[:, 1:2], scalar=-16.0, in1=pidx[:, 0:1],
                                                  op0=ALU.mult, op1=ALU.add), reads=[r_pidx], writes=[r_pidx])
    P.add("act", lambda e: e.activation(out=pidx[:, 2:3], in_=pidx[:, 2:3], func=AF.Exp, scale=-math.log(10000.0) / 16.0),
          reads=[r_pidx], writes=[r_pidx])
    P.add("dve", lambda e: e.tensor_tensor(out=pidx[:, 3:4], in0=pc[:, 0:1], in1=pc[:, 1:2], op=ALU.subtract),
          reads=[r_pc], writes=[r_pidx])
    P.add("dve", lambda e: e.tensor_tensor(out=pidx[:, 3:4], in0=pidx[:, 3:4], in1=pc[:, 2:3], op=ALU.add),
          reads=[r_pc, r_pidx], writes=[r_pidx])
    P.add("dve", lambda e: e.tensor_scalar(out=pidx[:, 3:4], in0=pidx[:, 3:4], scalar1=2.0, scalar2=-1.0,
                                           op0=ALU.mult, op1=ALU.add), reads=[r_pidx], writes=[r_pidx])

    def range_reduce(eng, dst, dres, src, sres, shift, n, tmp, tres, tint, tires):
        inv2pi = 1.0 / (2.0 * math.pi)
        P.add(eng, lambda e: e.tensor_scalar(out=tmp, in0=src, scalar1=inv2pi, scalar2=shift * inv2pi,
                                             op0=ALU.mult, op1=ALU.add), reads=sres, writes=tres)
        P.add(eng, lambda e: e.tensor_copy(out=tint, in_=tmp), reads=tres, writes=tires)
        P.add(eng, lambda e: e.tensor_copy(out=dst, in_=tint), reads=tires, writes=dres)
        P.add(eng, lambda e: e.tensor_tensor(out=tmp, in0=tmp, in1=dst, op=ALU.subtract), reads=tres + dres, writes=tres)
        P.add(eng, lambda e: e.tensor_scalar(out=dst, in0=tmp, scalar1=0.5, scalar2=None, op0=ALU.is_gt),
              reads=tres, writes=dres)
        P.add(eng, lambda e: e.tensor_tensor(out=tmp, in0=tmp, in1=dst, op=ALU.subtract), reads=tres + dres, writes=tres)
        P.add(eng, lambda e: e.tensor_scalar(out=dst, in0=tmp, scalar1=-0.5, scalar2=None, op0=ALU.is_lt),
              reads=tres, writes=dres)
        P.add(eng, lambda e: e.tensor_tensor(out=tmp, in0=tmp, in1=dst, op=ALU.add), reads=tres + dres, writes=tres)
        P.add(eng, lambda e: e.tensor_scalar(out=dst, in0=tmp, scalar1=2.0 * math.pi, scalar2=None, op0=ALU.mult),
              reads=tres, writes=dres)

    P.add("dve", lambda e: e.tensor_scalar(out=pos[:], in0=pos[:], scalar1=pidx[:, 2:3], scalar2=None, op0=ALU.mult),
          reads=[r_pos, r_pidx], writes=[r_pos])
    range_reduce("dve", tmpb[:], [r_tmpb], pos[:], [r_pos], 0.0, L, tmpa[:], [r_tmpa], tmpi[:], [r_tmpi])
    P.add("act", lambda e: e.activation(out=sinT[:], in_=tmpb[:], func=AF.Sin), reads=[r_tmpb], writes=[r_sin])
    P.add("dve", lambda e: e.tensor_scalar(out=sinT[:], in0=sinT[:], scalar1=pidx[:, 3:4],
                                           scalar2=None, op0=ALU.mult), reads=[r_sin, r_pidx], writes=[r_sin])
    range_reduce("dve", tmpb[:], [r_tmpb], pos[:], [r_pos], math.pi / 2.0, L, tmpa[:], [r_tmpa], tmpi[:], [r_tmpi])
    P.add("act", lambda e: e.activation(out=cosT[:], in_=tmpb[:], func=AF.Sin), reads=[r_tmpb], writes=[r_cos])
    A.release(m_rt)
    m_w = A.mark()
    dump("cosT", cosT[:], [r_cos], [64, L])
    dump("sinT", sinT[:], [r_sin], [64, L])

    A.release(m_w)
    xt, r_xt_all = A.tile("xt", [128, KT, NTOK], BF16)
    r_xt = [A.sub("xt_g%d" % g, r_xt_all) for g in range(9)]
    m_p1 = A.mark()
    xbuf = [A.tile("xbuf%d" % i, [128, 2, D], F32) for i in range(2)]
    junk, r_junk = A.tile("junk", [128, D], BF16)
    sst, r_sst = A.tile("sst", [128, 8], F32)
    wm0 = [A.tile("wm0_%d" % i, [128, KT, 256], BF16) for i in range(2)]
    rot4 = Rot([1, 2, 3, 4])
    mod_state = {"si": 0}

    def mod_early(nsl):
        for _ in range(nsl):
            si = mod_state["si"]
            if si >= 16:
                return
            mod_state["si"] += 1
            wt, wr = wm0[si % 2]
            P.dma("pool", wt[:], w_mod[:, si * 256:(si + 1) * 256].rearrange("(kt p) n -> p kt n", p=128), writes=[wr])
            for ntl in range(2):
                col = (si * 2 + ntl) * 2
                for kt in range(KT):
                    P.add("pe", lambda e, kt=kt, ntl=ntl, col=col, wt=wt: e.matmul(
                        psb[0][:, col:col + 2], wt[:, kt, ntl * 128:(ntl + 1) * 128], csb[:, kt, :],
                        start=(kt == 0), stop=(kt == KT - 1)), reads=[wr, r_csb], writes=[psr[0]])

    mod_early(2)
    groups = [(0, 2, 1)] + [(LC + 256 * g, 2, 0) for g in range(8)]
    for gi, (c0, ntl, v) in enumerate(groups):
        xb, xr = xbuf[gi % 2]
        P.dma("sp", xb[:, 0:ntl, :], xs[c0:c0 + ntl * 128, :].rearrange("(j p) d -> p j d", p=128), writes=[xr])
        for j in range(ntl):
            P.add("act", lambda e, j=j, xb=xb: e.activation(out=junk[:], in_=xb[:, j, :], func=AF.Square,
                                                         accum_out=sst[:, j:j + 1]),
                  reads=[xr], writes=[r_junk, r_sst])
        P.add("act", lambda e, ntl=ntl: e.activation(out=sst[:, 4:4 + ntl], in_=sst[:, 0:ntl], func=AF.Sqrt,
                                                    scale=1.0 / D, bias=EPS), reads=[r_sst], writes=[r_sst])
        P.add("dve", lambda e, ntl=ntl: e.reciprocal(out=sst[:, 4:4 + ntl], in_=sst[:, 4:4 + ntl]),
              reads=[r_sst], writes=[r_sst])
        for j in range(ntl):
            P.add("dve", lambda e, j=j, xb=xb: e.tensor_scalar(out=xb[:, j, :], in0=xb[:, j, :], scalar1=sst[:, 4 + j:5 + j],
                                                            scalar2=None, op0=ALU.mult),
                  reads=[xr, r_sst], writes=[xr])
        for dt_ in range(KT):
            pb_, pr_ = rot4.next()
            for j in range(ntl):
                P.add("pe", lambda e, j=j, dt_=dt_, xb=xb, pb_=pb_: e.transpose(
                    pb_[:, j * 128:(j + 1) * 128], xb[:, j, dt_ * 128:(dt_ + 1) * 128], ident_f[:]),
                    reads=[xr, r_identf], writes=[pr_])
            n = ntl * 128
            if dt_ % 2 == 0:
                P.add("act", lambda e, dt_=dt_, pb_=pb_, n=n, c0=c0: e.copy(out=xt[:, dt_, c0:c0 + n], in_=pb_[:, 0:n]),
                      reads=[pr_], writes=[r_xt[gi]])
            else:
                P.add("dve", lambda e, dt_=dt_, pb_=pb_, n=n, c0=c0: e.tensor_copy(out=xt[:, dt_, c0:c0 + n], in_=pb_[:, 0:n]),
                      reads=[pr_], writes=[r_xt[gi]])
        mod_early(2)
    mod_early(16)
    P.add("dve", lambda e: e.tensor_tensor(
        out=modt[:, 0:32, :], in0=psb[0][:, 0:64].rearrange("p (n v) -> p n v", v=2),
        in1=bmt[:, 0:32].unsqueeze(2).to_broadcast([128, 32, 2]), op=ALU.add), reads=[psr[0], r_bm], writes=[r_modt])
    P.add("dve", lambda e: e.tensor_scalar(out=s1[:], in0=modt[:, 16:32, :], scalar1=1.0, scalar2=None, op0=ALU.add),
          reads=[r_modt], writes=[r_s1])
    P.add("dve", lambda e: e.tensor_tensor(out=s1[:], in0=s1[:], in1=n1t[:].unsqueeze(2).to_broadcast([128, 16, 2]),
                                           op=ALU.mult), reads=[r_s1, r_n1], writes=[r_s1])
    dump("modt01", modt[:, 0:32, :], [r_modt], [128, 32, 2])
    for dt_ in range(KT):
        for (c0_, n_, v_) in ((0, LC, 1), (LC, L, 0)):
            if (dt_ + v_) % 2 == 0:
                P.add("dve", lambda e, dt_=dt_, c0_=c0_, n_=n_, v_=v_: e.tensor_scalar(
                    out=xt[:, dt_, c0_:c0_ + n_], in0=xt[:, dt_, c0_:c0_ + n_], scalar1=s1[:, dt_, v_:v_ + 1],
                    scalar2=modt[:, dt_, v_:v_ + 1], op0=ALU.mult, op1=ALU.add),
                    reads=r_xt + [r_s1, r_modt], writes=r_xt)
            else:
                P.add("act", lambda e, dt_=dt_, c0_=c0_, n_=n_, v_=v_: e.activation(
                    out=xt[:, dt_, c0_:c0_ + n_], in_=xt[:, dt_, c0_:c0_ + n_], func=AF.Identity,
                    scale=s1[:, dt_, v_:v_ + 1], bias=modt[:, dt_, v_:v_ + 1]),
                    reads=r_xt + [r_s1, r_modt], writes=r_xt)
    A.release(m_p1)
    P.dma("sp", xts_d, xt[:, :, LC:LC + OWN], reads=r_xt[1:5], writes=[r_xts])
    dump("xt", xt[:], r_xt, [128, KT, NTOK], BF16)
    if stop_after == "prep":
        P.emit()
        return nc, P, dbg_out

    m_p2 = A.mark()
    wbuf = [A.tile("wbuf%d" % i, [128, KT, 512], BF16) for i in range(2)]
    wbi = {"i": 0}

    def wnext():
        t = wbuf[wbi["i"] % 2]
        wbi["i"] += 1
        return t

    def wsl(c0, c1):
        return w_in[:, c0:c1].rearrange("(kt p) n -> p kt n", p=128)

    def xt_res(c0, n):
        out_ = []
        for gi, (g0, ntl, v) in enumerate(groups):
            if g0 < c0 + n and c0 < g0 + ntl * 128:
                out_.append(r_xt[gi])
        return out_

    uc, r_uc = A.tile("uc", [128, 32, 128], BF16)
    rot = Rot([0, 1, 2, 3, 4, 5, 6, 7])
    ctiles = [(0, 32, 0), (LC, 128, 32), (LC + 1024, 128, 160)]
    for cb in range(2):
        wb, wr = wnext()
        P.dma("pool", wb[:], wsl(cb * 512, (cb + 1) * 512), writes=[wr])
        for (c0, M, q0) in ctiles:
            ntk = M * 8
            xres = xt_res(c0, ntk)
            for s in range(8):
                pb_, pr_ = rot.next()
                for kt in range(KT):
                    P.add("pe", lambda e, kt=kt, pb_=pb_, M=M, c0=c0, s=s, ntk=ntk, wb=wb: e.matmul(
                        pb_[0:M, 0:512], xt[:, kt, c0 + s:c0 + ntk:8], wb[:, kt, :],
                        start=(kt == 0), stop=(kt == KT - 1)), reads=xres + [wr], writes=[pr_])
                eng = evac_eng()
                if eng == "act":
                    P.add("act", lambda e, pb_=pb_, M=M, s=s: e.copy(
                        out=uc[0:M, :, s * 16:(s + 1) * 16], in_=pb_[0:M, 0:512].rearrange("m (g p) -> m g p", p=16)),
                          reads=[pr_], writes=[r_uc])
                else:
                    P.add("dve", lambda e, pb_=pb_, M=M, s=s: e.tensor_copy(
                        out=uc[0:M, :, s * 16:(s + 1) * 16], in_=pb_[0:M, 0:512].rearrange("m (g p) -> m g p", p=16)),
                          reads=[pr_], writes=[r_uc])
            for gb in range(4):
                pb_, pr_ = rot.next()
                pbb = pb_[:].bitcast(BF16)
                for gl in range(8):
                    gloc = gb * 8 + gl
                    P.add("pe", lambda e, pbb=pbb, gl=gl, M=M, gloc=gloc: e.transpose(
                        pbb[:, gl * 128:gl * 128 + M], uc[0:M, gloc, :], ident_b[0:M, 0:M]),
                        reads=[r_uc, r_identb], writes=[pr_])
                g0 = cb * 32 + gb * 8
                eng = evac_eng()
                src = pbb.rearrange("p (g c) -> p g c", g=8)[:, :, 0:M]
                if eng == "act":
                    P.add("act", lambda e, src=src, g0=g0, q0=q0, M=M: e.copy(out=ug[:, g0:g0 + 8, q0:q0 + M], in_=src),
                          reads=[pr_], writes=[r_ug])
                else:
                    P.add("dve", lambda e, src=src, g0=g0, q0=q0, M=M: e.tensor_copy(out=ug[:, g0:g0 + 8, q0:q0 + M], in_=src),
                          reads=[pr_], writes=[r_ug])
    dump("ug", ug[:], [r_ug], [128, 64, 288], BF16)

    gq, r_gq = A.tile("gq", [128, 4], F32)
    gkv, r_gkv = A.tile("gkv", [128, 2], F32)
    P.dma("sp", gq[:], q_norm.rearrange("(mt p) -> p mt", p=128), writes=[r_gq], allow_slow_non_contiguous=True)
    P.dma("sp", gkv[:], kv_norm.rearrange("(mt p) -> p mt", p=128), writes=[r_gkv], allow_slow_non_contiguous=True)
    sq, r_sq = A.tile("sq", [128, 4, 512], BF16)
    rsd, r_rsd = A.tile("rsd", [128, 512], F32)
    tr1, r_tr1 = A.tile("tr1", [64, 512], F32)
    tr2, r_tr2 = A.tile("tr2", [64, 512], F32)
    own_blocks = [(LC, 512), (LC + 512, 512)]
    all_blocks = [(0, 256), (LC, 512), (LC + 512, 512), (LC + 1024, 512), (LC + 1536, 512)]

    def norm_block(nmt, c0, n, wb, wr, wcol0, gam, r_gam, rank, scale, dst, r_dst, dcol0):
        xres = xt_res(c0, n)
        for mt in range(nmt):
            for kt in range(KT):
                P.add("pe", lambda e, mt=mt, kt=kt: e.matmul(
                    psb[mt][:, 0:n], wb[:, kt, wcol0 + mt * 128:wcol0 + (mt + 1) * 128], xt[:, kt, c0:c0 + n],
                    start=(kt == 0), stop=(kt == KT - 1)), reads=xres + [wr], writes=[psr[mt]])
            P.add("act", lambda e, mt=mt: e.activation(out=sq[:, mt, 0:n], in_=psb[mt][:, 0:n], func=AF.Square),
                  reads=[psr[mt]], writes=[r_sq])
        for mt in range(nmt):
            P.add("pe", lambda e, mt=mt: e.matmul(psb[4][:, 0:n], ones_b[:], sq[:, mt, 0:n],
                                                  start=(mt == 0), stop=(mt == nmt - 1)),
                  reads=[r_sq, r_onesb], writes=[psr[4]])
        P.add("act", lambda e: e.activation(out=rsd[:, 0:n], in_=psb[4][:, 0:n], func=AF.Sqrt,
                                            scale=1.0 / (rank * scale * scale), bias=EPS / (scale * scale)),
              reads=[psr[4]], writes=[r_rsd])
        P.add("dve", lambda e: e.reciprocal(out=rsd[:, 0:n], in_=rsd[:, 0:n]), reads=[r_rsd], writes=[r_rsd])
        for mt in range(nmt):
            P.add("dve", lambda e, mt=mt: e.scalar_tensor_tensor(
                out=dst[:, mt, dcol0:dcol0 + n], in0=psb[mt][:, 0:n], scalar=gam[:, mt:mt + 1], in1=rsd[:, 0:n],
                op0=ALU.mult, op1=ALU.mult), reads=[psr[mt], r_gam, r_rsd], writes=[r_dst])

    wb, wr = wnext()
    P.dma("pool", wb[:], wsl(1024, 1536), writes=[wr])
    for (c0, n) in own_blocks:
        norm_block(4, c0, n, wb, wr, 0, gq, r_gq, 512.0, ATTN_SCALE, cqn, r_cqn, c0 - LC)
    wb, wr = wnext()
    P.dma("pool", wb[:, :, 0:320], wsl(1536, 1856), writes=[wr])
    for hh in range(2):
        for a_ in range(2):
            c_src = 1792 + a_ * 32 + (1 - hh) * 16
            c_dst = 320 + a_ * 32 + hh * 16
            P.dma("pool", wb[:, :, c_dst:c_dst + 16], wsl(c_src, c_src + 16), writes=[wr])
    for (c0, n) in all_blocks:
        norm_block(2, c0, n, wb, wr, 0, gkv, r_gkv, 256.0, 1.0, ckvn, r_ckvn, c0)
        xres = xt_res(c0, n)
        for i_, wc in enumerate((256, 320)):
            if c0 < LC and i_ == 1:
                continue
            for kt in range(KT):
                P.add("pe", lambda e, kt=kt, i_=i_, wc=wc, c0=c0, n=n, wb=wb: e.matmul(
                    psb[5 + i_][0:64, 0:n], wb[:, kt, wc:wc + 64], xt[:, kt, c0:c0 + n],
                    start=(kt == 0), stop=(kt == KT - 1)), reads=xres + [wr], writes=[psr[5 + i_]])
        if c0 < LC:
            P.add("act", lambda e, c0=c0, n=n: e.copy(out=krt[:, c0:c0 + n], in_=psb[5][0:64, 0:n]), reads=[psr[5]], writes=[r_krt])
        else:
            l0 = c0 - LC
            P.add("dve", lambda e, l0=l0, n=n: e.tensor_tensor(out=tr1[:, 0:n], in0=psb[5][0:64, 0:n], in1=cosT[:, l0:l0 + n],
                                                             op=ALU.mult), reads=[psr[5], r_cos], writes=[r_tr1])
            P.add("dve", lambda e, l0=l0, n=n: e.tensor_tensor(out=tr2[:, 0:n], in0=psb[6][0:64, 0:n], in1=sinT[:, l0:l0 + n],
                                                             op=ALU.mult), reads=[psr[6], r_sin], writes=[r_tr2])
            P.add("pool", lambda e, c0=c0, n=n: e.tensor_tensor(out=krt[:, c0:c0 + n], in0=tr1[:, 0:n], in1=tr2[:, 0:n], op=ALU.add),
                  reads=[r_tr1, r_tr2], writes=[r_krt])
    dump("cqn", cqn[:], [r_cqn], [128, 4, OWN], BF16)
    dump("ckvn", ckvn[:], [r_ckvn], [128, 2, NTOK], BF16)
    dump("krt", krt[:], [r_krt], [64, NTOK], BF16)
    A.release(m_p2)
    if stop_after == "inproj":
        P.emit()
        return nc, P, dbg_out

    A.release(m_w)
    m_p3 = A.mark()
    ot_tmp, r_ott = A.tile("ot_tmp", [128, 8, OWN], BF16)
    wuq, r_wuq = A.tile("wuq", [128, 4, 1536], BF16)
    wuqs, r_wuqs = A.tile("wuqs", [128, 4, 8, 64], BF16)
    wukv, r_wukv = A.tile("wukv", [128, 2, 2048], BF16)
    P.dma("pool", wuq[:], w_uq.rearrange("(kt p) n -> p kt n", p=128), writes=[r_wuq])
    P.dma("pool", wukv[:], w_ukv.rearrange("(kt p) n -> p kt n", p=128), writes=[r_wukv])
    for kt in range(4):
        w3 = w_uq[kt * 128:(kt + 1) * 128, :].rearrange("p (h x) -> p h x", x=192)
        for hh in range(2):
            for a_ in range(2):
                cs_ = 128 + a_ * 32 + (1 - hh) * 16
                cd_ = a_ * 32 + hh * 16
                P.dma("pool", wuqs[:, kt, :, cd_:cd_ + 16], w3[:, :, cs_:cs_ + 16], writes=[r_wuqs])
    kn, r_kn = A.tile("kn", [128, 4, NTOK], BF16)
    vt, r_vt = A.tile("vt", [128, 18, 4, 128], BF16)
    qn, r_qn = A.tile("qn", [128, 4, OWN], BF16)
    qr, r_qr = A.tile("qr", [64, 4, OWN], BF16)
    pts = [A.tile("pt%d" % i, [128, 512], BF16) for i in range(3)]
    rcp, r_rcp = A.tile("rcp", [128, 512], F32)
    t3a, r_t3a = A.tile("t3a", [64, 512], F32)
    t3b, r_t3b = A.tile("t3b", [64, 512], F32)
    rotg = Rot([0, 1, 2, 3])
    for hp in range(2):
        for hl in range(4):
            h = hp * 4 + hl
            for tb in range(2):
                t0 = tb * 512
                pb_, pr_ = rotg.next()
                for kt in range(4):
                    P.add("pe", lambda e, kt=kt, pb_=pb_, h=h, t0=t0: e.matmul(
                        pb_[:, 0:512], wuq[:, kt, h * 192:h * 192 + 128], cqn[:, kt, t0:t0 + 512],
                        start=(kt == 0), stop=(kt == 3)), reads=[r_wuq, r_cqn], writes=[pr_])
                P.add("act", lambda e, pb_=pb_, hl=hl, t0=t0: e.copy(out=qn[:, hl, t0:t0 + 512], in_=pb_[:, 0:512]),
                      reads=[pr_], writes=[r_qn])
                pa_, pra_ = rotg.next()
                pw_, prw_ = rotg.next()
                for kt in range(4):
                    P.add("pe", lambda e, kt=kt, pa_=pa_, h=h, t0=t0: e.matmul(
                        pa_[0:64, 0:512], wuq[:, kt, h * 192 + 128:h * 192 + 192], cqn[:, kt, t0:t0 + 512],
                        start=(kt == 0), stop=(kt == 3)), reads=[r_wuq, r_cqn], writes=[pra_])
                for kt in range(4):
                    P.add("pe", lambda e, kt=kt, pw_=pw_, h=h, t0=t0: e.matmul(
                        pw_[0:64, 0:512], wuqs[:, kt, h, :], cqn[:, kt, t0:t0 + 512],
                        start=(kt == 0), stop=(kt == 3)), reads=[r_wuqs, r_cqn], writes=[prw_])
                P.add("dve", lambda e, pa_=pa_, t0=t0: e.tensor_tensor(out=t3a[:], in0=pa_[0:64, 0:512], in1=cosT[:, t0:t0 + 512],
                                                                    op=ALU.mult), reads=[pra_, r_cos], writes=[r_t3a])
                P.add("dve", lambda e, pw_=pw_, t0=t0: e.tensor_tensor(out=t3b[:], in0=pw_[0:64, 0:512], in1=sinT[:, t0:t0 + 512],
                                                                    op=ALU.mult), reads=[prw_, r_sin], writes=[r_t3b])
                P.add("pool", lambda e, hl=hl, t0=t0: e.tensor_tensor(out=qr[:, hl, t0:t0 + 512], in0=t3a[:], in1=t3b[:], op=ALU.add),
                      reads=[r_t3a, r_t3b], writes=[r_qr])
        for hl in range(4):
            h = hp * 4 + hl
            for (c0, n) in all_blocks:
                pb_, pr_ = rotg.next()
                for kt in range(2):
                    P.add("pe", lambda e, kt=kt, pb_=pb_, h=h, c0=c0, n=n: e.matmul(
                        pb_[:, 0:n], wukv[:, kt, h * 256:h * 256 + 128], ckvn[:, kt, c0:c0 + n],
                        start=(kt == 0), stop=(kt == 1)), reads=[r_wukv, r_ckvn], writes=[pr_])
                eng = evac_eng()
                if eng == "act":
                    P.add("act", lambda e, pb_=pb_, hl=hl, c0=c0, n=n: e.copy(out=kn[:, hl, c0:c0 + n], in_=pb_[:, 0:n]),
                          reads=[pr_], writes=[r_kn])
                else:
                    P.add("dve", lambda e, pb_=pb_, hl=hl, c0=c0, n=n: e.tensor_copy(out=kn[:, hl, c0:c0 + n], in_=pb_[:, 0:n]),
                          reads=[pr_], writes=[r_kn])
        wv4 = [wukv[:, kt, :].rearrange("p (h x) -> p h x", x=256)[:, hp * 4:(hp + 1) * 4, 128:256] for kt in range(2)]
        for ti in range(18):
            pb_, pr_ = rotg.next()
            for kt in range(2):
                P.add("pe", lambda e, kt=kt, pb_=pb_, ti=ti, wv=wv4[kt]: e.matmul(
                    pb_[:, 0:512], ckvn[:, kt, ti * 128:(ti + 1) * 128], wv,
                    start=(kt == 0), stop=(kt == 1)), reads=[r_wukv, r_ckvn], writes=[pr_])
            eng = evac_eng()
            if eng == "act":
                P.add("act", lambda e, pb_=pb_, ti=ti: e.copy(out=vt[:, ti, :, :], in_=pb_[:, 0:512].rearrange("p (h x) -> p h x", x=128)),
                      reads=[pr_], writes=[r_vt])
            else:
                P.add("dve", lambda e, pb_=pb_, ti=ti: e.tensor_copy(out=vt[:, ti, :, :], in_=pb_[:, 0:512].rearrange("p (h x) -> p h x", x=128)),
                      reads=[pr_], writes=[r_vt])
        pti = 0
        for hl in range(4):
            h = hp * 4 + hl
            for qb in range(2):
                t0 = qb * 512
                acc = (hl * 2 + qb) % 2
                pO, rO = psb[4 + acc], psr[4 + acc]
                pR, rR = psb[6 + acc], psr[6 + acc]

                def s_mm(ki, h=h, hl=hl, t0=t0):
                    pS, rS = psb[ki % 2], psr[ki % 2]
                    P.add("pe", lambda e: e.matmul(pS[:, 0:512], kn[:, hl, ki * 128:(ki + 1) * 128], qn[:, hl, t0:t0 + 512],
                                                   start=True, stop=False), reads=[r_kn, r_qn], writes=[rS])
                    P.add("pe", lambda e: e.matmul(pS[:, 0:512], krt[:, ki * 128:(ki + 1) * 128], qr[:, hl, t0:t0 + 512],
                                                   start=False, stop=True), reads=[r_krt, r_qr], writes=[rS])

                s_mm(0)
                for ki in range(18):
                    if ki + 1 < 18:
                        s_mm(ki + 1)
                    pS, rS = psb[ki % 2], psr[ki % 2]
                    ptt, ptr = pts[pti % 3]
                    pti += 1
                    P.add("act", lambda e, pS=pS, ptt=ptt: e.activation(out=ptt[:], in_=pS[:, 0:512], func=AF.Exp),
                          reads=[rS], writes=[ptr])
                    P.add("pe", lambda e, ki=ki, ptt=ptt, hl=hl, pO=pO: e.matmul(
                        pO[:, 0:512], vt[:, ki, hl, :], ptt[:], start=(ki == 0), stop=(ki == 17)),
                        reads=[r_vt, ptr], writes=[rO])
                    P.add("pe", lambda e, ki=ki, ptt=ptt, pR=pR: e.matmul(
                        pR[:, 0:512], ones_b[:], ptt[:], start=(ki == 0), stop=(ki == 17)),
                        reads=[r_onesb, ptr], writes=[rR])
                P.add("dve", lambda e, pR=pR: e.reciprocal(out=rcp[:], in_=pR[:, 0:512]), reads=[rR], writes=[r_rcp])
                P.add("dve", lambda e, pO=pO, h=h, t0=t0: e.tensor_tensor(out=ot_tmp[:, h, t0:t0 + 512], in0=pO[:, 0:512], in1=rcp[:],
                                                                       op=ALU.mult), reads=[rO, r_rcp], writes=[r_ott])
    A.release(m_mla)
    zt, r_zt = A.tile("zt", [128, 8, OWN], BF16)
    dump("ot", ot_tmp[:], [r_ott], [128, 8, OWN], BF16)
    if stop_after == "mla":
        P.emit()
        return nc, P, dbg_out

    ots_d = nc.dram_tensor("ots_scr", [128, 8, OWN], BF16, kind="Internal").ap()
    r_ots = Res("ots_d")
    P.dma("sp", ots_d, ot_tmp[:], reads=[r_ott], writes=[r_ots])
    sp_, r_sp = A.tile("sprime", [128, 288, 2, 64], BF16)
    r_hs = A.sub("hstates", r_sp)
    F1 = [128, 64]
    are, r_are = A.tile("are", F1, F32)
    aim, r_aim = A.tile("aim", F1, F32)
    dtt, r_dtt = A.tile("dtt", F1, F32)
    lrdt, r_lrdt = A.tile("lrdt", F1, F32)
    ang, r_ang = A.tile("ang", F1, F32)
    pwr, r_pwr = A.tile("pwr", [128, 24, 64], F32)
    pwi, r_pwi = A.tile("pwi", [128, 24, 64], F32)
    bbr, r_bbr = A.tile("bbr", [128, 64, 16], F32)
    bbi, r_bbi = A.tile("bbi", [128, 64, 16], F32)
    ctr, r_ctr = A.tile("ctr", [128, 64, 16], F32)
    cti, r_cti = A.tile("cti", [128, 64, 16], F32)
    dsk, r_dsk = A.tile("dsk", [128, 64], F32)
    mkf, r_mkf = A.tile("mkf", [128, 128], F32)
    mkb, r_mkb = A.tile("mkb", [128, 128], F32)
    arar, r_arar = A.tile("arar", [128, 2, 64], F32)
    aiai, r_aiai = A.tile("aiai", [128, 2, 64], F32)
    hc, r_hc = A.tile("hc", [128, 2, 64], F32)
    t1, r_t1 = A.tile("t1", [128, 2, 64], F32)
    t2, r_t2 = A.tile("t2", [128, 2, 64], F32)
    t3, r_t3 = A.tile("t3", [128, 2, 64], F32)
    pws = {}
    for nm in ("phi", "psim", "psir"):
        pws[nm] = (A.tile(nm + "_re", [128, 8, 64], F32), A.tile(nm + "_im", [128, 8, 64], F32))
    m_g0 = A.mark()
    for d_ in range(2):
        hs_ = slice(d_ * 64, (d_ + 1) * 64)
        P.dma("sp", are[hs_, :], a_re[d_].rearrange("g n -> n g"), writes=[r_are], allow_slow_non_contiguous=True)
        P.dma("sp", aim[hs_, :], a_im[d_].rearrange("g n -> n g"), writes=[r_aim], allow_slow_non_contiguous=True)
        P.dma("sp", dtt[hs_, :], log_dt[d_:d_ + 1, :].partition_broadcast(64), writes=[r_dtt])
    for s_ in range(8):
        P.dma("sp", dsk[s_ * 16:(s_ + 1) * 16, :], s5_d.rearrange("g p -> p g"), writes=[r_dsk], allow_slow_non_contiguous=True)
    cnr, r_cnr = A.tile("cnr", [128, 8, 2, 64], F32)
    cni, r_cni = A.tile("cni", [128, 8, 2, 64], F32)
    for d_ in range(2):
        P.dma("sp", cnr[:, :, d_, :], c_re[d_].rearrange("(gt gl) p n -> (gl p) gt n", gl=8), writes=[r_cnr])
        P.dma("sp", cni[:, :, d_, :], c_im[d_].rearrange("(gt gl) p n -> (gl p) gt n", gl=8), writes=[r_cni])
    for (cn_, r_cn_, ct_, r_ct_) in ((cnr, r_cnr, ctr, r_ctr), (cni, r_cni, cti, r_cti)):
        for half_ in range(2):
            pb_, pr_ = psb[half_], psr[half_]
            for q_ in range(4):
                gt_ = half_ * 4 + q_
                P.add("pe", lambda e, pb_=pb_, q_=q_, gt_=gt_, cn_=cn_: e.transpose(
                    pb_[:, q_ * 128:(q_ + 1) * 128], cn_[:, gt_, :, :].rearrange("q d n -> q (d n)"), ident_f[:]),
                    reads=[r_cn_, r_identf], writes=[pr_])
            P.add("dve", lambda e, pb_=pb_, half_=half_, ct_=ct_: e.tensor_copy(
                out=ct_[:, half_ * 32:(half_ + 1) * 32, :].rearrange("q g p -> q (g p)"), in_=pb_[:, 0:512]),
                reads=[pr_], writes=[r_ct_])

    A.release(m_g0)

    def exp_poly(dst, r_dst, z, r_z, tmp, r_tmp):
        cf = [1.0 / math.factorial(i) for i in range(11)]
        P.add("dve", lambda e: e.tensor_scalar(out=tmp, in0=z, scalar1=cf[10], scalar2=None, op0=ALU.mult),
              reads=[r_z], writes=[r_tmp])
        for i in range(9, 0, -1):
            P.add("dve", lambda e, i=i: e.scalar_tensor_tensor(out=tmp, in0=tmp, scalar=cf[i], in1=z, op0=ALU.add, op1=ALU.mult),
                  reads=[r_tmp, r_z], writes=[r_tmp])
        P.add("dve", lambda e: e.tensor_scalar(out=dst, in0=tmp, scalar1=1.0, scalar2=None, op0=ALU.add),
              reads=[r_tmp], writes=[r_dst])

    zt_, r_zt_ = A.tile("ztmp", F1, F32)
    pt_, r_pt_ = A.tile("ptmp", F1, F32)
    P.add("dve", lambda e: e.tensor_scalar(out=zt_[:], in0=dtt[:], scalar1=0.125, scalar2=None, op0=ALU.mult),
          reads=[r_dtt], writes=[r_zt_])
    exp_poly(dtt[:], r_dtt, zt_[:], r_zt_, pt_[:], r_pt_)
    for _ in range(3):
        P.add("dve", lambda e: e.tensor_tensor(out=dtt[:], in0=dtt[:], in1=dtt[:], op=ALU.mult), reads=[r_dtt], writes=[r_dtt])
    P.add("dve", lambda e: e.tensor_tensor(out=lrdt[:], in0=are[:], in1=dtt[:], op=ALU.mult), reads=[r_are, r_dtt], writes=[r_lrdt])
    P.add("dve", lambda e: e.tensor_tensor(out=ang[:], in0=aim[:], in1=dtt[:], op=ALU.mult), reads=[r_aim, r_dtt], writes=[r_ang])
    emag, r_emag = A.tile("emag", [128, 24, 64], F32)
    ka, r_ka = A.tile("ka", [128, 17, 64], F32)
    kb, r_kb = A.tile("kb", [128, 17, 64], F32)
    kc, r_kc = A.tile("kc", [128, 17, 64], F32)
    ki_, r_ki = A.tile("ki", [128, 17, 64], I32)
    P.add("pool", lambda e: e.iota(emag[:], pattern=[[1, 24], [0, 64]], base=-7, channel_multiplier=0,
                                   allow_small_or_imprecise_dtypes=True), writes=[r_emag])
    P.add("dve", lambda e: e.tensor_tensor(out=ka[:], in0=emag[:, 7:24, :], in1=ang[:].unsqueeze(1).to_broadcast([128, 17, 64]),
                                           op=ALU.mult), reads=[r_emag, r_ang], writes=[r_ka])
    P.add("dve", lambda e: e.tensor_tensor(out=emag[:], in0=emag[:], in1=lrdt[:].unsqueeze(1).to_broadcast([128, 24, 64]),
                                           op=ALU.mult), reads=[r_emag, r_lrdt], writes=[r_emag])
    P.add("dve", lambda e: e.tensor_copy(out=zt_[:], in_=emag[:, 15, :]), reads=[r_emag], writes=[r_zt_])
    P.add("act", lambda e: e.activation(out=emag[:], in_=emag[:], func=AF.Exp), reads=[r_emag], writes=[r_emag])
    exp_poly(emag[:, 15, :], r_emag, zt_[:], r_zt_, pt_[:], r_pt_)
    range_reduce("dve", kb[:], [r_kb], ka[:], [r_ka], 0.0, 0, kc[:], [r_kc], ki_[:], [r_ki])
    P.add("act", lambda e: e.activation(out=pwi[:, 7:24, :], in_=kb[:], func=AF.Sin), reads=[r_kb], writes=[r_pwi])
    range_reduce("dve", kb[:], [r_kb], ka[:], [r_ka], math.pi / 2.0, 0, kc[:], [r_kc], ki_[:], [r_ki])
    P.add("act", lambda e: e.activation(out=pwr[:, 7:24, :], in_=kb[:], func=AF.Sin), reads=[r_kb], writes=[r_pwr])
    P.add("dve", lambda e: e.tensor_tensor(out=pwr[:, 0:7, :], in0=emag[:, 0:7, :], in1=pwr[:, 14:7:-1, :], op=ALU.mult),
          reads=[r_emag, r_pwr], writes=[r_pwr])
    P.add("dve", lambda e: e.scalar_tensor_tensor(out=pwi[:, 0:7, :], in0=emag[:, 0:7, :], scalar=-1.0, in1=pwi[:, 14:7:-1, :],
                                                  op0=ALU.mult, op1=ALU.mult), reads=[r_emag, r_pwi], writes=[r_pwi])
    P.add("dve", lambda e: e.tensor_tensor(out=pwr[:, 7:24, :], in0=pwr[:, 7:24, :], in1=emag[:, 7:24, :], op=ALU.mult),
          reads=[r_emag, r_pwr], writes=[r_pwr])
    P.add("dve", lambda e: e.tensor_tensor(out=pwi[:, 7:24, :], in0=pwi[:, 7:24, :], in1=emag[:, 7:24, :], op=ALU.mult),
          reads=[r_emag, r_pwi], writes=[r_pwi])
    for c_ in range(2):
        P.add("dve", lambda e, c_=c_: e.tensor_copy(out=arar[:, c_, :], in_=pwr[:, 15, :]), reads=[r_pwr], writes=[r_arar])
    P.add("dve", lambda e: e.tensor_scalar(out=aiai[:, 0, :], in0=pwi[:, 15, :], scalar1=-1.0, scalar2=None, op0=ALU.mult),
          reads=[r_pwi], writes=[r_aiai])
    P.add("dve", lambda e: e.tensor_copy(out=aiai[:, 1, :], in_=pwi[:, 15, :]), reads=[r_pwi], writes=[r_aiai])
    A.release(m_g0)
    braw, r_braw = A.tile("braw", [128, 64, 16], F32)
    biraw, r_biraw = A.tile("biraw", [128, 64, 16], F32)
    for d_ in range(2):
        hs_ = slice(d_ * 64, (d_ + 1) * 64)
        P.dma("sp", braw[hs_, :, :], b_re[d_].rearrange("g n p -> n g p"), writes=[r_braw])
        P.dma("sp", biraw[hs_, :, :], b_im[d_].rearrange("g n p -> n g p"), writes=[r_biraw])
    zq_, r_zq_ = A.tile("ztmp2", F1, F32)
    den, r_den = A.tile("den", F1, F32)
    cor, r_cor = A.tile("cor", F1, F32)
    coi, r_coi = A.tile("coi", F1, F32)
    nr_, r_nr = A.tile("nr", F1, F32)
    P.add("dve", lambda e: e.tensor_tensor(out=den[:], in0=are[:], in1=are[:], op=ALU.mult), reads=[r_are], writes=[r_den])
    P.add("dve", lambda e: e.tensor_tensor(out=zq_[:], in0=aim[:], in1=aim[:], op=ALU.mult), reads=[r_aim], writes=[r_zq_])
    P.add("dve", lambda e: e.tensor_tensor(out=den[:], in0=den[:], in1=zq_[:], op=ALU.add), reads=[r_den, r_zq_], writes=[r_den])
    P.add("dve", lambda e: e.reciprocal(out=den[:], in_=den[:]), reads=[r_den], writes=[r_den])
    P.add("dve", lambda e: e.tensor_scalar(out=nr_[:], in0=pwr[:, 8, :], scalar1=-1.0, scalar2=None, op0=ALU.add),
          reads=[r_pwr], writes=[r_nr])
    P.add("dve", lambda e: e.tensor_tensor(out=cor[:], in0=nr_[:], in1=are[:], op=ALU.mult), reads=[r_nr, r_are], writes=[r_cor])
    P.add("dve", lambda e: e.tensor_tensor(out=zq_[:], in0=pwi[:, 8, :], in1=aim[:], op=ALU.mult), reads=[r_pwi, r_aim], writes=[r_zq_])
    P.add("dve", lambda e: e.tensor_tensor(out=cor[:], in0=cor[:], in1=zq_[:], op=ALU.add), reads=[r_cor, r_zq_], writes=[r_cor])
    P.add("dve", lambda e: e.tensor_tensor(out=cor[:], in0=cor[:], in1=den[:], op=ALU.mult), reads=[r_cor, r_den], writes=[r_cor])
    P.add("dve", lambda e: e.tensor_tensor(out=coi[:], in0=pwi[:, 8, :], in1=are[:], op=ALU.mult), reads=[r_pwi, r_are], writes=[r_coi])
    P.add("dve", lambda e: e.tensor_tensor(out=zq_[:], in0=nr_[:], in1=aim[:], op=ALU.mult), reads=[r_nr, r_aim], writes=[r_zq_])
    P.add("dve", lambda e: e.tensor_tensor(out=coi[:], in0=coi[:], in1=zq_[:], op=ALU.subtract), reads=[r_coi, r_zq_], writes=[r_coi])
    P.add("dve", lambda e: e.tensor_tensor(out=coi[:], in0=coi[:], in1=den[:], op=ALU.mult), reads=[r_coi, r_den], writes=[r_coi])
    B3 = [128, 64, 16]

    def bc3(t):
        return t[:].unsqueeze(2).to_broadcast(B3)

    P.add("dve", lambda e: e.tensor_tensor(out=bbr[:], in0=braw[:], in1=bc3(cor), op=ALU.mult), reads=[r_braw, r_cor], writes=[r_bbr])
    P.add("dve", lambda e: e.tensor_tensor(out=bbi[:], in0=biraw[:], in1=bc3(coi), op=ALU.mult), reads=[r_biraw, r_coi], writes=[r_bbi])
    P.add("dve", lambda e: e.tensor_tensor(out=bbr[:], in0=bbr[:], in1=bbi[:], op=ALU.subtract), reads=[r_bbr, r_bbi], writes=[r_bbr])
    P.add("dve", lambda e: e.tensor_tensor(out=bbi[:], in0=biraw[:], in1=bc3(cor), op=ALU.mult), reads=[r_biraw, r_cor], writes=[r_bbi])
    P.add("dve", lambda e: e.tensor_tensor(out=braw[:], in0=braw[:], in1=bc3(coi), op=ALU.mult), reads=[r_braw, r_coi], writes=[r_braw])
    P.add("dve", lambda e: e.tensor_tensor(out=bbi[:], in0=bbi[:], in1=braw[:], op=ALU.add), reads=[r_bbi, r_braw], writes=[r_bbi])
    P.add("pool", lambda e: e.iota(mkf[:], pattern=[[16, 8], [0, 16]], base=0, channel_multiplier=-1,
                                   allow_small_or_imprecise_dtypes=True), writes=[r_mkf])
    P.add("dve", lambda e: e.tensor_scalar(out=mkb[:], in0=mkf[:], scalar1=0.5, scalar2=None, op0=ALU.is_le),
          reads=[r_mkf], writes=[r_mkb])
    P.add("dve", lambda e: e.tensor_scalar(out=mkf[:], in0=mkf[:], scalar1=-15.5, scalar2=None, op0=ALU.is_ge),
          reads=[r_mkf], writes=[r_mkf])
    FW, BW = slice(0, 64), slice(64, 128)
    for nm, fsl, bsl in (("phi", slice(7, None, -1), slice(7, 15)), ("psim", slice(7, 15), slice(7, None, -1)),
                         ("psir", slice(15, 23), slice(15, 7, -1))):
        for ri, src, r_src in ((0, pwr, r_pwr), (1, pwi, r_pwi)):
            (tt, r_tt) = pws[nm][ri]
            P.add("dve", lambda e, tt=tt, src=src, fsl=fsl: e.tensor_copy(out=tt[FW, :, :], in_=src[FW, fsl, :]),
                  reads=[r_src], writes=[r_tt])
            P.add("dve", lambda e, tt=tt, src=src, bsl=bsl: e.tensor_copy(out=tt[BW, :, :], in_=src[BW, bsl, :]),
                  reads=[r_src], writes=[r_tt])
    A.release(m_g0)
    dump("dtt", dtt[:], [r_dtt], [128, 64])
    dump("lrdt", lrdt[:], [r_lrdt], [128, 64])
    dump("ang", ang[:], [r_ang], [128, 64])
    dump("are", are[:], [r_are], [128, 64])
    dump("pwr", pwr[:], [r_pwr], [128, 24, 64])
    dump("pwi", pwi[:], [r_pwi], [128, 24, 64])
    dump("bbr", bbr[:], [r_bbr], [128, 64, 16])
    dump("ctr", ctr[:], [r_ctr], [128, 64, 16])

    G8 = [128, 8, 8, 16]
    m_b = A.mark()
    phr, r_phr = A.tile("phr", G8, BF16)
    phi_, r_phi = A.tile("phi", G8, BF16)
    gtm, r_gtm = A.tile("gtm", G8, F32)
    gtn, r_gtn = A.tile("gtn", G8, F32)
    phr2, r_phr2 = A.tile("phr2", G8, BF16)
    phi2, r_phi2 = A.tile("phi2", G8, BF16)
    PHB = [(phr, r_phr, phi_, r_phi), (phr2, r_phr2, phi2, r_phi2)]

    def gen_cplx(outr, r_outr, outi, r_outi, tab, vr, r_vr, vi, r_vi, gt_, conj_neg_im=False):
        (tr_, r_tr_), (ti_, r_ti_) = tab
        gs = slice(gt_ * 8, (gt_ + 1) * 8)
        tb_r = tr_[:, :, gs].rearrange("q s g -> q g s").unsqueeze(3).to_broadcast(G8)
        tb_i = ti_[:, :, gs].rearrange("q s g -> q g s").unsqueeze(3).to_broadcast(G8)
        v_r = vr[:, gs, :].unsqueeze(2).to_broadcast(G8)
        v_i = vi[:, gs, :].unsqueeze(2).to_broadcast(G8)
        P.add("dve", lambda e: e.tensor_tensor(out=gtn[:], in0=tb_r, in1=v_r, op=ALU.mult), reads=[r_tr_, r_vr], writes=[r_gtn])
        P.add("pool", lambda e: e.tensor_tensor(out=gtm[:], in0=tb_i, in1=v_i, op=ALU.mult), reads=[r_ti_, r_vi], writes=[r_gtm])
        P.add("dve", lambda e: e.tensor_tensor(out=outr[:], in0=gtn[:], in1=gtm[:], op=ALU.subtract),
              reads=[r_gtn, r_gtm], writes=[r_outr])
        P.add("dve", lambda e: e.tensor_tensor(out=gtn[:], in0=tb_r, in1=v_i, op=ALU.mult), reads=[r_tr_, r_vi], writes=[r_gtn])
        P.add("pool", lambda e: e.tensor_tensor(out=gtm[:], in0=tb_i, in1=v_r, op=ALU.mult), reads=[r_ti_, r_vr], writes=[r_gtm])
        if conj_neg_im:
            P.add("dve", lambda e: e.scalar_tensor_tensor(out=outi[:], in0=gtn[:], scalar=-1.0, in1=gtm[:], op0=ALU.mult,
                                                          op1=ALU.subtract), reads=[r_gtn, r_gtm], writes=[r_outi])
        else:
            P.add("dve", lambda e: e.tensor_tensor(out=outi[:], in0=gtn[:], in1=gtm[:], op=ALU.add),
                  reads=[r_gtn, r_gtm], writes=[r_outi])

    m_ws = A.mark()
    wsb = [A.tile("ws%d" % i, [128, 8, 2, 128], BF16) for i in range(2)]
    def gen_a(g_):
        a_r, r_a_r, a_i, r_a_i = PHB[g_ % 2]
        gen_cplx(a_r, r_a_r, a_i, r_a_i, pws["phi"], bbr, r_bbr, bbi, r_bbi, g_)

    gen_a(0)
    for gt_ in range(8):
        ws, r_ws = wsb[gt_ % 2]
        pa_r, r_pa_r, pa_i, r_pa_i = PHB[gt_ % 2]
        for half_ in range(2):
            pb_, pr_ = psb[half_ % 2], psr[half_ % 2]
            pbh = pb_[:].bitcast(BF16)
            for q_ in range(8):
                gl = half_ * 4 + q_ // 2
                src_ = (pa_r, pa_i)[q_ % 2]
                r_src_ = (r_pa_r, r_pa_i)[q_ % 2]
                P.add("pe", lambda e, pbh=pbh, q_=q_, gl=gl, src_=src_: e.transpose(
                    pbh[:, q_ * 128:(q_ + 1) * 128], src_[:, gl, :, :].rearrange("q s p -> q (s p)"), ident_b[:]),
                    reads=[r_src_, r_identb], writes=[pr_])
            P.add("act", lambda e, pbh=pbh, half_=half_, ws=ws: e.copy(
                out=ws[:, half_ * 4:half_ * 4 + 4, :, :].rearrange("q g r n -> q (g r n)"), in_=pbh[:, 0:1024]),
                reads=[pr_], writes=[r_ws])
        if gt_ + 1 < 8:
            gen_a(gt_ + 1)
        pY, rY = psb[4 + gt_ % 2], psr[4 + gt_ % 2]
        for gl in range(8):
            g = gt_ * 8 + gl
            pX, rX = psb[2 + gl % 2], psr[2 + gl % 2]
            for ri in range(2):
                P.add("pe", lambda e, pX=pX, gl=gl, g=g, ri=ri, ws=ws: e.matmul(
                    pX[:, ri * 256:(ri + 1) * 256], ws[:, gl, ri, :], ug[:, g, 32:288], start=True, stop=True),
                    reads=[r_ws, r_ug], writes=[rX])
                P.add("pe", lambda e, pY=pY, gl=gl, g=g, ri=ri, ws=ws: e.matmul(
                    pY[:, gl * 64 + ri * 32:gl * 64 + ri * 32 + 32], ws[:, gl, ri, :], ug[:, g, 0:32], start=True, stop=True),
                    reads=[r_ws, r_ug], writes=[rY])
            eng = evac_eng()
            src = pX[:, 0:512].rearrange("q (r c) -> q c r", r=2)
            if eng == "act":
                P.add("act", lambda e, src=src, g=g: e.copy(out=sp_[:, 32:288, :, g], in_=src), reads=[rX], writes=[r_sp])
            else:
                P.add("dve", lambda e, src=src, g=g: e.tensor_copy(out=sp_[:, 32:288, :, g], in_=src), reads=[rX], writes=[r_sp])
        srcY = pY[:, 0:512].rearrange("q (g r c) -> q c r g", g=8, r=2)
        P.add("dve", lambda e, srcY=srcY, gt_=gt_: e.tensor_copy(out=sp_[:, 0:32, :, gt_ * 8:(gt_ + 1) * 8], in_=srcY),
              reads=[rY], writes=[r_sp])
    dump("sprime", sp_[:], [r_sp], [128, 288, 2, 64], BF16)
    if stop_after == "s5a":
        P.emit()
        return nc, P, dbg_out

    wmq = [A.tile("wmq%d" % i, [128, KT, 128], BF16) for i in range(2)]
    mod_finish = mod_slabs(4, 12, 7, cols=128, bufs=wmq)
    t3b_, r_t3b_ = A.tile("t3b", [128, 2, 64], F32)
    t3s = [(t3, r_t3), (t3b_, r_t3b_)]
    P.add("dve", lambda e: e.memset(hc[:], 0.0), writes=[r_hc])
    for i in range(288):
        qf = i
        qb = 31 - i if i < 32 else 319 - i
        tt3, r_tt3 = t3s[i % 2]
        P.add("dve", lambda e: e.tensor_tensor(out=t1[:], in0=hc[:], in1=arar[:], op=ALU.mult), reads=[r_hc, r_arar], writes=[r_t1])
        P.add("dve", lambda e: e.tensor_tensor(out=t2[:], in0=hc[:, ::-1, :], in1=aiai[:], op=ALU.mult),
              reads=[r_hc, r_aiai], writes=[r_t2])
        P.add("dve", lambda e, tt3=tt3: e.tensor_tensor(out=tt3[:], in0=t1[:], in1=t2[:], op=ALU.add),
              reads=[r_t1, r_t2], writes=[r_tt3])
        P.add("dve", lambda e, qf=qf, tt3=tt3: e.tensor_tensor(out=hc[FW, :, :], in0=tt3[FW, :, :], in1=sp_[FW, qf, :, :], op=ALU.add),
              reads=[r_tt3, r_sp], writes=[r_hc])
        P.add("dve", lambda e, qb=qb, tt3=tt3: e.tensor_tensor(out=hc[BW, :, :], in0=tt3[BW, :, :], in1=sp_[BW, qb, :, :], op=ALU.add),
              reads=[r_tt3, r_sp], writes=[r_hc])
        if 32 <= qf <= 159:
            P.add("act", lambda e, qf=qf, tt3=tt3: e.copy(out=sp_[FW, qf, :, :], in_=tt3[FW, :, :]), reads=[r_tt3, r_hc], writes=[r_hs])
        if 32 <= qb <= 159:
            P.add("act", lambda e, qb=qb, tt3=tt3: e.copy(out=sp_[BW, qb, :, :], in_=tt3[BW, :, :]), reads=[r_tt3, r_hc], writes=[r_hs])
    dump("hstates", sp_[:], [r_hs], [128, 288, 2, 64], BF16)
    mod_finish()
    P.add("dve", lambda e: e.tensor_scalar(out=s2[:], in0=modt[:, 64:80, 0], scalar1=1.0, scalar2=None, op0=ALU.add),
          reads=[r_modt], writes=[r_s2])
    P.add("dve", lambda e: e.tensor_tensor(out=s2[:], in0=s2[:], in1=n2t[:], op=ALU.mult), reads=[r_s2, r_n2], writes=[r_s2])
    if stop_after == "s5b":
        P.emit()
        return nc, P, dbg_out

    A.release(m_ws)
    psr_, r_psr = A.tile("psr_", G8, BF16)
    psi_, r_psi = A.tile("psi_", G8, BF16)
    psr2, r_psr2 = A.tile("psr2", G8, BF16)
    psi2, r_psi2 = A.tile("psi2", G8, BF16)
    PSB = [(psr_, r_psr, psi_, r_psi), (psr2, r_psr2, psi2, r_psi2)]
    mi, r_mi = A.tile("mi", [128, 8, 128], BF16)
    mtm4, r_mtm4 = A.tile("mtm4", [128, 4, 128], F32)
    dg4, r_dg4 = A.tile("dg4", [128, 4, 128], F32)
    zc, r_zc = A.tile("zc", [128, 8, 128], BF16)
    M4 = [128, 4, 128]
    def gen_c(g_):
        f_r, r_f_r, f_i, r_f_i = PHB[g_ % 2]
        q_r, r_q_r, q_i, r_q_i = PSB[g_ % 2]
        gen_cplx(f_r, r_f_r, f_i, r_f_i, pws["phi"], bbr, r_bbr, bbi, r_bbi, g_)
        gen_cplx(q_r, r_q_r, q_i, r_q_i, pws["psim"], ctr, r_ctr, cti, r_cti, g_, conj_neg_im=True)

    gen_c(0)
    for gt_ in range(8):
        fr, r_fr, fi, r_fi = PHB[gt_ % 2]
        qr_, r_qr_, qi_, r_qi_ = PSB[gt_ % 2]
        for hb in range(2):
            pF, rF = psb[0 + 6 * hb], psr[0 + 6 * hb]
            pB, rB = psb[1 + 6 * hb], psr[1 + 6 * hb]
            for gq in range(4):
                gl = hb * 4 + gq
                for (pp, rr_, hsl) in ((pF, rF, FW), (pB, rB, BW)):
                    osl = pp[:, gq * 128:(gq + 1) * 128]
                    P.add("pe", lambda e, osl=osl, hsl=hsl, gl=gl, fr=fr, qr_=qr_: e.matmul(
                        osl, fr[hsl, gl, :, :].rearrange("q s p -> q (s p)"), qr_[hsl, gl, :, :].rearrange("q s p -> q (s p)"),
                        start=True, stop=False), reads=[r_fr, r_qr_], writes=[rr_])
                    P.add("pe", lambda e, osl=osl, hsl=hsl, gl=gl, fi=fi, qi_=qi_: e.matmul(
                        osl, fi[hsl, gl, :, :].rearrange("q s p -> q (s p)"), qi_[hsl, gl, :, :].rearrange("q s p -> q (s p)"),
                        start=False, stop=True), reads=[r_fi, r_qi_], writes=[rr_])
        if gt_ + 1 < 8:
            gen_c(gt_ + 1)
        for hb in range(2):
            pF, rF = psb[0 + 6 * hb], psr[0 + 6 * hb]
            pB, rB = psb[1 + 6 * hb], psr[1 + 6 * hb]
            g0 = gt_ * 8 + hb * 4
            P.add("pool", lambda e, g0=g0: e.tensor_tensor(
                out=dg4[:], in0=ident_f[:].unsqueeze(1).to_broadcast(M4), in1=dsk[:, g0:g0 + 4].unsqueeze(2).to_broadcast(M4),
                op=ALU.mult), reads=[r_identf, r_dsk], writes=[r_dg4])
            P.add("dve", lambda e, pF=pF: e.tensor_tensor(
                out=mtm4[:], in0=pF[:, 0:512].rearrange("q (g c) -> q g c", g=4), in1=mkf[:].unsqueeze(1).to_broadcast(M4),
                op=ALU.mult), reads=[rF, r_mkf], writes=[r_mtm4])
            P.add("pool", lambda e: e.tensor_tensor(out=mtm4[:], in0=mtm4[:], in1=dg4[:], op=ALU.add),
                  reads=[r_mtm4, r_dg4], writes=[r_mtm4])
            P.add("dve", lambda e, pB=pB, hb=hb: e.tensor_tensor(
                out=mi[:, hb * 4:(hb + 1) * 4, :], in0=pB[:, 0:512].rearrange("q (g c) -> q g c", g=4),
                in1=mkb[:].unsqueeze(1).to_broadcast(M4), op=ALU.mult), reads=[rB, r_mkb], writes=[r_mi])
            P.add("pool", lambda e, hb=hb: e.tensor_tensor(out=mi[:, hb * 4:(hb + 1) * 4, :], in0=mi[:, hb * 4:(hb + 1) * 4, :],
                                                          in1=mtm4[:], op=ALU.add), reads=[r_mi, r_mtm4], writes=[r_mi])
        for hb in range(2):
            pO, rO = psb[2 + hb], psr[2 + hb]
            for gq in range(4):
                gl = hb * 4 + gq
                g = gt_ * 8 + gl
                osl = pO[:, gq * 128:(gq + 1) * 128]
                P.add("pe", lambda e, osl=osl, gl=gl, g=g: e.matmul(osl, ug[:, g, 32:160], mi[:, gl, :], start=True, stop=False),
                      reads=[r_ug, r_mi], writes=[rO])
                P.add("pe", lambda e, osl=osl, gl=gl, g=g, qr_=qr_: e.matmul(osl, sp_[:, 32:160, 0, g], qr_[:, gl, :, :].rearrange("q s p -> q (s p)"),
                                                                 start=False, stop=False), reads=[r_hs, r_qr_], writes=[rO])
                P.add("pe", lambda e, osl=osl, gl=gl, g=g, qi_=qi_: e.matmul(osl, sp_[:, 32:160, 1, g], qi_[:, gl, :, :].rearrange("q s p -> q (s p)"),
                                                                 start=False, stop=True), reads=[r_hs, r_qi_], writes=[rO])
            P.add("act", lambda e, hb=hb, pO=pO: e.activation(
                out=zc[:, :, hb * 64:(hb + 1) * 64].rearrange("c j (g p) -> c g j p", g=4),
                in_=pO[:, 0:512].rearrange("c (g j p) -> c g j p", g=4, j=8), func=AF.Gelu_apprx_tanh),
                reads=[rO], writes=[r_zc])
        pT, rT = psb[4 + gt_ % 2], psr[4 + gt_ % 2]
        pTb = pT[:].bitcast(BF16)
        for j in range(8):
            P.add("pe", lambda e, pTb=pTb, j=j: e.transpose(pTb[:, j * 128:(j + 1) * 128], zc[:, j, :], ident_b[:]),
                  reads=[r_zc, r_identb], writes=[rT])
        P.add("dve", lambda e, pTb=pTb, gt_=gt_: e.tensor_copy(
            out=zt[:, gt_, :].rearrange("q (c j) -> q j c", j=8), in_=pTb[:, 0:1024].rearrange("q (j c) -> q j c", j=8)),
            reads=[rT], writes=[r_zt])
    dump("zt", zt[:], [r_zt], [128, 8, OWN], BF16)
    A.release(m_b)
    if stop_after == "s5":
        P.emit()
        return nc, P, dbg_out

    A.release(m_ug)
    zt2, r_zt2 = A.tile("zt2", [128, 8, OWN], BF16)
    for hh_ in range(2):
        P.add("pool", lambda e, hh_=hh_: e.tensor_copy(out=zt2[:, hh_ * 4:(hh_ + 1) * 4, :], in_=zt[:, hh_ * 4:(hh_ + 1) * 4, :]),
              reads=[r_zt], writes=[r_zt2])
    m5 = A.mark()
    A.release(m5)
    mg, r_mg = A.tile("mg", [128, KT, OWN], BF16)
    m5a = A.mark()
    otl, r_otl = A.tile("otl", [128, 8, OWN], BF16)
    xto, r_xto = A.tile("xto", [128, KT, OWN], BF16)
    P.dma("sp", otl[:], ots_d, reads=[r_ots], writes=[r_otl])
    P.dma("sp", xto[:], xts_d, reads=[r_xts], writes=[r_xto])
    wmg = [A.tile("wmg%d" % i, [128, 8, 384], BF16) for i in range(2)]
    wgt = [A.tile("wgt%d" % i, [128, KT, 256], BF16) for i in range(2)]
    ta, r_ta = A.tile("ta", [128, 512], F32)
    tb_, r_tb = A.tile("tb_", [128, 512], F32)
    tc, r_tc = A.tile("tc", [128, 512], F32)
    td, r_td = A.tile("td", [128, 512], F32)
    te, r_te = A.tile("te", [128, 512], F32)
    rot8 = Rot([0, 1, 2, 3, 4, 5, 6])
    dump("zt2", zt2[:], [r_zt2], [128, 8, OWN], BF16)
    dump("otl", otl[:], [r_otl], [128, 8, OWN], BF16)
    dump("xto", xto[:], [r_xto], [128, KT, OWN], BF16)

    def mm_acc(pb_, pr_, nk, lhs_fn, rhs_fn, reads):
        for kt in range(nk):
            P.add("pe", lambda e, kt=kt: e.matmul(pb_[:, 0:512], lhs_fn(kt), rhs_fn(kt), start=(kt == 0), stop=(kt == nk - 1)),
                  reads=reads, writes=[pr_])

    def merge_dma(mt_):
        wm_, r_wm_ = wmg[mt_ % 2]
        wg_, r_wg_ = wgt[mt_ % 2]
        c_ = mt_ * 128
        P.dma("pool", wm_[:, :, 0:128], w_glu[:, c_:c_ + 128].rearrange("(kt p) n -> p kt n", p=128), writes=[r_wm_])
        P.dma("pool", wm_[:, :, 128:256], w_glu[:, D + c_:D + c_ + 128].rearrange("(kt p) n -> p kt n", p=128), writes=[r_wm_])
        P.dma("pool", wm_[:, :, 256:384], w_mla_o[:, c_:c_ + 128].rearrange("(kt p) n -> p kt n", p=128), writes=[r_wm_])
        P.dma("pool", wg_[:, :, 0:128], w_in[:, 1856 + c_:1856 + c_ + 128].rearrange("(kt p) n -> p kt n", p=128), writes=[r_wg_])
        P.dma("pool", wg_[:, :, 128:256], w_in[:, 1856 + D + c_:1856 + D + c_ + 128].rearrange("(kt p) n -> p kt n", p=128),
              writes=[r_wg_])

    merge_dma(0)
    for mt in range(16):
        wm, r_wm = wmg[mt % 2]
        wg, r_wg = wgt[mt % 2]
        if mt + 1 < 16:
            merge_dma(mt + 1)
        for tbk in range(2):
            t0 = tbk * 512
            pB, rB_ = rot8.next()
            mm_acc(pB, rB_, 8, lambda kt, wm=wm: wm[:, kt, 128:256], lambda kt, t0=t0: zt2[:, kt, t0:t0 + 512], [r_wm, r_zt2])
            P.add("act", lambda e, pB=pB: e.activation(out=ta[:], in_=pB[:, 0:512], func=AF.Sigmoid), reads=[rB_], writes=[r_ta])
            pA, rA_ = rot8.next()
            mm_acc(pA, rA_, 8, lambda kt, wm=wm: wm[:, kt, 0:128], lambda kt, t0=t0: zt2[:, kt, t0:t0 + 512], [r_wm, r_zt2])
            P.add("dve", lambda e, pA=pA: e.tensor_tensor(out=tb_[:], in0=pA[:, 0:512], in1=ta[:], op=ALU.mult),
                  reads=[rA_, r_ta], writes=[r_tb])
            pG, rG_ = rot8.next()
            mm_acc(pG, rG_, KT, lambda kt, wg=wg: wg[:, kt, 0:128], lambda kt, t0=t0: xto[:, kt, t0:t0 + 512], [r_wg, r_xto])
            P.add("act", lambda e, pG=pG: e.activation(out=tc[:], in_=pG[:, 0:512], func=AF.Sigmoid), reads=[rG_], writes=[r_tc])
            P.add("dve", lambda e: e.tensor_tensor(out=tb_[:], in0=tb_[:], in1=tc[:], op=ALU.mult), reads=[r_tb, r_tc], writes=[r_tb])
            pH, rH_ = rot8.next()
            mm_acc(pH, rH_, KT, lambda kt, wg=wg: wg[:, kt, 128:256], lambda kt, t0=t0: xto[:, kt, t0:t0 + 512], [r_wg, r_xto])
            P.add("act", lambda e, pH=pH: e.activation(out=td[:], in_=pH[:, 0:512], func=AF.Sigmoid), reads=[rH_], writes=[r_td])
            pM, rM_ = rot8.next()
            mm_acc(pM, rM_, 8, lambda kt, wm=wm: wm[:, kt, 256:384], lambda kt, t0=t0: otl[:, kt, t0:t0 + 512], [r_wm, r_otl])
            P.add("dve", lambda e, pM=pM: e.tensor_tensor(out=te[:], in0=pM[:, 0:512], in1=td[:], op=ALU.mult),
                  reads=[rM_, r_td], writes=[r_te])
            P.add("dve", lambda e, mt=mt, t0=t0: e.tensor_tensor(out=mg[:, mt, t0:t0 + 512], in0=tb_[:], in1=te[:], op=ALU.add),
                  reads=[r_tb, r_te], writes=[r_mg])
    dump("mg", mg[:], [r_mg], [128, KT, OWN], BF16)
    A.release(m5a)
    xT, r_xT = A.tile("xT", [128, KT, OWN], F32)
    m5b = A.mark()
    xrb = [A.tile("xrb%d" % i, [128, D], F32) for i in range(2)]
    for ti in range(8):
        xb, xr = xrb[ti % 2]
        P.dma("sp", xb[:], xs[LC + ti * 128:LC + (ti + 1) * 128, :], writes=[xr])
        for dq in range(4):
            pb_, pr_ = rot8.next()
            for i_ in range(4):
                dt_ = dq * 4 + i_
                P.add("pe", lambda e, pb_=pb_, i_=i_, dt_=dt_, xb=xb: e.transpose(
                    pb_[:, i_ * 128:(i_ + 1) * 128], xb[:, dt_ * 128:(dt_ + 1) * 128], ident_f[:]),
                    reads=[xr, r_identf], writes=[pr_])
            eng = evac_eng()
            src = pb_[:, 0:512].rearrange("q (i t) -> q i t", i=4)
            if eng == "act":
                P.add("act", lambda e, src=src, dq=dq, ti=ti: e.copy(out=xT[:, dq * 4:(dq + 1) * 4, ti * 128:(ti + 1) * 128], in_=src),
                      reads=[pr_], writes=[r_xT])
            else:
                P.add("dve", lambda e, src=src, dq=dq, ti=ti: e.tensor_copy(out=xT[:, dq * 4:(dq + 1) * 4, ti * 128:(ti + 1) * 128], in_=src),
                      reads=[pr_], writes=[r_xT])
    wob = [A.tile("wob%d" % i, [128, KT, 128], BF16) for i in range(2)]
    for mt in range(16):
        wo, r_wo = wob[mt % 2]
        P.dma("pool", wo[:], w_out[:, mt * 128:(mt + 1) * 128].rearrange("(kt p) n -> p kt n", p=128), writes=[r_wo])
        for tbk in range(2):
            t0 = tbk * 512
            pb_, pr_ = rot8.next()
            mm_acc(pb_, pr_, KT, lambda kt, wo=wo: wo[:, kt, :], lambda kt, t0=t0: mg[:, kt, t0:t0 + 512], [r_wo, r_mg])
            P.add("dve", lambda e, pb_=pb_, mt=mt, t0=t0: e.scalar_tensor_tensor(
                out=xT[:, mt, t0:t0 + 512], in0=pb_[:, 0:512], scalar=modt[:, 32 + mt, 0:1], in1=xT[:, mt, t0:t0 + 512],
                op0=ALU.mult, op1=ALU.add), reads=[pr_, r_modt, r_xT], writes=[r_xT])
    dump("x1", xT[:], [r_xT], [128, KT, OWN])
    A.release(m5b)
    if stop_after == "merge":
        P.emit()
        return nc, P, dbg_out

    sqb, r_sqb = A.tile("sqb", [128, KT, 512], BF16)
    rn, r_rn = A.tile("rn", [128, OWN], F32)
    tn, r_tn = A.tile("tn", [128, 512], F32)

    def rms_bcast(src, r_src, sqt, r_sqt, rnt, r_rnt):
        for tbk in range(2):
            t0 = tbk * 512
            for dt_ in range(KT):
                P.add("act", lambda e, dt_=dt_, t0=t0: e.activation(out=sqt[:, dt_, :], in_=src[:, dt_, t0:t0 + 512], func=AF.Square),
                      reads=[r_src], writes=[r_sqt])
            pb_, pr_ = rot8.next()
            mm_acc(pb_, pr_, KT, lambda kt: ones_b[:], lambda kt: sqt[:, kt, :], [r_onesb, r_sqt])
            P.add("act", lambda e, pb_=pb_, t0=t0: e.activation(out=rnt[:, t0:t0 + 512], in_=pb_[:, 0:512], func=AF.Sqrt,
                                                               scale=1.0 / D, bias=EPS), reads=[pr_], writes=[r_rnt])
            P.add("dve", lambda e, t0=t0: e.reciprocal(out=rnt[:, t0:t0 + 512], in_=rnt[:, t0:t0 + 512]), reads=[r_rnt], writes=[r_rnt])

    rms_bcast(xT, r_xT, sqb, r_sqb, rn, r_rn)
    h2, r_h2 = A.tile("h2", [128, KT, OWN], BF16)
    for tbk in range(2):
        t0 = tbk * 512
        for dt_ in range(KT):
            P.add("dve", lambda e, dt_=dt_, t0=t0: e.tensor_tensor(out=tn[:], in0=xT[:, dt_, t0:t0 + 512], in1=rn[:, t0:t0 + 512],
                                                                op=ALU.mult), reads=[r_xT, r_rn], writes=[r_tn])
            P.add("act", lambda e, dt_=dt_, t0=t0: e.activation(out=h2[:, dt_, t0:t0 + 512], in_=tn[:], func=AF.Identity,
                                                               scale=s2[:, dt_:dt_ + 1], bias=modt[:, 48 + dt_, 0:1]),
                  reads=[r_tn, r_s2, r_modt], writes=[r_h2])
    dump("h2", h2[:], [r_h2], [128, KT, OWN], BF16)
    if stop_after == "norm2":
        P.emit()
        return nc, P, dbg_out
    P.dma("sp", x1s_d, xT[:], reads=[r_xT], writes=[r_x1s])
    h2b, r_h2b = h2, r_h2
    h2_end = A.off
    A.release(m5)
    hh, r_hh = A.tile("hh", [128, 44, OWN], BF16)
    hh_end = A.off
    A.off = h2_end
    m6 = A.mark()
    wfb = [A.tile("wfb%d" % i, [128, KT, 256], BF16) for i in range(2)]
    sa, r_sa = A.tile("sa", [128, 512], F32)
    for j in range(44):
        wf, r_wf = wfb[j % 2]
        P.dma("pool", wf[:, :, 0:128], w_ffn_in[:, j * 128:(j + 1) * 128].rearrange("(kt p) n -> p kt n", p=128), writes=[r_wf])
        P.dma("pool", wf[:, :, 128:256], w_ffn_in[:, DFF + j * 128:DFF + (j + 1) * 128].rearrange("(kt p) n -> p kt n", p=128),
              writes=[r_wf])
        for tbk in range(2):
            t0 = tbk * 512
            pA, rA_ = rot8.next()
            mm_acc(pA, rA_, KT, lambda kt, wf=wf: wf[:, kt, 0:128], lambda kt, t0=t0: h2b[:, kt, t0:t0 + 512], [r_wf, r_h2b])
            P.add("act", lambda e, pA=pA: e.activation(out=sa[:], in_=pA[:, 0:512], func=AF.Silu), reads=[rA_], writes=[r_sa])
            pB, rB_ = rot8.next()
            mm_acc(pB, rB_, KT, lambda kt, wf=wf: wf[:, kt, 128:256], lambda kt, t0=t0: h2b[:, kt, t0:t0 + 512], [r_wf, r_h2b])
            P.add("dve", lambda e, pB=pB, j=j, t0=t0: e.tensor_tensor(out=hh[:, j, t0:t0 + 512], in0=pB[:, 0:512], in1=sa[:],
                                                                    op=ALU.mult), reads=[rB_, r_sa], writes=[r_hh])
    A.release(m6)
    dump("hh", hh[:], [r_hh], [128, 44, OWN], BF16)
    A.off = hh_end
    xT2, r_xT2 = A.tile("xT2", [128, KT, OWN], F32)
    P.dma("sp", xT2[:], x1s_d, reads=[r_x1s], writes=[r_xT2])
    wfo = [A.tile("wfo%d" % i, [128, 44, 128], BF16) for i in range(2)]
    for mt in range(16):
        wo, r_wo = wfo[mt % 2]
        P.dma("pool", wo[:], w_ffn_out[:, mt * 128:(mt + 1) * 128].rearrange("(kt p) n -> p kt n", p=128), writes=[r_wo])
        for tbk in range(2):
            t0 = tbk * 512
            pb_, pr_ = rot8.next()
            mm_acc(pb_, pr_, 44, lambda kt, wo=wo: wo[:, kt, :], lambda kt, t0=t0: hh[:, kt, t0:t0 + 512], [r_wo, r_hh])
            P.add("dve", lambda e, pb_=pb_, mt=mt, t0=t0: e.scalar_tensor_tensor(
                out=xT2[:, mt, t0:t0 + 512], in0=pb_[:, 0:512], scalar=modt[:, 80 + mt, 0:1], in1=xT2[:, mt, t0:t0 + 512],
                op0=ALU.mult, op1=ALU.add), reads=[pr_, r_modt, r_xT2], writes=[r_xT2])
    dump("x2", xT2[:], [r_xT2], [128, KT, OWN])

    A.release(m5)
    sqb2, r_sqb2 = A.tile("sqb2", [128, KT, 512], BF16)
    rn2, r_rn2 = A.tile("rn2", [128, OWN], F32)
    rms_bcast(xT2, r_xT2, sqb2, r_sqb2, rn2, r_rn2)
    yv, r_yv = A.tile("yv", [128, KT, 128], F32)
    obuf = [A.tile("obuf%d" % i, [128, D], F32) for i in range(2)]
    for ti in range(8):
        c0 = ti * 128
        for dt_ in range(KT):
            P.add("dve", lambda e, dt_=dt_, c0=c0: e.scalar_tensor_tensor(
                out=yv[:, dt_, :], in0=xT2[:, dt_, c0:c0 + 128], scalar=nft[:, dt_:dt_ + 1], in1=rn2[:, c0:c0 + 128],
                op0=ALU.mult, op1=ALU.mult), reads=[r_xT2, r_nf, r_rn2], writes=[r_yv])
        ob, r_ob = obuf[ti % 2]
        for dq in range(4):
            pb_, pr_ = rot8.next()
            for i_ in range(4):
                P.add("pe", lambda e, pb_=pb_, i_=i_, dq=dq: e.transpose(pb_[:, i_ * 128:(i_ + 1) * 128], yv[:, dq * 4 + i_, :], ident_f[:]),
                      reads=[r_yv, r_identf], writes=[pr_])
            eng = evac_eng()
            if eng == "act":
                P.add("act", lambda e, pb_=pb_, dq=dq, ob=ob: e.copy(out=ob[:, dq * 512:(dq + 1) * 512], in_=pb_[:, 0:512]),
                      reads=[pr_], writes=[r_ob])
            else:
                P.add("dve", lambda e, pb_=pb_, dq=dq, ob=ob: e.tensor_copy(out=ob[:, dq * 512:(dq + 1) * 512], in_=pb_[:, 0:512]),
                      reads=[pr_], writes=[r_ob])
        P.final.append(P.dma("sp", out[c0:c0 + 128, :], ob[:], reads=[r_ob]))
    P.emit()
    return nc, P, dbg_out


def make_in_maps(inputs):
    f = np.float32
    g = lambda k: np.ascontiguousarray(np.asarray(inputs[k], dtype=f))
    shared = {
        "w_mod": g("w_mod")[0], "b_mod": g("b_mod")[0], "norm1": g("norm1")[0], "norm2": g("norm2")[0],
        "w_in": g("w_in")[0], "s5_d": g("s5_d")[0], "w_glu": g("w_glu")[0], "q_norm": g("q_norm")[0],
        "kv_norm": g("kv_norm")[0], "w_uq": g("w_uq")[0], "w_ukv": g("w_ukv")[0], "w_mla_o": g("w_mla_o")[0],
        "w_out": g("w_out")[0], "w_ffn_in": g("w_ffn_in")[0], "w_ffn_out": g("w_ffn_out")[0], "norm_f": g("norm_f"),
    }
    s5n = ["s5_a_re", "s5_a_im", "s5_log_dt", "s5_b_re", "s5_b_im", "s5_c_re", "s5_c_im"]
    s5 = {k: g(k)[0] for k in s5n}
    s5sw = {k: np.ascontiguousarray(v[::-1]) for k, v in s5.items()}
    x = g("x")
    ctx = g("ctx")
    c = g("c")
    cc = g("c_ctx")
    maps = []
    for core in range(8):
        b, hf = core // 2, core % 2
        if hf == 0:
            seq = np.concatenate([ctx[b], x[b]], axis=0)
            cfgv = np.array([[0.0, 0.0, 1.0, 0.0]], dtype=f)
            sp = s5
        else:
            seq = np.concatenate([ctx[b][::-1], x[b][::-1]], axis=0)
            cfgv = np.array([[31.0, 63.0, -1.0, 0.0]], dtype=f)
            sp = s5sw
        m = dict(shared)
        m.update(sp)
        m["xs"] = np.ascontiguousarray(seq)
        m["cvec"] = np.ascontiguousarray(np.stack([c[b], cc], axis=0))
        m["cfg"] = cfgv
        maps.append(m)
    return maps


def assemble(results):
    outp = np.zeros((4, L, D), dtype=np.float32)
    for core in range(8):
        b, hf = core // 2, core % 2
        o = np.asarray(results[core]["out"])
        if hf == 0:
            outp[b, 0:OWN] = o
        else:
            outp[b, OWN:L] = o[::-1]
    return outp


def kernel(**inputs):
    nc, P, _ = build()
    maps = make_in_maps(inputs)
    res = run_bass_kernel_spmd(nc, maps, core_ids=list(range(8)))
    return assemble(res.results)
```
